# Optimizing a Trainium2 kernel written in Bass

```python
import jax, jax.numpy as jnp
from jax import lax
import numpy as np

D_MODEL = 1024
BATCH = 1
SEQ = 16384
DEPTH = 1

N_META = 16
GRID_W = 64
CHUNK = 128
Q_BLOCK = 128
EPS = 1e-6
HG_HEADS = 4
HG_K = 128
HG_V = 128
HG_KW = HG_HEADS * HG_K
HG_VW = HG_HEADS * HG_V
AT_HEADS = 8
AT_KV_HEADS = 2
AT_HD = 64
AT_GROUP = AT_HEADS // AT_KV_HEADS
AT_W = AT_HEADS * AT_HD
AT_KVW = AT_KV_HEADS * AT_HD
ROPE_THETA = 10000.0
ROPE_AXIS = AT_HD // 2
D_FF = 2816
IN_SIZES = (HG_KW, HG_VW, HG_KW, HG_KW, HG_VW, AT_W, AT_KVW, AT_KVW, D_MODEL, D_MODEL)
D_IN = sum(IN_SIZES)

kernel_name = 'hybrid_hgrn2_axial_gqa_macaron_block'


def rms_norm(x, w):
    xf = x.astype(jnp.float32)
    y = xf * lax.rsqrt(jnp.mean(xf * xf, axis=-1, keepdims=True) + EPS)
    return (y * w.astype(jnp.float32)).astype(x.dtype)


def swiglu(x, w_gate, w_up, w_down):
    return (jax.nn.silu(x @ w_gate) * (x @ w_up)) @ w_down


def split_cols(z, sizes):
    outs = []
    start = 0
    for s in sizes:
        outs.append(z[..., start:start + s])
        start += s
    return outs


def axial_rope_tables(n_real):
    rows = n_real // GRID_W
    row = jnp.repeat(jnp.arange(rows, dtype=jnp.float32), GRID_W)
    col = jnp.tile(jnp.arange(GRID_W, dtype=jnp.float32), rows)
    zeros = jnp.zeros((N_META,), jnp.float32)
    row = jnp.concatenate([zeros, row])
    col = jnp.concatenate([zeros, col])
    inv = ROPE_THETA ** (-jnp.arange(0, ROPE_AXIS, 2, dtype=jnp.float32) / ROPE_AXIS)
    ang = jnp.concatenate([row[:, None] * inv, col[:, None] * inv], axis=-1)
    return jnp.cos(ang), jnp.sin(ang)


def apply_rope(x, cos, sin):
    xf = x.astype(jnp.float32).reshape(x.shape[:-1] + (AT_HD // 2, 2))
    x1, x2 = xf[..., 0], xf[..., 1]
    c = cos[None, :, None, :]
    s = sin[None, :, None, :]
    out = jnp.stack([x1 * c - x2 * s, x1 * s + x2 * c], axis=-1).reshape(x.shape)
    return out.astype(x.dtype)


def attend(qb, k, v):
    s = jnp.einsum('bkgqd,bksd->bkgqs', qb, k, preferred_element_type=jnp.float32) * (AT_HD ** -0.5)
    p = jax.nn.softmax(s, axis=-1)
    return jnp.einsum('bkgqs,bksd->bkgqd', p.astype(v.dtype), v)


def axial_gqa(zq, zk, zv, q_norm_w, k_norm_w, cos, sin):
    B, L, _ = zq.shape
    q = rms_norm(zq.reshape(B, L, AT_HEADS, AT_HD), q_norm_w)
    k = rms_norm(zk.reshape(B, L, AT_KV_HEADS, AT_HD), k_norm_w)
    v = zv.reshape(B, L, AT_KV_HEADS, AT_HD)
    q = apply_rope(q, cos, sin)
    k = apply_rope(k, cos, sin)
    q = q.transpose(0, 2, 1, 3).reshape(B, AT_KV_HEADS, AT_GROUP, L, AT_HD)
    k = k.transpose(0, 2, 1, 3)
    v = v.transpose(0, 2, 1, 3)
    n_real = L - N_META
    n_blk = n_real // Q_BLOCK
    o_meta = attend(q[:, :, :, :N_META], k, v)
    q_real = jnp.moveaxis(q[:, :, :, N_META:].reshape(B, AT_KV_HEADS, AT_GROUP, n_blk, Q_BLOCK, AT_HD), 3, 0)
    o_real = lax.map(lambda qb: attend(qb, k, v), q_real)
    o_real = jnp.moveaxis(o_real, 0, 3).reshape(B, AT_KV_HEADS, AT_GROUP, n_real, AT_HD)
    o = jnp.concatenate([o_meta, o_real], axis=3)
    return o.transpose(0, 3, 1, 2, 4).reshape(B, L, AT_W)


def gla_chunked(q, k, v, logf):
    B, T, H, K = q.shape
    n = T // CHUNK
    def to_chunks(a):
        return a.reshape(B, n, CHUNK, H, a.shape[-1]).transpose(1, 0, 3, 2, 4)
    mask = jnp.tril(jnp.ones((CHUNK, CHUNK), dtype=bool))

    def step(S, xs):
        qc, kc, vc, lc = xs
        b = jnp.cumsum(lc, axis=2)
        o_inter = jnp.einsum('bhck,bhkv->bhcv', qc * jnp.exp(b), S)
        diff = jnp.where(mask[None, None, :, :, None], b[:, :, :, None, :] - b[:, :, None, :, :], -jnp.inf)
        attn = jnp.einsum('bhtk,bhsk,bhtsk->bhts', qc, kc, jnp.exp(diff))
        o_intra = jnp.einsum('bhts,bhsv->bhtv', attn, vc)
        b_last = b[:, :, -1:, :]
        S_new = jnp.exp(b_last[:, :, 0, :])[..., None] * S + jnp.einsum('bhsk,bhsv->bhkv', kc * jnp.exp(b_last - b), vc)
        return S_new, o_inter + o_intra

    S0 = jnp.zeros((B, H, K, v.shape[-1]), jnp.float32)
    _, o = lax.scan(step, S0, (to_chunks(q), to_chunks(k), to_chunks(v), to_chunks(logf)))
    return o.transpose(1, 0, 3, 2, 4).reshape(B, T, H, v.shape[-1])


def hgrn2_bidir(zq, zi, zf_f, zf_b, zg, lb_f, lb_b, out_norm_w):
    B, L, _ = zq.shape
    q = jax.nn.silu(zq.astype(jnp.float32)).reshape(B, L, HG_HEADS, HG_K)
    v = zi.astype(jnp.float32).reshape(B, L, HG_HEADS, HG_V)

    def gates(zf, lb):
        lb = lb.reshape(HG_HEADS, HG_K)
        kk = (1.0 - lb) * jax.nn.sigmoid(-zf.astype(jnp.float32).reshape(B, L, HG_HEADS, HG_K))
        return kk, jnp.log1p(-kk)

    k_f, lf_f = gates(zf_f, lb_f)
    k_b, lf_b = gates(zf_b, lb_b)
    n_pad = CHUNK - N_META
    pad = lambda a: jnp.pad(a, ((0, 0), (n_pad, 0), (0, 0), (0, 0)))
    flip = lambda a: jnp.flip(a, axis=1)
    q_p, v_p = pad(q), pad(v)
    o_fwd = gla_chunked(q_p, pad(k_f), v_p, pad(lf_f))
    o_bwd = flip(gla_chunked(flip(q_p), flip(pad(k_b)), flip(v_p), flip(pad(lf_b))))
    o = (o_fwd + o_bwd)[:, n_pad:]
    o = o * lax.rsqrt(jnp.mean(o * o, axis=-1, keepdims=True) + EPS) * out_norm_w.astype(jnp.float32).reshape(HG_HEADS, HG_V)
    o = o.reshape(B, L, HG_VW) * jax.nn.silu(zg.astype(jnp.float32))
    return o.astype(zq.dtype)


def setup_inputs(seed: int = 0) -> dict:
    key = jax.random.key(seed)
    ks = jax.random.split(key, 24)
    nrm = lambda k, shape, fan_in: jax.random.normal(k, shape, jnp.float32) * (fan_in ** -0.5)
    gain = lambda k, shape: 1.0 + 0.01 * jax.random.normal(k, shape, jnp.float32)
    return {
        'x': jax.random.normal(ks[0], (BATCH, SEQ, D_MODEL), jnp.float32),
        'meta_tokens': jax.random.normal(ks[1], (N_META, D_MODEL), jnp.float32),
        'ffn1_norm': gain(ks[2], (DEPTH, D_MODEL)),
        'ffn1_w_gate': nrm(ks[3], (DEPTH, D_MODEL, D_FF), D_MODEL),
        'ffn1_w_up': nrm(ks[4], (DEPTH, D_MODEL, D_FF), D_MODEL),
        'ffn1_w_down': nrm(ks[5], (DEPTH, D_FF, D_MODEL), D_FF),
        'mix_norm': gain(ks[6], (DEPTH, D_MODEL)),
        'w_in': nrm(ks[7], (DEPTH, D_MODEL, D_IN), D_MODEL),
        'hg_lb_fwd': 0.1 * jax.random.normal(ks[8], (DEPTH + 1, HG_KW), jnp.float32),
        'hg_lb_bwd': 0.1 * jax.random.normal(ks[9], (DEPTH + 1, HG_KW), jnp.float32),
        'hg_out_norm': gain(ks[10], (DEPTH, HG_VW)),
        'q_norm': gain(ks[11], (DEPTH, AT_HD)),
        'k_norm': gain(ks[12], (DEPTH, AT_HD)),
        'w_up_a': nrm(ks[13], (DEPTH, HG_VW, D_MODEL), HG_VW),
        'w_up_b': nrm(ks[14], (DEPTH, AT_W, D_MODEL), AT_W),
        'w_out': nrm(ks[15], (DEPTH, D_MODEL, D_MODEL), D_MODEL),
        'ffn2_norm': gain(ks[16], (DEPTH, D_MODEL)),
        'ffn2_w_gate': nrm(ks[17], (DEPTH, D_MODEL, D_FF), D_MODEL),
        'ffn2_w_up': nrm(ks[18], (DEPTH, D_MODEL, D_FF), D_MODEL),
        'ffn2_w_down': nrm(ks[19], (DEPTH, D_FF, D_MODEL), D_FF),
    }


def reference(x, meta_tokens, ffn1_norm, ffn1_w_gate, ffn1_w_up, ffn1_w_down, mix_norm, w_in, hg_lb_fwd, hg_lb_bwd, hg_out_norm, q_norm, k_norm, w_up_a, w_up_b, w_out, ffn2_norm, ffn2_w_gate, ffn2_w_up, ffn2_w_down):
    B, n_real, _ = x.shape
    meta = jnp.broadcast_to(meta_tokens.astype(x.dtype)[None], (B, N_META, D_MODEL))
    h = jnp.concatenate([meta, x], axis=1)
    cos, sin = axial_rope_tables(n_real)
    lb_fwd_all = jnp.cumsum(jax.nn.softmax(hg_lb_fwd.astype(jnp.float32), axis=0), axis=0)
    lb_bwd_all = jnp.cumsum(jax.nn.softmax(hg_lb_bwd.astype(jnp.float32), axis=0), axis=0)
    for layer in range(DEPTH):
        h = h + 0.5 * swiglu(rms_norm(h, ffn1_norm[layer]), ffn1_w_gate[layer], ffn1_w_up[layer], ffn1_w_down[layer])
        u = rms_norm(h, mix_norm[layer])
        z = u @ w_in[layer]
        zq_a, zi_a, zf_f, zf_b, zg_a, zq_b, zk_b, zv_b, zgate_a, zgate_b = split_cols(z, IN_SIZES)
        y_a = hgrn2_bidir(zq_a, zi_a, zf_f, zf_b, zg_a, lb_fwd_all[layer], lb_bwd_all[layer], hg_out_norm[layer])
        y_b = axial_gqa(zq_b, zk_b, zv_b, q_norm[layer], k_norm[layer], cos, sin)
        mixed = jax.nn.sigmoid(zgate_a) * (y_a @ w_up_a[layer]) + jax.nn.sigmoid(zgate_b) * (y_b @ w_up_b[layer])
        h = h + mixed @ w_out[layer]
        h = h + 0.5 * swiglu(rms_norm(h, ffn2_norm[layer]), ffn2_w_gate[layer], ffn2_w_up[layer], ffn2_w_down[layer])
    return h[:, N_META:]
```

```python
import numpy as np
from contextlib import ExitStack
import concourse.bass as bass
import concourse.mybir as mybir
from concourse.bass_utils import run_bass_kernel_spmd

F32 = mybir.dt.float32
BF16 = mybir.dt.bfloat16
I32 = mybir.dt.int32
AF = mybir.ActivationFunctionType
ALU = mybir.AluOpType

NCORE = 8
D = 1024
DC = 8
NT = 2048
NM = 16
NA = NT + NM
DFF = 2816
JC = 22
EPS = 1e-6
CG = [(0, 512), (512, 512), (1024, 512), (1536, 512), (2048, 16)]
NV = 52
WIDE_EXP = False
TWO_PI = 6.28318
HGW = 1032


class Buf:
    __slots__ = ("name", "w", "r")

    def __init__(self, name=""):
        self.name = name
        self.w = None
        self.r = {}


class Sched:
    ENG = ("pe", "act", "dve", "pool", "sp")

    def __init__(self, nc, ctx):
        self.nc = nc
        self.ctx = ctx
        self.h = {"pe": nc.tensor, "act": nc.scalar, "dve": nc.vector, "pool": nc.gpsimd, "sp": nc.sync}
        self.sems = {}
        self.count = {}
        self.waited = {e: {} for e in self.ENG}
        for e in self.ENG:
            self.sems[e] = ctx.enter_context(nc.semaphore("sem_" + e))
            self.count[e] = 0
        self.nchan = 0
        self.ninstr = 0

    def chan(self):
        key = "ch%d" % self.nchan
        self.nchan += 1
        self.sems[key] = self.ctx.enter_context(self.nc.semaphore("sem_" + key))
        self.count[key] = 0
        return key

    def _waits(self, eng, reads, writes, self_sync):
        need = {}
        for b in reads:
            if b.w is not None:
                k, v = b.w
                if need.get(k, 0) < v:
                    need[k] = v
        for b in writes:
            if b.w is not None:
                k, v = b.w
                if need.get(k, 0) < v:
                    need[k] = v
            for k, v in b.r.items():
                if need.get(k, 0) < v:
                    need[k] = v
        wd = self.waited[eng]
        hnd = self.h[eng]
        for k, v in need.items():
            if (not self_sync) and k == eng:
                continue
            if wd.get(k, 0) < v:
                wd[k] = v
                hnd.wait_ge(self.sems[k], v)

    def _mark(self, key, val, reads, writes):
        for b in reads:
            if b.r.get(key, 0) < val:
                b.r[key] = val
        for b in writes:
            b.w = (key, val)
            b.r = {}

    def op(self, eng, fn, reads=(), writes=(), self_sync=True):
        self._waits(eng, reads, writes, self_sync)
        self.count[eng] += 1
        fn(self.h[eng]).then_inc(self.sems[eng], 1)
        self._mark(eng, self.count[eng], reads, writes)
        self.ninstr += 1

    def dma(self, q, ch, fn, reads=(), writes=(), inc=16):
        self._waits(q, reads, writes, True)
        self.count[ch] += inc
        fn(self.h[q]).then_inc(self.sems[ch], inc)
        self._mark(ch, self.count[ch], reads, writes)
        self.ninstr += 1

    def wait_bufs(self, eng, bufs):
        self._waits(eng, bufs, (), True)

    def barrier(self):
        for e in self.ENG:
            wd = self.waited[e]
            for k, v in self.count.items():
                if k == e or v == 0:
                    continue
                if wd.get(k, 0) < v:
                    wd[k] = v
                    self.h[e].wait_ge(self.sems[k], v)


def build_program():
    nc = bass.Bass("TRN2", target_bir_lowering=False)
    dt = lambda name, shape, dtype=F32, kind="ExternalInput": nc.dram_tensor(name, shape, dtype, kind=kind)
    xT_d = dt("xT", [D, NT]).ap()
    metaT_d = dt("metaT", [D, NM]).ap()
    vecs_d = dt("vecs", [128, NV]).ap()
    corec_d = dt("corec", [128, 17]).ap()
    cmat_d = dt("cmat", [128, 3 * 128]).ap()
    wg_d = [dt("wg%d" % i, [JC, 128, 1024]).ap() for i in (1, 2)]
    wu_d = [dt("wu%d" % i, [JC, 128, 1024]).ap() for i in (1, 2)]
    wd_d = [dt("wd%d" % i, [DC, 128, DFF]).ap() for i in (1, 2)]
    wfm_d = dt("wfm", [42, 128, 1024]).ap()
    wzi_d = dt("wzi", [4, 128, 1024]).ap()
    wv_d = dt("wv", [128, 1024]).ap()
    wua_d = dt("wua", [4, 128, 1024]).ap()
    wub_d = dt("wub", [4, 128, 1024]).ap()
    wo_d = dt("wo", [DC, 128, 1024]).ap()
    outT_d = dt("outT", [D, NT], F32, "ExternalOutput").ap()
    kt_loc_d = nc.dram_tensor("kt_loc", [128, NT], BF16)
    kt_all_d = nc.dram_tensor("kt_all", [NCORE * 128, NT], BF16)
    v_loc_d = nc.dram_tensor("v_loc", [128, 16 * 130], BF16)
    v_all_d = nc.dram_tensor("v_all", [NCORE * 128, 16 * 130], BF16)
    hg_loc_d = nc.dram_tensor("hg_loc", [128, HGW], F32)
    hg_all_d = nc.dram_tensor("hg_all", [NCORE * 128, HGW], F32)

    with ExitStack() as ctx:
        S = Sched(nc, ctx)

        uniq = [0]

        def sbt(c, name, shape, dtype=F32, side="left"):
            uniq[0] += 1
            return c.enter_context(nc.sbuf_tensor("s%d_%s" % (uniq[0], name), shape, dtype, side=side))

        def sbr(c, name, shape, dtype=F32):
            return sbt(c, name, shape, dtype, side="right")

        PSbig = ctx.enter_context(nc.psum_tensor("psbig", [128, 2048], F32))
        PS = [PSbig[:, i * 512:(i + 1) * 512] for i in range(4)] + \
             [ctx.enter_context(nc.psum_tensor("ps%d" % i, [128, 512], F32)) for i in range(4, 7)]
        pb = [Buf("ps%d" % i) for i in range(7)]
        PT = ctx.enter_context(nc.psum_tensor("pt", [128, 1024], BF16))
        ptb = Buf("pt")

        hT = sbr(ctx, "hT", [128, DC, NA])
        hb = [[Buf("h") for _ in CG] for _ in range(DC)]
        vecs = sbr(ctx, "vecs", [128, NV]); vecb = Buf("vecs")
        corec = sbr(ctx, "corec", [128, 17]); coreb = Buf("corec")
        der = sbr(ctx, "der", [128, 32]); derb = Buf("der")
        cmat = sbr(ctx, "cmat", [128, 384], BF16); cmatb = Buf("cmat")
        mask4 = sbr(ctx, "mask4", [128, 2, 512], BF16); maskb = Buf("mask4")
        ones_bf = sbr(ctx, "ones_bf", [128, 128], BF16); onesb = Buf("ones")
        ones_f = sbr(ctx, "ones_f", [128, 128]); onesfb = Buf("onesf")
        blk_f = sbr(ctx, "blk_f", [128, 128]); blkb = Buf("blk")
        KT_meta = sbr(ctx, "KT_meta", [128, NM], BF16); ktmb = Buf("ktm")
        V_meta = sbr(ctx, "V_meta", [128, 130], BF16); vmb = Buf("vm")

        def vcol(i, n=1):
            return vecs[:, i:i + n]
        D_LBF, D_OMLF, D_NOMLF, D_LBB, D_OMLB, D_NOMLB = 0, 4, 8, 12, 16, 20
        D_INVR, D_SINSC = 24, 25

        act = lambda fn, r=(), w=(): S.op("act", fn, r, w)
        dve = lambda fn, r=(), w=(): S.op("dve", fn, r, w)
        pool = lambda fn, r=(), w=(): S.op("pool", fn, r, w)

        def mm(out, lhsT, rhs, start, stop, r, w):
            S.op("pe", lambda h: h.matmul(out, lhsT, rhs, start=start, stop=stop), r, w, self_sync=False)

        ch_in = [S.chan() for _ in range(8)]
        S.dma("sp", ch_in[0], lambda h: h.dma_start(out=vecs[:], in_=vecs_d), writes=[vecb])
        S.dma("sp", ch_in[1], lambda h: h.dma_start(out=corec[:], in_=corec_d), writes=[coreb])
        S.dma("pool", ch_in[2], lambda h: h.dma_start(out=cmat[:], in_=cmat_d), writes=[cmatb])
        xT_v = xT_d.rearrange("(c p) t -> p c t", p=128)
        for g in range(4):
            c0, n = CG[g]
            S.dma("sp", ch_in[3 + g], lambda h, c0=c0, n=n: h.dma_start(out=hT[:, :, c0:c0 + n], in_=xT_v[:, :, c0:c0 + n]),
                  writes=[hb[c][g] for c in range(DC)])
        S.dma("sp", ch_in[7], lambda h: h.dma_start(out=hT[:, :, NT:NA], in_=metaT_d.rearrange("(c p) t -> p c t", p=128)),
              writes=[hb[c][4] for c in range(DC)])
        dve(lambda h: h.memset(ones_bf[:], 1.0), w=[onesb])
        dve(lambda h: h.memset(ones_f[:], 1.0), w=[onesfb])
        dve(lambda h: h.memset(blk_f[:], 0.0), w=[blkb])
        dve(lambda h: h.memset(blk_f[0:64, 0:64], 1.0), w=[blkb])
        dve(lambda h: h.memset(blk_f[64:128, 64:128], 1.0), w=[blkb])
        for k in range(4):
            dve(lambda h, k=k: h.tensor_copy(out=mask4[:, 0, k * 128:(k + 1) * 128], in_=cmat[:, 128:256]), r=[cmatb], w=[maskb])
            dve(lambda h, k=k: h.tensor_copy(out=mask4[:, 1, k * 128:(k + 1) * 128], in_=cmat[:, 256:384]), r=[cmatb], w=[maskb])
        for (src0, dl, do, dn) in ((28, D_LBF, D_OMLF, D_NOMLF), (36, D_LBB, D_OMLB, D_NOMLB)):
            dve(lambda h, s=src0, dl=dl: h.tensor_tensor(out=der[:, dl:dl + 4], in0=vecs[:, s:s + 4], in1=vecs[:, s + 4:s + 8], op=ALU.subtract),
                r=[vecb], w=[derb])
            act(lambda h, dl=dl, do=do: h.activation(out=der[:, do:do + 4], in_=der[:, dl:dl + 4], func=AF.Sigmoid, scale=-1.0), r=[derb], w=[derb])
            act(lambda h, dl=dl: h.activation(out=der[:, dl:dl + 4], in_=der[:, dl:dl + 4], func=AF.Sigmoid), r=[derb], w=[derb])
            dve(lambda h, do=do, dn=dn: h.tensor_scalar(out=der[:, dn:dn + 4], in0=der[:, do:do + 4], scalar1=-1.0, scalar2=None, op0=ALU.mult),
                r=[derb], w=[derb])
        act(lambda h: h.activation(out=der[:, D_INVR:D_INVR + 1], in_=vecs[:, 48:49], func=AF.Exp, scale=-float(np.log(10000.0) / 16.0)),
            r=[vecb], w=[derb])
        dve(lambda h: h.tensor_scalar(out=der[:, D_INVR:D_INVR + 1], in0=der[:, D_INVR:D_INVR + 1], scalar1=float(1.0 / (2 * np.pi)), scalar2=None, op0=ALU.mult),
            r=[derb], w=[derb])
        dve(lambda h: h.tensor_scalar(out=der[:, D_SINSC:D_SINSC + 1], in0=vecs[:, 51:52], scalar1=-TWO_PI, scalar2=None, op0=ALU.mult),
            r=[vecb], w=[derb])

        class Slots:
            def __init__(self, c, n, width):
                self.t = [sbt(c, "slot%d_%d" % (id(self) % 1000, i), [128, width], BF16) for i in range(n)]
                self.b = [Buf("slot") for _ in range(n)]
                self.ch = [S.chan() for _ in range(n)]
                self.n = n
                self.i = 0

            def load(self, src, width):
                k = self.i % self.n
                self.i += 1
                t, b = self.t[k], self.b[k]
                S.dma("pool", self.ch[k], lambda h: h.dma_start(out=t[:, 0:width], in_=src), writes=[b])
                return t, b

        def rmsnorm_all(c, wcol0, uT, ub, groups, coff=0):
            with ExitStack() as lc:
                sq = [sbt(lc, "nsq%d" % i, [128, DC, 512], BF16) for i in range(2)]
                sqb = [Buf("sq") for _ in range(2)]
                t1 = sbt(lc, "nt1", [128, 512]); t1b = Buf("t1")
                rs = sbt(lc, "nrs", [128, 512]); rsb = Buf("rs")
                for gi, g in enumerate(groups):
                    c0, n = CG[g]
                    q = gi % 2
                    for cc in range(DC):
                        act(lambda h, cc=cc: h.activation(out=sq[q][:, cc, 0:n], in_=hT[:, cc, c0:c0 + n], func=AF.Square),
                            r=[hb[cc][g]], w=[sqb[q]])
                    for cc in range(DC):
                        mm(PS[6][:, 0:n], ones_bf[:, :], sq[q][:, cc, 0:n], cc == 0, cc == DC - 1, [sqb[q], onesb], [pb[6]])
                    act(lambda h: h.activation(out=t1[:, 0:n], in_=PS[6][:, 0:n], func=AF.Ln, scale=1.0 / D, bias=EPS), r=[pb[6]], w=[t1b])
                    act(lambda h: h.activation(out=rs[:, 0:n], in_=t1[:, 0:n], func=AF.Exp, scale=-0.5), r=[t1b], w=[rsb])
                    for cc in range(DC):
                        dve(lambda h, cc=cc: h.scalar_tensor_tensor(out=uT[:, cc, c0 - coff:c0 - coff + n], in0=hT[:, cc, c0:c0 + n],
                                                                    scalar=vecs[:, wcol0 + cc:wcol0 + cc + 1], in1=rs[:, 0:n],
                                                                    op0=ALU.mult, op1=ALU.mult),
                            r=[hb[cc][g], rsb, vecb], w=[ub[g]])
                S.barrier()

        def ffn(c, slots, wg, wu, wd, uT, ub, groups):
            with ExitStack() as lc:
                aT = sbt(lc, "aT", [128, 6, NA], BF16)
                ab = [[Buf("a") for _ in CG] for _ in range(6)]
                sg = [sbt(lc, "fsg%d" % i, [128, 512]) for i in range(2)]
                sgb = [Buf("sg") for _ in range(2)]
                rot = 0
                for (j0, nj) in ((0, 6), (6, 6), (12, 5), (17, 5)):
                    for jj in range(nj):
                        j = j0 + jj
                        tg, bg = slots.load(wg[j], 1024)
                        tu, bu = slots.load(wu[j], 1024)
                        for g in groups:
                            c0, n = CG[g]
                            r = rot % 2
                            rot += 1
                            Gp, Up = PS[r], PS[2 + r]
                            for kc in range(DC):
                                mm(Gp[:, 0:n], tg[:, kc * 128:(kc + 1) * 128], uT[:, kc, c0:c0 + n], kc == 0, kc == DC - 1, [bg, ub[g]], [pb[r]])
                            for kc in range(DC):
                                mm(Up[:, 0:n], tu[:, kc * 128:(kc + 1) * 128], uT[:, kc, c0:c0 + n], kc == 0, kc == DC - 1, [bu, ub[g]], [pb[2 + r]])
                            act(lambda h, r=r, Gp=Gp: h.activation(out=sg[r][:, 0:n], in_=Gp[:, 0:n], func=AF.Silu), r=[pb[r]], w=[sgb[r]])
                            dve(lambda h, r=r, Up=Up, jj=jj: h.tensor_tensor(out=aT[:, jj, c0:c0 + n], in0=sg[r][:, 0:n], in1=Up[:, 0:n], op=ALU.mult),
                                r=[sgb[r], pb[2 + r]], w=[ab[jj][g]])
                    for o in range(DC):
                        td, bd = slots.load(wd[o][:, j0 * 128:(j0 + nj) * 128], nj * 128)
                        for g in groups:
                            c0, n = CG[g]
                            r = rot % 2
                            rot += 1
                            Yp = PS[4 + r]
                            for jj in range(nj):
                                mm(Yp[:, 0:n], td[:, jj * 128:(jj + 1) * 128], aT[:, jj, c0:c0 + n], jj == 0, jj == nj - 1, [bd, ab[jj][g]], [pb[4 + r]])
                            dve(lambda h, Yp=Yp, o=o: h.scalar_tensor_tensor(out=hT[:, o, c0:c0 + n], in0=Yp[:, 0:n], scalar=0.5,
                                                                             in1=hT[:, o, c0:c0 + n], op0=ALU.mult, op1=ALU.add),
                                r=[pb[4 + r], hb[o][g]], w=[hb[o][g]])
                S.barrier()

        def rope_tables(lc_t, tdiv, tmod, tdb, g, want):
            (Ct, Cb, St, Sb, tmp, tmpb, tmi, tmib, rowoff, rowoffb) = lc_t
            c0, n = CG[g]
            dve(lambda h: h.tensor_scalar(out=rowoff[:, 0:1], in0=corec[:, 0:1], scalar1=float(c0 // 64), scalar2=vecs[:, 49:50],
                                          op0=ALU.add, op1=ALU.mult), r=[coreb, vecb], w=[rowoffb])
            dve(lambda h: h.tensor_scalar(out=tmp[0][:, :], in0=tdiv[:, :], scalar1=vecs[:, 49:50], scalar2=rowoff[:, 0:1],
                                          op0=ALU.mult, op1=ALU.add), r=[tdb, vecb, rowoffb], w=[tmpb[0]])
            dve(lambda h: h.scalar_tensor_tensor(out=tmp[1][:, :], in0=tmod[:, :], scalar=vecs[:, 50:51], in1=tmp[0][:, :],
                                                 op0=ALU.mult, op1=ALU.add), r=[tdb, vecb, tmpb[0]], w=[tmpb[1]])
            for (shift, T, Tb, sc) in ((0.0, St, Sb, der[:, D_SINSC:D_SINSC + 1]), (0.25, Ct, Cb, -TWO_PI)):
                dve(lambda h, shift=shift: h.tensor_scalar(out=tmp[0][:, :], in0=tmp[1][:, :], scalar1=der[:, D_INVR:D_INVR + 1], scalar2=shift,
                                                           op0=ALU.mult, op1=ALU.add), r=[tmpb[1], derb], w=[tmpb[0]])
                dve(lambda h: h.tensor_copy(out=tmi[:, :], in_=tmp[0][:, :]), r=[tmpb[0]], w=[tmib])
                dve(lambda h: h.tensor_copy(out=tmp[2][:, :], in_=tmi[:, :]), r=[tmib], w=[tmpb[2]])
                dve(lambda h: h.tensor_tensor(out=tmp[0][:, :], in0=tmp[0][:, :], in1=tmp[2][:, :], op=ALU.subtract), r=[tmpb[0], tmpb[2]], w=[tmpb[0]])
                dve(lambda h: h.scalar_tensor_tensor(out=tmp[2][:, :], in0=tmp[0][:, :], scalar=0.5, in1=tmp[0][:, :],
                                                     op0=ALU.is_gt, op1=ALU.subtract), r=[tmpb[0]], w=[tmpb[2]])
                act(lambda h, T=T, sc=sc: h.activation(out=T[:, :], in_=tmp[2][:, :], func=AF.Sin, scale=sc), r=[tmpb[2], derb], w=[Tb])

        def make_rope_ctx(lc):
            Ct = sbt(lc, "ropeC", [128, 512]); St = sbt(lc, "ropeS", [128, 512])
            tmp = [sbt(lc, "ropet%d" % i, [128, 512]) for i in range(3)]
            tmi = sbt(lc, "ropei", [128, 512], I32)
            rowoff = sbt(lc, "rowoff", [128, 1])
            tdiv = sbt(lc, "tdiv", [128, 512]); tmod = sbt(lc, "tmod", [128, 512]); tdb = Buf("td")
            pool(lambda h: h.iota(tdiv[:, :], [[1, 8], [0, 64]], base=0, channel_multiplier=0, allow_small_or_imprecise_dtypes=True), w=[tdb])
            pool(lambda h: h.iota(tmod[:, :], [[0, 8], [1, 64]], base=0, channel_multiplier=0, allow_small_or_imprecise_dtypes=True), w=[tdb])
            return (Ct, Buf("C"), St, Buf("S"), tmp, [Buf("t") for _ in range(3)], tmi, Buf("ti"), rowoff, Buf("ro")), tdiv, tmod, tdb

        def qk_post(lcw, Zp, zpb, Zsp, zspb, n, gcol, gscol, rope, out_ap, outb):
            (sqf, sqfb, t1, t1b, rs, rsb, qn, qnb, qs, qsb) = lcw
            act(lambda h: h.activation(out=sqf[:, 0:n], in_=Zp[:, 0:n], func=AF.Square), r=[zpb], w=[sqfb])
            mm(PS[6][:, 0:n], blk_f[:, :], sqf[:, 0:n], True, True, [blkb, sqfb], [pb[6]])
            act(lambda h: h.activation(out=t1[:, 0:n], in_=PS[6][:, 0:n], func=AF.Ln, scale=1.0 / 64, bias=EPS), r=[pb[6]], w=[t1b])
            act(lambda h: h.activation(out=rs[:, 0:n], in_=t1[:, 0:n], func=AF.Exp, scale=-0.5), r=[t1b], w=[rsb])
            if rope is None:
                dve(lambda h: h.scalar_tensor_tensor(out=out_ap, in0=Zp[:, 0:n], scalar=vecs[:, gcol:gcol + 1], in1=rs[:, 0:n],
                                                     op0=ALU.mult, op1=ALU.mult), r=[zpb, rsb, vecb], w=outb)
                return
            Ct, Cb, St, Sb = rope
            dve(lambda h: h.scalar_tensor_tensor(out=qn[:, 0:n], in0=Zp[:, 0:n], scalar=vecs[:, gcol:gcol + 1], in1=rs[:, 0:n],
                                                 op0=ALU.mult, op1=ALU.mult), r=[zpb, rsb, vecb], w=[qnb])
            dve(lambda h: h.scalar_tensor_tensor(out=qs[:, 0:n], in0=Zsp[:, 0:n], scalar=vecs[:, gscol:gscol + 1], in1=rs[:, 0:n],
                                                 op0=ALU.mult, op1=ALU.mult), r=[zspb, rsb, vecb], w=[qsb])
            dve(lambda h: h.tensor_tensor(out=qn[:, 0:n], in0=qn[:, 0:n], in1=Ct[:, 0:n], op=ALU.mult), r=[qnb, Cb], w=[qnb])
            dve(lambda h: h.tensor_tensor(out=qs[:, 0:n], in0=qs[:, 0:n], in1=St[:, 0:n], op=ALU.mult), r=[qsb, Sb], w=[qsb])
            dve(lambda h: h.tensor_tensor(out=out_ap, in0=qn[:, 0:n], in1=qs[:, 0:n], op=ALU.add), r=[qnb, qsb], w=outb)

        def make_qk_ctx(lc):
            names = ["sqf", "t1", "rs", "qn", "qs"]
            out = []
            for nm in names:
                out.append(sbt(lc, "qk_" + nm, [128, 512]))
                out.append(Buf(nm))
            return tuple(out)

        def hg_gates(lcw, Zf, zfb, n, L, d, h, full, qsil, qsilb, khT_ap, khb, dcol_ap, dcolb, blc_ap,
                     qt_ap=None, kt_ap=None, qkb=None, ecol_ap=None):
            (sgt, sgb_, lf, lfb, kk, kkb, cum, cumb, bb, bbb, nb, nbb, E, Eb) = lcw
            nch = n // L
            lb_c, oml_c, noml_c = (D_LBF, D_OMLF, D_NOMLF) if d == 0 else (D_LBB, D_OMLB, D_NOMLB)
            act(lambda hh: hh.activation(out=sgt[:, 0:n], in_=Zf[:, 0:n], func=AF.Sigmoid), r=[zfb], w=[sgb_])
            act(lambda hh: hh.activation(out=lf[:, 0:n], in_=sgt[:, 0:n], func=AF.Ln, scale=der[:, oml_c + h:oml_c + h + 1],
                                         bias=der[:, lb_c + h:lb_c + h + 1]), r=[sgb_, derb], w=[lfb])
            dve(lambda hh: hh.tensor_scalar(out=kk[:, 0:n], in0=sgt[:, 0:n], scalar1=der[:, noml_c + h:noml_c + h + 1],
                                            scalar2=der[:, oml_c + h:oml_c + h + 1], op0=ALU.mult, op1=ALU.add), r=[sgb_, derb], w=[kkb])
            for c in range(nch):
                dve(lambda hh, c=c: hh.tensor_tensor_scan(out=cum[:, c * L:(c + 1) * L], data0=ones_f[:, 0:L], data1=lf[:, c * L:(c + 1) * L],
                                                           initial=0.0, op0=ALU.mult, op1=ALU.add), r=[onesfb, lfb], w=[cumb])
            if d == 0:
                b, bbuf = cum, cumb
                last = lambda c: (c + 1) * L - 1
                mid = lambda c: c * L + L // 2 - 1
            else:
                dve(lambda hh: hh.tensor_tensor(out=bb[:, 0:n], in0=lf[:, 0:n], in1=cum[:, 0:n], op=ALU.subtract), r=[lfb, cumb], w=[bbb])
                for c in range(nch):
                    dve(lambda hh, c=c: hh.tensor_scalar(out=bb[:, c * L:(c + 1) * L], in0=bb[:, c * L:(c + 1) * L],
                                                         scalar1=cum[:, (c + 1) * L - 1:(c + 1) * L], scalar2=None, op0=ALU.add),
                        r=[bbb, cumb], w=[bbb])
                b, bbuf = bb, bbb
                last = lambda c: c * L
                mid = lambda c: c * L + L // 2
            for c in range(nch):
                act(lambda hh, c=c: hh.activation(out=E[:, c * L:(c + 1) * L], in_=b[:, c * L:(c + 1) * L], func=AF.Exp, scale=-1.0,
                                                  bias=b[:, last(c):last(c) + 1]), r=[bbuf], w=[Eb])
            dve(lambda hh: hh.tensor_tensor(out=khT_ap, in0=kk[:, 0:n], in1=E[:, 0:n], op=ALU.mult), r=[kkb, Eb], w=[khb])
            for c in range(nch):
                act(lambda hh, c=c: hh.activation(out=dcol_ap[:, c:c + 1], in_=b[:, last(c):last(c) + 1], func=AF.Exp), r=[bbuf], w=[dcolb])
                dve(lambda hh, c=c: hh.tensor_copy(out=blc_ap[:, c:c + 1], in_=b[:, last(c):last(c) + 1]), r=[bbuf], w=[dcolb])
            if not full:
                return
            dve(lambda hh: hh.tensor_scalar(out=nb[:, 0:n], in0=b[:, 0:n], scalar1=-1.0, scalar2=None, op0=ALU.mult), r=[bbuf], w=[nbb])
            for c in range(nch):
                act(lambda hh, c=c: hh.activation(out=E[:, c * L:(c + 1) * L], in_=b[:, c * L:(c + 1) * L], func=AF.Exp, scale=1.0,
                                                  bias=nb[:, mid(c):mid(c) + 1]), r=[bbuf, nbb, khb], w=[Eb])
            dve(lambda hh: hh.tensor_tensor(out=qt_ap, in0=qsil[:, 0:n], in1=E[:, 0:n], op=ALU.mult), r=[qsilb, Eb], w=[qkb])
            for c in range(nch):
                act(lambda hh, c=c: hh.activation(out=E[:, c * L:(c + 1) * L], in_=b[:, c * L:(c + 1) * L], func=AF.Exp, scale=-1.0,
                                                  bias=b[:, mid(c):mid(c) + 1]), r=[bbuf, qkb], w=[Eb])
            dve(lambda hh: hh.tensor_tensor(out=kt_ap, in0=kk[:, 0:n], in1=E[:, 0:n], op=ALU.mult), r=[kkb, Eb], w=[qkb])
            for c in range(nch):
                act(lambda hh, c=c: hh.activation(out=ecol_ap[:, c:c + 1], in_=b[:, mid(c):mid(c) + 1], func=AF.Exp), r=[bbuf], w=[dcolb])

        def make_gate_ctx(lc):
            out = []
            for nm in ["sg", "lf", "kk", "cum", "bb", "nb", "E"]:
                out.append(sbt(lc, "hg_" + nm, [128, 512]))
                out.append(Buf(nm))
            return tuple(out)

        cY = ExitStack()
        with ExitStack() as cA:
            slots = Slots(cA, 8, 1024)
            uT = sbt(cA, "uT", [128, DC, NA], BF16)
            ub = [Buf("u") for _ in CG]
            sbin0 = sbt(cA, "sbin0", [128, 4, 16, 128], BF16); sbin0b = [Buf("sbin0") for _ in range(4)]
            pbcol = sbt(cA, "pbcol", [128, 4, 16]); pbcolb = [Buf("pbcol") for _ in range(4)]
            s0f = sbt(cA, "s0f", [128, 4, 128]); s0fb = Buf("s0f")
            s0b = sbt(cA, "s0b", [128, 4, 128]); s0bb = Buf("s0b")
            cX = ExitStack()
            smeta = sbt(cX, "smeta", [128, 4, 128]); smetab = Buf("smeta")
            hgpay = sbt(cX, "hgpay", [128, HGW]); hgpayb = Buf("hgpay")
            ALLG = [0, 1, 2, 3, 4]
            rmsnorm_all(cA, 0, uT, ub, ALLG)
            ffn(cA, slots, wg_d[0], wu_d[0], wd_d[0], uT, ub, ALLG)
            rmsnorm_all(cA, 8, uT, ub, ALLG)

            with ExitStack() as c4:
                KT_loc = sbt(c4, "KT_loc", [128, NA], BF16); ktb = Buf("ktloc")
                V_loc = sbt(c4, "V_loc", [128, 17, 130], BF16); vlb = Buf("vloc")
                ropec, tdiv, tmod, tdb = make_rope_ctx(c4)
                qkc = make_qk_ctx(c4)
                dve(lambda h: h.memset(V_loc[:, :, 64:65], 1.0), w=[vlb])
                dve(lambda h: h.memset(V_loc[:, :, 129:130], 1.0), w=[vlb])
                tk, bk = slots.load(wfm_d[24], 1024)
                tks, bks = slots.load(wfm_d[25], 1024)
                for g in ALLG:
                    c0, n = CG[g]
                    r = g % 2
                    for kc in range(DC):
                        mm(PS[r][:, 0:n], tk[:, kc * 128:(kc + 1) * 128], uT[:, kc, c0:c0 + n], kc == 0, kc == DC - 1, [bk, ub[g]], [pb[r]])
                    for kc in range(DC):
                        mm(PS[2 + r][:, 0:n], tks[:, kc * 128:(kc + 1) * 128], uT[:, kc, c0:c0 + n], kc == 0, kc == DC - 1, [bks, ub[g]], [pb[2 + r]])
                    if g < 4:
                        rope_tables(ropec, tdiv, tmod, tdb, g, None)
                        rp = (ropec[0], ropec[1], ropec[2], ropec[3])
                    else:
                        rp = None
                    qk_post(qkc, PS[r], pb[r], PS[2 + r], pb[2 + r], n, 45, 47, rp, KT_loc[:, c0:c0 + n], [ktb])
                tv, bv = slots.load(wv_d, 1024)
                for blk in range(17):
                    nb_ = 128 if blk < 16 else NM
                    t0 = blk * 128
                    r = blk % 2
                    for kc in range(DC):
                        mm(PS[4 + r][0:nb_, 0:128], uT[:, kc, t0:t0 + nb_], tv[:, kc * 128:(kc + 1) * 128], kc == 0, kc == DC - 1,
                           [bv, ub[min(blk // 4, 4)]], [pb[4 + r]])
                    act(lambda h, r=r, blk=blk, nb_=nb_: h.copy(out=V_loc[0:nb_, blk, 0:64], in_=PS[4 + r][0:nb_, 0:64]), r=[pb[4 + r]], w=[vlb])
                    dve(lambda h, r=r, blk=blk, nb_=nb_: h.tensor_copy(out=V_loc[0:nb_, blk, 65:129], in_=PS[4 + r][0:nb_, 64:128]), r=[pb[4 + r]], w=[vlb])
                ch_k, ch_v = S.chan(), S.chan()
                ktlb, ktab, vlob, vab = Buf("ktl_d"), Buf("kta_d"), Buf("vl_d"), Buf("va_d")
                S.dma("sp", ch_k, lambda h: h.dma_start(out=kt_loc_d.ap(), in_=KT_loc[:, 0:NT]), reads=[ktb], writes=[ktlb])
                S.dma("sp", ch_v, lambda h: h.dma_start(out=v_loc_d.ap().rearrange("p (b n) -> p b n", n=130), in_=V_loc[:, 0:16, :]),
                      reads=[vlb], writes=[vlob])
                ch_cc = [S.chan() for _ in range(3)]
                S.wait_bufs("pool", slots.b)
                S.dma("pool", ch_cc[0], lambda h: h.collective_compute("AllGather", ALU.bypass, replica_groups=[list(range(NCORE))],
                                                                        ins=[kt_loc_d.ap().opt()], outs=[kt_all_d.ap().opt()]),
                      reads=[ktlb], writes=[ktab], inc=1)
                S.wait_bufs("pool", [ktab])
                S.dma("pool", ch_cc[1], lambda h: h.collective_compute("AllGather", ALU.bypass, replica_groups=[list(range(NCORE))],
                                                                        ins=[v_loc_d.ap().opt()], outs=[v_all_d.ap().opt()]),
                      reads=[vlob], writes=[vab], inc=1)
                S.wait_bufs("pool", [vab])
                dve(lambda h: h.tensor_copy(out=KT_meta[:, :], in_=KT_loc[:, NT:NA]), r=[ktb], w=[ktmb])
                dve(lambda h: h.tensor_copy(out=V_meta[0:NM, :], in_=V_loc[0:NM, 16, :]), r=[vlb], w=[vmb])
                S.barrier()

            with ExitStack() as c5:
                gatec = make_gate_ctx(c5)
                v_h = sbt(c5, "v_h", [128, 17, 128], BF16); vhb = Buf("vh")
                khT = sbt(c5, "khT", [128, NA], BF16); khb = Buf("khT")
                khat = sbt(c5, "khat", [128, 17, 128], BF16); khatb = Buf("khat")
                dcol = sbt(c5, "dcol", [128, 17]); dcolb = Buf("dcol")
                blc = sbt(c5, "blc", [128, 17])
                Sst = sbt(c5, "Sst", [128, 128]); Sstb = Buf("Sst")
                prun = sbt(c5, "prun", [128, 2]); prunb = Buf("prun")
                for h in range(4):
                    tzi, bzi = slots.load(wzi_d[h], 1024)
                    for blk in range(17):
                        nb_ = 128 if blk < 16 else NM
                        t0 = blk * 128
                        r = blk % 2
                        for kc in range(DC):
                            mm(PS[2 + r][0:nb_, 0:128], uT[:, kc, t0:t0 + nb_], tzi[:, kc * 128:(kc + 1) * 128], kc == 0, kc == DC - 1,
                               [bzi, ub[min(blk // 4, 4)]], [pb[2 + r]])
                        if blk % 2 == 0:
                            act(lambda hh, r=r, blk=blk, nb_=nb_: hh.copy(out=v_h[0:nb_, blk, :], in_=PS[2 + r][0:nb_, 0:128]), r=[pb[2 + r]], w=[vhb])
                        else:
                            dve(lambda hh, r=r, blk=blk, nb_=nb_: hh.tensor_copy(out=v_h[0:nb_, blk, :], in_=PS[2 + r][0:nb_, 0:128]), r=[pb[2 + r]], w=[vhb])
                    for d in range(2):
                        tf, bf_ = slots.load(wfm_d[4 + 4 * d + h], 1024)
                        groups = ALLG if d == 0 else [0, 1, 2, 3]
                        for g in groups:
                            c0, n = CG[g]
                            r = g % 2
                            for kc in range(DC):
                                mm(PS[r][:, 0:n], tf[:, kc * 128:(kc + 1) * 128], uT[:, kc, c0:c0 + n], kc == 0, kc == DC - 1, [bf_, ub[g]], [pb[r]])
                            L = 128 if g < 4 else NM
                            hg_gates(gatec, PS[r], pb[r], n, L, d, h, False, None, None, khT[:, c0:c0 + n], khb,
                                     dcol[:, 4 * g:4 * g + n // L], dcolb, blc[:, 4 * g:4 * g + n // L])
                        nblk = 17 if d == 0 else 16
                        for b0 in range(0, nblk, 4):
                            bl = list(range(b0, min(b0 + 4, nblk)))
                            for i, blk in enumerate(bl):
                                nb_ = 128 if blk < 16 else NM
                                S.op("pe", lambda hh, i=i, blk=blk, nb_=nb_: hh.transpose(PT[0:nb_, i * 128:(i + 1) * 128], khT[:, blk * 128:blk * 128 + nb_], cmat[:, 0:128]),
                                     [khb, cmatb], [ptb], self_sync=False)
                            for i, blk in enumerate(bl):
                                nb_ = 128 if blk < 16 else NM
                                dve(lambda hh, i=i, blk=blk, nb_=nb_: hh.tensor_copy(out=khat[0:nb_, blk, :], in_=PT[0:nb_, i * 128:(i + 1) * 128]), r=[ptb], w=[khatb])
                        if d == 0:
                            mm(PS[4][:, 0:128], khat[0:NM, 16, :], v_h[0:NM, 16, :], True, True, [khatb, vhb], [pb[4]])
                            act(lambda hh, h=h: hh.copy(out=smeta[:, h, :], in_=PS[4][:, 0:128]), r=[pb[4]], w=[smetab])
                        order = list(range(16)) if d == 0 else list(range(15, -1, -1))
                        for idx, j in enumerate(order):
                            r = idx % 2
                            mm(PS[4 + r][:, 0:128], khat[:, j, :], v_h[:, j, :], True, True, [khatb, vhb], [pb[4 + r]])
                            if d == 1:
                                if idx == 0:
                                    dve(lambda hh, h=h, j=j: hh.memset(sbin0[:, h, j, :], 0.0), w=[sbin0b[h]])
                                    dve(lambda hh, h=h, j=j: hh.memset(pbcol[:, h, j:j + 1], 1.0), w=[pbcolb[h]])
                                    dve(lambda hh: hh.memset(prun[:, 0:1], 1.0), w=[prunb])
                                else:
                                    dve(lambda hh, h=h, j=j: hh.tensor_copy(out=sbin0[:, h, j, :], in_=Sst[:, :]), r=[Sstb], w=[sbin0b[h]])
                                    dve(lambda hh, h=h, j=j: hh.tensor_copy(out=pbcol[:, h, j:j + 1], in_=prun[:, 0:1]), r=[prunb], w=[pbcolb[h]])
                                dve(lambda hh, j=j: hh.tensor_tensor(out=prun[:, 0:1], in0=prun[:, 0:1], in1=dcol[:, j:j + 1], op=ALU.mult),
                                    r=[prunb, dcolb], w=[prunb])
                            if idx == 0:
                                act(lambda hh, r=r: hh.copy(out=Sst[:, :], in_=PS[4 + r][:, 0:128]), r=[pb[4 + r]], w=[Sstb])
                            else:
                                dve(lambda hh, r=r, j=j: hh.scalar_tensor_tensor(out=Sst[:, :], in0=Sst[:, :], scalar=dcol[:, j:j + 1], in1=PS[4 + r][:, 0:128],
                                                                                 op0=ALU.mult, op1=ALU.add), r=[Sstb, dcolb, pb[4 + r]], w=[Sstb])
                        off = (0 if d == 0 else 512) + h * 128
                        dve(lambda hh, off=off: hh.tensor_copy(out=hgpay[:, off:off + 128], in_=Sst[:, :]), r=[Sstb], w=[hgpayb])
                        dve(lambda hh: hh.tensor_reduce(out=prun[:, 1:2], in_=blc[:, 0:16], op=ALU.add, axis=mybir.AxisListType.X), r=[dcolb], w=[prunb])
                        ac = 1024 + 4 * d + h
                        act(lambda hh, ac=ac: hh.activation(out=hgpay[:, ac:ac + 1], in_=prun[:, 1:2], func=AF.Exp), r=[prunb], w=[hgpayb])
                S.barrier()
            ch_h = S.chan()
            hglb, hgab = Buf("hgl"), Buf("hga")
            S.dma("sp", ch_h, lambda h: h.dma_start(out=hg_loc_d.ap(), in_=hgpay[:, :]), reads=[hgpayb], writes=[hglb])
            S.wait_bufs("pool", slots.b)
            S.dma("pool", ch_cc[2], lambda h: h.collective_compute("AllGather", ALU.bypass, replica_groups=[list(range(NCORE))],
                                                                    ins=[hg_loc_d.ap().opt()], outs=[hg_all_d.ap().opt()]),
                  reads=[hglb], writes=[hgab], inc=1)
            S.wait_bufs("pool", [hgab])
            with ExitStack() as c6:
                stg = [sbt(c6, "hgstg%d" % i, [128, HGW]) for i in range(2)]
                stgb = [Buf("stg") for _ in range(2)]
                stch = [S.chan() for _ in range(2)]
                aeff = sbt(c6, "aeff", [128, 8]); aeffb = Buf("aeff")
                tmpB = sbt(c6, "tmpB", [128, 128]); tmpBb = Buf("tmpB")
                dve(lambda hh: hh.tensor_copy(out=s0f[:, :, :], in_=smeta[:, :, :]), r=[smetab], w=[s0fb])
                dve(lambda hh: hh.memset(s0b[:, :, :], 0.0), w=[s0bb])
                hg_v = hg_all_d.ap().rearrange("(r p) n -> r p n", p=128)
                k = 0
                for (d, order) in ((0, list(range(NCORE))), (1, list(range(NCORE - 1, -1, -1)))):
                    st, stb_ = (s0f, s0fb) if d == 0 else (s0b, s0bb)
                    for j in order:
                        q = k % 2
                        k += 1
                        S.dma("sp", stch[q], lambda h, q=q, j=j: h.dma_start(out=stg[q][:, :], in_=hg_v[j]), reads=[hgab], writes=[stgb[q]])
                        mcol = (1 + j) if d == 0 else (9 + j)
                        dve(lambda hh, q=q, d=d, mcol=mcol: hh.tensor_scalar(out=aeff[:, 0:4], in0=stg[q][:, 1024 + 4 * d:1028 + 4 * d], scalar1=-1.0,
                                                                            scalar2=corec[:, mcol:mcol + 1], op0=ALU.add, op1=ALU.mult),
                            r=[stgb[q], coreb], w=[aeffb])
                        dve(lambda hh: hh.tensor_scalar(out=aeff[:, 0:4], in0=aeff[:, 0:4], scalar1=1.0, scalar2=None, op0=ALU.add), r=[aeffb], w=[aeffb])
                        for h in range(4):
                            off = 512 * d + h * 128
                            dve(lambda hh, q=q, off=off, mcol=mcol: hh.tensor_scalar(out=tmpB[:, :], in0=stg[q][:, off:off + 128],
                                                                                    scalar1=corec[:, mcol:mcol + 1], scalar2=None, op0=ALU.mult),
                                r=[stgb[q], coreb], w=[tmpBb])
                            dve(lambda hh, h=h, st=st: hh.scalar_tensor_tensor(out=st[:, h, :], in0=st[:, h, :], scalar=aeff[:, h:h + 1], in1=tmpB[:, :],
                                                                               op0=ALU.mult, op1=ALU.add), r=[aeffb, tmpBb, stb_], w=[stb_])
                S.barrier()
            cX.close()

            yaT = sbr(cY, "yaT", [128, 4, NT], BF16)
            yab = [[Buf("ya") for _ in range(4)] for _ in range(4)]
            QT = sbr(cY, "QT", [128, 4, NT], BF16)
            qtb = [[[Buf("qt") for _ in range(4)] for _ in range(4)] for _ in range(4)]
            with ExitStack() as c7:
                gatec = make_gate_ctx(c7)
                qsil = sbt(c7, "qsil", [128, 512]); qsilb = Buf("qsil")
                zgs = sbt(c7, "zgs", [128, 512]); zgsb = Buf("zgs")
                v_h = sbt(c7, "v_h2", [128, 4, 128], BF16); vhb = Buf("vh")
                khT = sbt(c7, "khT2", [128, 512], BF16); khb = Buf("khT")
                khat = sbt(c7, "khat2", [128, 4, 128], BF16); khatb = Buf("khat")
                qtl = [sbt(c7, "qtl%d" % i, [128, 512], BF16) for i in range(2)]
                ktl = [sbt(c7, "ktl%d" % i, [128, 512], BF16) for i in range(2)]
                qkb = [Buf("qk") for _ in range(2)]
                dcol = sbt(c7, "dcol2", [128, 2, 4]); dcolb = Buf("dcol")
                ecol = sbt(c7, "ecol2", [128, 2, 4])
                blc = sbt(c7, "blc2", [128, 8])
                AT = [sbt(c7, "AT%d" % i, [128, 512], BF16) for i in range(2)]
                ATb = [Buf("AT") for _ in range(2)]
                Sf = sbt(c7, "Sf", [128, 128]); Sfb = Buf("Sf")
                Sp = [sbt(c7, "Sp%d" % i, [128, 128], BF16) for i in range(2)]
                Spb = [Buf("Sp") for _ in range(2)]
                stmp = sbt(c7, "stmp", [128, 128]); stmpb = Buf("stmp")
                osq = sbt(c7, "osq", [128, 512], BF16); osqb = Buf("osq")
                ot1, ot1b, ors, orsb, oy, oyb = gatec[0], gatec[1], gatec[2], gatec[3], gatec[4], gatec[5]
                for h in range(4):
                    tq, bq = slots.load(wfm_d[0 + h], 1024)
                    tff, bff = slots.load(wfm_d[4 + h], 1024)
                    tfb, bfb = slots.load(wfm_d[8 + h], 1024)
                    tg_, bg_ = slots.load(wfm_d[12 + h], 1024)
                    tzi, bzi = slots.load(wzi_d[h], 1024)
                    dve(lambda hh, h=h: hh.tensor_copy(out=Sf[:, :], in_=s0f[:, h, :]), r=[s0fb], w=[Sfb])
                    for g in range(4):
                        c0, n = CG[g]
                        for (pi, tw, bw) in ((0, tq, bq), (1, tff, bff), (2, tfb, bfb), (3, tg_, bg_)):
                            for kc in range(DC):
                                mm(PS[pi][:, 0:n], tw[:, kc * 128:(kc + 1) * 128], uT[:, kc, c0:c0 + n], kc == 0, kc == DC - 1, [bw, ub[g]], [pb[pi]])
                        for bi in range(4):
                            t0 = c0 + bi * 128
                            for kc in range(DC):
                                mm(PS[4][:, bi * 128:(bi + 1) * 128], uT[:, kc, t0:t0 + 128], tzi[:, kc * 128:(kc + 1) * 128], kc == 0, kc == DC - 1,
                                   [bzi, ub[g]], [pb[4]])
                        act(lambda hh: hh.activation(out=qsil[:, :], in_=PS[0][:, :], func=AF.Silu), r=[pb[0]], w=[qsilb])
                        act(lambda hh: hh.activation(out=zgs[:, :], in_=PS[3][:, :], func=AF.Silu), r=[pb[3]], w=[zgsb])
                        dve(lambda hh: hh.tensor_copy(out=v_h[:, :, :], in_=PS[4][:, :].rearrange("p (b n) -> p b n", n=128)), r=[pb[4]], w=[vhb])
                        for d in range(2):
                            hg_gates(gatec, PS[1 + d], pb[1 + d], n, 128, d, h, True, qsil, qsilb, khT[:, :], khb,
                                     dcol[:, d, :], dcolb, blc[:, 4 * d:4 * d + 4], qtl[d][:, :], ktl[d][:, :], qkb[d], ecol[:, d, :])
                            if d == 0:
                                for i in range(4):
                                    S.op("pe", lambda hh, i=i: hh.transpose(PT[:, i * 128:(i + 1) * 128], khT[:, i * 128:(i + 1) * 128], cmat[:, 0:128]),
                                         [khb, cmatb], [ptb], self_sync=False)
                                dve(lambda hh: hh.tensor_copy(out=khat[:, :, :], in_=PT[:, 0:512].rearrange("p (b n) -> p b n", n=128)), r=[ptb], w=[khatb])
                        for d in range(2):
                            for i in range(4):
                                mm(PS[d][:, i * 128:(i + 1) * 128], ktl[d][:, i * 128:(i + 1) * 128], qtl[d][:, i * 128:(i + 1) * 128], True, True,
                                   [qkb[d]], [pb[d]])
                            dve(lambda hh, d=d: hh.tensor_tensor(out=AT[d][:, :], in0=PS[d][:, :], in1=mask4[:, d, :], op=ALU.mult),
                                r=[pb[d], maskb], w=[ATb[d]])
                        for i in range(4):
                            j = g * 4 + i
                            dve(lambda hh, i=i: hh.tensor_scalar(out=Sp[0][:, :], in0=Sf[:, :], scalar1=ecol[:, 0, i:i + 1], scalar2=None, op0=ALU.mult),
                                r=[Sfb, dcolb], w=[Spb[0]])
                            dve(lambda hh, h=h, j=j: hh.scalar_tensor_tensor(out=stmp[:, :], in0=s0b[:, h, :], scalar=pbcol[:, h, j:j + 1], in1=sbin0[:, h, j, :],
                                                                             op0=ALU.mult, op1=ALU.add), r=[s0bb, pbcolb[h], sbin0b[h]], w=[stmpb])
                            dve(lambda hh, i=i: hh.tensor_scalar(out=Sp[1][:, :], in0=stmp[:, :], scalar1=ecol[:, 1, i:i + 1], scalar2=None, op0=ALU.mult),
                                r=[stmpb, dcolb], w=[Spb[1]])
                            osl = PS[2][:, i * 128:(i + 1) * 128]
                            mm(osl, v_h[:, i, :], AT[0][:, i * 128:(i + 1) * 128], True, False, [vhb, ATb[0]], [pb[2]])
                            mm(osl, v_h[:, i, :], AT[1][:, i * 128:(i + 1) * 128], False, False, [vhb, ATb[1]], [pb[2]])
                            mm(osl, Sp[0][:, :], qtl[0][:, i * 128:(i + 1) * 128], False, False, [Spb[0], qkb[0]], [pb[2]])
                            mm(osl, Sp[1][:, :], qtl[1][:, i * 128:(i + 1) * 128], False, True, [Spb[1], qkb[1]], [pb[2]])
                            mm(PS[5][:, 0:128], khat[:, i, :], v_h[:, i, :], True, True, [khatb, vhb], [pb[5]])
                            dve(lambda hh, i=i: hh.scalar_tensor_tensor(out=Sf[:, :], in0=Sf[:, :], scalar=dcol[:, 0, i:i + 1], in1=PS[5][:, 0:128],
                                                                        op0=ALU.mult, op1=ALU.add), r=[Sfb, dcolb, pb[5]], w=[Sfb])
                        act(lambda hh: hh.activation(out=osq[:, :], in_=PS[2][:, :], func=AF.Square), r=[pb[2]], w=[osqb])
                        mm(PS[3][:, :], ones_bf[:, :], osq[:, :], True, True, [onesb, osqb], [pb[3]])
                        act(lambda hh: hh.activation(out=ot1[:, :], in_=PS[3][:, :], func=AF.Ln, scale=1.0 / 128, bias=EPS), r=[pb[3]], w=[ot1b])
                        act(lambda hh: hh.activation(out=ors[:, :], in_=ot1[:, :], func=AF.Exp, scale=-0.5), r=[ot1b], w=[orsb])
                        dve(lambda hh, h=h: hh.scalar_tensor_tensor(out=oy[:, :], in0=PS[2][:, :], scalar=vecs[:, 24 + h:25 + h], in1=ors[:, :],
                                                                    op0=ALU.mult, op1=ALU.mult), r=[pb[2], orsb, vecb], w=[oyb])
                        dve(lambda hh, h=h, c0=c0: hh.tensor_tensor(out=yaT[:, h, c0:c0 + 512], in0=oy[:, :], in1=zgs[:, :], op=ALU.mult),
                            r=[oyb, zgsb], w=[yab[h][g]])
                S.barrier()
            with ExitStack() as c8:
                ropec, tdiv, tmod, tdb = make_rope_ctx(c8)
                qkc = make_qk_ctx(c8)
                wq = [slots.load(wfm_d[16 + c], 1024) for c in range(4)]
                wqs = [slots.load(wfm_d[20 + c], 1024) for c in range(4)]
                for g in range(4):
                    c0, n = CG[g]
                    rope_tables(ropec, tdiv, tmod, tdb, g, None)
                    rp = (ropec[0], ropec[1], ropec[2], ropec[3])
                    for c in range(4):
                        r = c % 2
                        for kc in range(DC):
                            mm(PS[r][:, 0:n], wq[c][0][:, kc * 128:(kc + 1) * 128], uT[:, kc, c0:c0 + n], kc == 0, kc == DC - 1, [wq[c][1], ub[g]], [pb[r]])
                        for kc in range(DC):
                            mm(PS[2 + r][:, 0:n], wqs[c][0][:, kc * 128:(kc + 1) * 128], uT[:, kc, c0:c0 + n], kc == 0, kc == DC - 1, [wqs[c][1], ub[g]], [pb[2 + r]])
                        qk_post(qkc, PS[r], pb[r], PS[2 + r], pb[2 + r], n, 44, 46, rp, QT[:, c, c0:c0 + n], qtb[c][g])
                S.barrier()
        S.barrier()

        with ExitStack() as cB:
            KT_all = sbt(cB, "KT_all", [128, NCORE * NT + NM], BF16); ktallb = Buf("ktall")
            V_all = sbt(cB, "V_all", [128, NCORE * 16 + 1, 130], BF16); vallb = Buf("vall")
            PTs = [sbt(cB, "PTs%d" % i, [128, 1024], BF16) for i in range(2)]
            PTb = [Buf("pts") for _ in range(2)]
            rinv = sbt(cB, "rinv", [128, 512]); rinvb = Buf("rinv")
            rbc = sbt(cB, "rbc", [128, 512]); rbcb = Buf("rbc")
            ch_l = [S.chan() for _ in range(2)]
            S.dma("sp", ch_l[0], lambda h: h.dma_start(out=KT_all[:, 0:NCORE * NT].rearrange("p (r n) -> p r n", n=NT),
                                                        in_=kt_all_d.ap().rearrange("(r p) n -> p r n", p=128)), reads=[ktab], writes=[ktallb])
            S.dma("sp", ch_l[1], lambda h: h.dma_start(out=V_all[:, 0:NCORE * 16, :].rearrange("p (r b) n -> p r (b n)", b=16),
                                                        in_=v_all_d.ap().rearrange("(r p) n -> p r n", p=128)), reads=[vab], writes=[vallb])
            dve(lambda h: h.tensor_copy(out=KT_all[:, NCORE * NT:NCORE * NT + NM], in_=KT_meta[:, :]), r=[ktmb], w=[ktallb])
            dve(lambda h: h.tensor_copy(out=V_all[0:NM, NCORE * 16, :], in_=V_meta[0:NM, :]), r=[vmb], w=[vallb])
            NKT = NCORE * 16 + 1
            NP = (NKT + 1) // 2
            grp = 0
            pending = [None]
            QZ = [[sbt(cB, "QZ%d_%d" % (k, i), [128, 512], BF16) for i in range(2)] for k in range(2)]
            QZb = [[Buf("qz") for _ in range(2)] for _ in range(2)]
            for k in range(2):
                for i in range(2):
                    pool(lambda hh, k=k, i=i: hh.memset(QZ[k][i][:, :], 0.0), w=[QZb[k][i]])
            qzi = [0, 0]
            pglob = 0
            for g in range(4):
                c0 = CG[g][0]
                for kvh in range(2):
                    pbase = kvh * 64
                    for half in range(2):
                        t0 = c0 + half * 256
                        for cp in range(2):
                            ob = 4 + grp % 2
                            grp += 1
                            Op = PS[ob]
                            qsl = QT[pbase:pbase + 64, 2 * cp:2 * cp + 2, t0:t0 + 256]
                            qbufs = [qtb[2 * cp][g][kvh * 2 + half], qtb[2 * cp + 1][g][kvh * 2 + half]]
                            zi = qzi[kvh] % 2
                            qzi[kvh] += 1
                            qz, qzb_ = QZ[kvh][zi], QZb[kvh][zi]
                            pool(lambda hh, qz=qz, qsl=qsl, pbase=pbase: hh.tensor_copy(out=qz[pbase:pbase + 64, :].rearrange("p (c n) -> p c n", n=256), in_=qsl),
                                 r=qbufs, w=[qzb_])
                            pg0 = pglob
                            pglob += NP

                            def qk(pp, qz=qz, qzb_=qzb_, pg0=pg0):
                                sp = (pg0 + pp) % 2
                                for hf in range(2):
                                    kt = 2 * pp + hf
                                    if kt >= NKT:
                                        continue
                                    nk = 128 if kt < NKT - 1 else NM
                                    S.op("pe", lambda hh, kt=kt, nk=nk, hf=hf: hh.matmul(PSbig[0:nk, sp * 1024 + hf * 512:sp * 1024 + hf * 512 + 512],
                                                                                       KT_all[:, kt * 128:kt * 128 + nk], qz[:, :], start=True, stop=True),
                                         [ktallb, qzb_], [pb[2 * sp + hf]], self_sync=False)

                            qk(0)
                            for pp in range(NP):
                                sp = (pg0 + pp) % 2
                                full = (2 * pp + 1 < NKT - 1) and WIDE_EXP
                                if full:
                                    act(lambda hh, sp=sp: hh.activation(out=PTs[sp][:, 0:1024], in_=PSbig[:, sp * 1024:sp * 1024 + 1024], func=AF.Exp, scale=0.125),
                                        r=[pb[2 * sp], pb[2 * sp + 1]], w=[PTb[sp]])
                                else:
                                    for hf in range(2):
                                        kt = 2 * pp + hf
                                        if kt >= NKT:
                                            continue
                                        nk = 128 if kt < NKT - 1 else NM
                                        act(lambda hh, sp=sp, hf=hf, nk=nk: hh.activation(out=PTs[sp][0:nk, hf * 512:hf * 512 + 512],
                                                                                        in_=PSbig[0:nk, sp * 1024 + hf * 512:sp * 1024 + hf * 512 + 512], func=AF.Exp, scale=0.125),
                                            r=[pb[2 * sp + hf]], w=[PTb[sp]])
                                if pp + 1 < NP:
                                    qk(pp + 1)
                                for hf in range(2):
                                    kt = 2 * pp + hf
                                    if kt >= NKT:
                                        continue
                                    nk = 128 if kt < NKT - 1 else NM
                                    S.op("pe", lambda hh, Op=Op, kt=kt, nk=nk, sp=sp, hf=hf, kvh=kvh: hh.matmul(Op[0:65, :], V_all[0:nk, kt, 65 * kvh:65 * kvh + 65],
                                                                                                            PTs[sp][0:nk, hf * 512:hf * 512 + 512],
                                                                                                            start=(kt == 0), stop=(kt == NKT - 1)),
                                         [vallb, PTb[sp]], [pb[ob]], self_sync=False)
                                if pp == 6 and pending[0] is not None:
                                    pending[0]()
                                    pending[0] = None

                            def finalize(Op=Op, ob=ob, qsl=qsl, qbufs=qbufs):
                                dve(lambda hh: hh.reciprocal(out=rinv[64:65, :], in_=Op[64:65, :]), r=[pb[ob]], w=[rinvb])
                                mm(PS[6][0:64, :], ones_f[64:65, 0:64], rinv[64:65, :], True, True, [onesfb, rinvb], [pb[6]])
                                act(lambda hh: hh.copy(out=rbc[0:64, :], in_=PS[6][0:64, :]), r=[pb[6]], w=[rbcb])
                                dve(lambda hh: hh.tensor_tensor(out=qsl, in0=Op[0:64, :].rearrange("p (c n) -> p c n", n=256),
                                                                in1=rbc[0:64, :].rearrange("p (c n) -> p c n", n=256), op=ALU.mult),
                                    r=[pb[ob], rbcb], w=qbufs)
                            pending[0] = finalize
            pending[0]()
            S.barrier()
        S.barrier()

        with ExitStack() as cC:
            slots = Slots(cC, 8, 1024)
            for half in range(2):
                HG = [2 * half, 2 * half + 1]
                hoff = half * 1024
                with ExitStack() as c9:
                    uT = sbt(c9, "uTm", [128, DC, 1024], BF16)
                    ub = [Buf("u") for _ in CG]
                    mixT = sbt(c9, "mixT", [128, DC, 1024], BF16)
                    mixb = [[Buf("mix") for _ in range(4)] for _ in range(DC)]
                    sga = [sbt(c9, "sga%d" % i, [128, 512]) for i in range(2)]; sgab = [Buf("sga") for _ in range(2)]
                    sgb2 = [sbt(c9, "sgb%d" % i, [128, 512]) for i in range(2)]; sgbb = [Buf("sgb") for _ in range(2)]
                    m1 = [sbt(c9, "m1%d" % i, [128, 512]) for i in range(2)]; m1b = [Buf("m1") for _ in range(2)]
                    m2 = [sbt(c9, "m2%d" % i, [128, 512]) for i in range(2)]; m2b = [Buf("m2") for _ in range(2)]
                    rmsnorm_all(c9, 8, uT, ub, HG, hoff)
                    rot = 0
                    for s in range(4):
                        tua, bua = slots.load(wua_d[s], 1024)
                        tub, bub = slots.load(wub_d[s], 1024)
                        for oo in range(2):
                            o = 2 * s + oo
                            tga, bga = slots.load(wfm_d[26 + o], 1024)
                            tgb, bgb = slots.load(wfm_d[34 + o], 1024)
                            for g in HG:
                                c0, n = CG[g]
                                r = rot % 2
                                rot += 1
                                za, zb, pa, pbk = 0, 1, 2 + r, 4 + r
                                for kc in range(DC):
                                    mm(PS[za][:, :], tga[:, kc * 128:(kc + 1) * 128], uT[:, kc, c0 - hoff:c0 - hoff + n], kc == 0, kc == DC - 1, [bga, ub[g]], [pb[za]])
                                for kc in range(DC):
                                    mm(PS[zb][:, :], tgb[:, kc * 128:(kc + 1) * 128], uT[:, kc, c0 - hoff:c0 - hoff + n], kc == 0, kc == DC - 1, [bgb, ub[g]], [pb[zb]])
                                for kc in range(4):
                                    mm(PS[pa][:, :], tua[:, (oo * 4 + kc) * 128:(oo * 4 + kc + 1) * 128], yaT[:, kc, c0:c0 + n], kc == 0, kc == 3,
                                       [bua, yab[kc][g]], [pb[pa]])
                                for kc in range(4):
                                    mm(PS[pbk][:, :], tub[:, (oo * 4 + kc) * 128:(oo * 4 + kc + 1) * 128], QT[:, kc, c0:c0 + n], kc == 0, kc == 3,
                                       [bub] + qtb[kc][g], [pb[pbk]])
                                act(lambda hh, r=r, za=za: hh.activation(out=sga[r][:, :], in_=PS[za][:, :], func=AF.Sigmoid), r=[pb[za]], w=[sgab[r]])
                                act(lambda hh, r=r, zb=zb: hh.activation(out=sgb2[r][:, :], in_=PS[zb][:, :], func=AF.Sigmoid), r=[pb[zb]], w=[sgbb[r]])
                                dve(lambda hh, r=r, pa=pa: hh.tensor_tensor(out=m1[r][:, :], in0=sga[r][:, :], in1=PS[pa][:, :], op=ALU.mult),
                                    r=[sgab[r], pb[pa]], w=[m1b[r]])
                                dve(lambda hh, r=r, pbk=pbk: hh.tensor_tensor(out=m2[r][:, :], in0=sgb2[r][:, :], in1=PS[pbk][:, :], op=ALU.mult),
                                    r=[sgbb[r], pb[pbk]], w=[m2b[r]])
                                dve(lambda hh, r=r, o=o, c0=c0: hh.tensor_tensor(out=mixT[:, o, c0 - hoff:c0 - hoff + 512], in0=m1[r][:, :], in1=m2[r][:, :], op=ALU.add),
                                    r=[m1b[r], m2b[r]], w=[mixb[o][g]])
                    rot = 0
                    for o in range(DC):
                        two, bwo = slots.load(wo_d[o], 1024)
                        for g in HG:
                            c0, n = CG[g]
                            r = rot % 2
                            rot += 1
                            for kc in range(DC):
                                mm(PS[4 + r][:, :], two[:, kc * 128:(kc + 1) * 128], mixT[:, kc, c0 - hoff:c0 - hoff + n], kc == 0, kc == DC - 1,
                                   [bwo, mixb[kc][g]], [pb[4 + r]])
                            dve(lambda hh, r=r, o=o, c0=c0: hh.tensor_tensor(out=hT[:, o, c0:c0 + 512], in0=PS[4 + r][:, :], in1=hT[:, o, c0:c0 + 512], op=ALU.add),
                                r=[pb[4 + r], hb[o][g]], w=[hb[o][g]])
                    S.barrier()
            cY.close()
            with ExitStack() as c10:
                uT = sbt(c10, "uT2", [128, DC, NA], BF16)
                ub = [Buf("u") for _ in CG]
                rmsnorm_all(c10, 16, uT, ub, [0, 1, 2, 3])
                ffn(c10, slots, wg_d[1], wu_d[1], wd_d[1], uT, ub, [0, 1, 2, 3])
            outT_v = outT_d.rearrange("(c p) t -> p c t", p=128)
            obufs = []
            for g in range(4):
                c0, n = CG[g]
                cho = S.chan()
                ob_ = Buf("out")
                S.dma("sp", cho, lambda h, c0=c0, n=n: h.dma_start(out=outT_v[:, :, c0:c0 + n], in_=hT[:, :, c0:c0 + n]),
                      reads=[hb[c][g] for c in range(DC)], writes=[ob_])
                obufs.append(ob_)
            S.wait_bufs("sp", obufs)
            S.barrier()
    return nc


def _fm_layout(w_cols):
    return np.ascontiguousarray(w_cols.reshape(8, 128, 128).transpose(1, 0, 2).reshape(128, 1024))


def _host_inputs(x, meta_tokens, ffn1_norm, ffn1_w_gate, ffn1_w_up, ffn1_w_down, mix_norm, w_in, hg_lb_fwd, hg_lb_bwd,
                 hg_out_norm, q_norm, k_norm, w_up_a, w_up_b, w_out, ffn2_norm, ffn2_w_gate, ffn2_w_up, ffn2_w_down):
    f32 = lambda a: np.asarray(a, dtype=np.float32)
    shared = {}
    p = np.arange(128)

    def ffn_pack(idx, wg, wu, wd):
        wg, wu, wd = f32(wg)[0], f32(wu)[0], f32(wd)[0]
        shared["wg%d" % idx] = np.stack([_fm_layout(wg[:, j * 128:(j + 1) * 128]) for j in range(JC)])
        shared["wu%d" % idx] = np.stack([_fm_layout(wu[:, j * 128:(j + 1) * 128]) for j in range(JC)])
        shared["wd%d" % idx] = np.ascontiguousarray(wd.reshape(JC, 128, DC, 128).transpose(2, 1, 0, 3).reshape(DC, 128, DFF))

    ffn_pack(1, ffn1_w_gate, ffn1_w_up, ffn1_w_down)
    ffn_pack(2, ffn2_w_gate, ffn2_w_up, ffn2_w_down)
    W = f32(w_in)[0]
    cols = []
    for base in (0, 1024, 1536, 2048):
        for h in range(4):
            cols.append(base + h * 128 + p)
    dd = np.arange(64)
    for swap in (0, 1):
        for c in range(4):
            d_ = dd ^ 1 if swap else dd
            cols.append(np.concatenate([2560 + c * 64 + d_, 2560 + (4 + c) * 64 + d_]))
    for swap in (0, 1):
        d_ = dd ^ 1 if swap else dd
        cols.append(np.concatenate([3072 + d_, 3072 + 64 + d_]))
    for base in (3328, 4352):
        for o in range(8):
            cols.append(base + o * 128 + p)
    assert len(cols) == 42
    shared["wfm"] = np.stack([_fm_layout(W[:, c]) for c in cols])
    shared["wzi"] = np.stack([_fm_layout(W[:, 512 + h * 128:512 + (h + 1) * 128]) for h in range(4)])
    shared["wv"] = _fm_layout(W[:, 3200:3328])

    def up_pack(wu_, rowperm):
        wu_ = f32(wu_)[0][rowperm]
        out = np.zeros((4, 128, 1024), np.float32)
        for s in range(4):
            for oo in range(2):
                o = 2 * s + oo
                blk = wu_[:, o * 128:(o + 1) * 128].reshape(4, 128, 128).transpose(1, 0, 2).reshape(128, 512)
                out[s, :, oo * 512:(oo + 1) * 512] = blk
        return out

    shared["wua"] = up_pack(w_up_a, np.arange(512))
    permb = np.concatenate([np.concatenate([c * 64 + dd, (4 + c) * 64 + dd]) for c in range(4)])
    shared["wub"] = up_pack(w_up_b, permb)
    Wo = f32(w_out)[0]
    shared["wo"] = np.stack([_fm_layout(Wo[:, o * 128:(o + 1) * 128]) for o in range(DC)])
    vecs = np.zeros((128, NV), np.float32)
    vecs[:, 0:8] = f32(ffn1_norm)[0].reshape(8, 128).T
    vecs[:, 8:16] = f32(mix_norm)[0].reshape(8, 128).T
    vecs[:, 16:24] = f32(ffn2_norm)[0].reshape(8, 128).T
    vecs[:, 24:28] = f32(hg_out_norm)[0].reshape(4, 128).T
    vecs[:, 28:32] = f32(hg_lb_fwd)[0].reshape(4, 128).T
    vecs[:, 32:36] = f32(hg_lb_fwd)[1].reshape(4, 128).T
    vecs[:, 36:40] = f32(hg_lb_bwd)[0].reshape(4, 128).T
    vecs[:, 40:44] = f32(hg_lb_bwd)[1].reshape(4, 128).T
    qn, kn = f32(q_norm)[0], f32(k_norm)[0]
    vecs[:, 44] = qn[p % 64]
    vecs[:, 45] = kn[p % 64]
    vecs[:, 46] = qn[(p % 64) ^ 1]
    vecs[:, 47] = kn[(p % 64) ^ 1]
    pi = (p % 64) // 2
    vecs[:, 48] = (pi % 16).astype(np.float32)
    vecs[:, 49] = (pi < 16).astype(np.float32)
    vecs[:, 50] = (pi >= 16).astype(np.float32)
    vecs[:, 51] = np.where(p % 2 == 0, -1.0, 1.0)
    shared["vecs"] = vecs
    cm = np.zeros((128, 384), np.float32)
    cm[:, 0:128] = np.eye(128, dtype=np.float32)
    cm[:, 128:256] = np.triu(np.ones((128, 128), np.float32))
    cm[:, 256:384] = np.tril(np.ones((128, 128), np.float32))
    shared["cmat"] = cm
    shared["metaT"] = np.ascontiguousarray(f32(meta_tokens).T)
    xs = f32(x)[0]
    in_maps = []
    for r in range(NCORE):
        m = dict(shared)
        m["xT"] = np.ascontiguousarray(xs[r * NT:(r + 1) * NT].T)
        cc = np.zeros((128, 17), np.float32)
        cc[:, 0] = r * (NT // 64)
        for j in range(NCORE):
            cc[:, 1 + j] = 1.0 if j < r else 0.0
            cc[:, 9 + j] = 1.0 if j > r else 0.0
        m["corec"] = cc
        in_maps.append(m)
    return in_maps


_NC_CACHE = {}


def kernel(**inputs):
    in_maps = _host_inputs(**inputs)
    if "nc" not in _NC_CACHE:
        _NC_CACHE["nc"] = build_program()
    res = run_bass_kernel_spmd(_NC_CACHE["nc"], in_maps, core_ids=list(range(NCORE)))
    outs = [np.asarray(res.results[r]["outT"]).T for r in range(NCORE)]
    return np.ascontiguousarray(np.concatenate(outs, axis=0)[None].astype(np.float32))
```

```python
import numpy as np
from contextlib import ExitStack
import concourse.bass as bass
import concourse.mybir as mybir
from concourse.bass_utils import run_bass_kernel_spmd

F32 = mybir.dt.float32
BF16 = mybir.dt.bfloat16
I32 = mybir.dt.int32
AF = mybir.ActivationFunctionType
ALU = mybir.AluOpType

NCORE = 8
D = 1024
DC = 8
NT = 2048
NM = 16
NA = NT + NM
DFF = 2816
JC = 22
EPS = 1e-6
CG = [(0, 512), (512, 512), (1024, 512), (1536, 512), (2048, 16)]
NV = 52
WIDE_EXP = False
TWO_PI = 6.28318
HGW = 1032


class Buf:
    __slots__ = ("name", "w", "r")

    def __init__(self, name=""):
        self.name = name
        self.w = None
        self.r = {}


class Sched:
    ENG = ("pe", "act", "dve", "pool", "sp")

    def __init__(self, nc, ctx):
        self.nc = nc
        self.ctx = ctx
        self.h = {"pe": nc.tensor, "act": nc.scalar, "dve": nc.vector, "pool": nc.gpsimd, "sp": nc.sync}
        self.sems = {}
        self.count = {}
        self.waited = {e: {} for e in self.ENG}
        for e in self.ENG:
            self.sems[e] = ctx.enter_context(nc.semaphore("sem_" + e))
            self.count[e] = 0
        self.nchan = 0
        self.ninstr = 0

    def chan(self):
        key = "ch%d" % self.nchan
        self.nchan += 1
        self.sems[key] = self.ctx.enter_context(self.nc.semaphore("sem_" + key))
        self.count[key] = 0
        return key

    def _waits(self, eng, reads, writes, self_sync):
        need = {}
        for b in reads:
            if b.w is not None:
                k, v = b.w
                if need.get(k, 0) < v:
                    need[k] = v
        for b in writes:
            if b.w is not None:
                k, v = b.w
                if need.get(k, 0) < v:
                    need[k] = v
            for k, v in b.r.items():
                if need.get(k, 0) < v:
                    need[k] = v
        wd = self.waited[eng]
        hnd = self.h[eng]
        for k, v in need.items():
            if (not self_sync) and k == eng:
                continue
            if wd.get(k, 0) < v:
                wd[k] = v
                hnd.wait_ge(self.sems[k], v)

    def _mark(self, key, val, reads, writes):
        for b in reads:
            if b.r.get(key, 0) < val:
                b.r[key] = val
        for b in writes:
            b.w = (key, val)
            b.r = {}

    def op(self, eng, fn, reads=(), writes=(), self_sync=True):
        self._waits(eng, reads, writes, self_sync)
        self.count[eng] += 1
        fn(self.h[eng]).then_inc(self.sems[eng], 1)
        self._mark(eng, self.count[eng], reads, writes)
        self.ninstr += 1

    def dma(self, q, ch, fn, reads=(), writes=(), inc=16):
        self._waits(q, reads, writes, True)
        self.count[ch] += inc
        fn(self.h[q]).then_inc(self.sems[ch], inc)
        self._mark(ch, self.count[ch], reads, writes)
        self.ninstr += 1

    def wait_bufs(self, eng, bufs):
        self._waits(eng, bufs, (), True)

    def barrier(self):
        for e in self.ENG:
            wd = self.waited[e]
            for k, v in self.count.items():
                if k == e or v == 0:
                    continue
                if wd.get(k, 0) < v:
                    wd[k] = v
                    self.h[e].wait_ge(self.sems[k], v)


def build_program():
    nc = bass.Bass("TRN2", target_bir_lowering=False)
    dt = lambda name, shape, dtype=F32, kind="ExternalInput": nc.dram_tensor(name, shape, dtype, kind=kind)
    xT_d = dt("xT", [D, NT]).ap()
    metaT_d = dt("metaT", [D, NM]).ap()
    vecs_d = dt("vecs", [128, NV]).ap()
    corec_d = dt("corec", [128, 17]).ap()
    cmat_d = dt("cmat", [128, 3 * 128]).ap()
    wg_d = [dt("wg%d" % i, [JC, 128, 1024]).ap() for i in (1, 2)]
    wu_d = [dt("wu%d" % i, [JC, 128, 1024]).ap() for i in (1, 2)]
    wd_d = [dt("wd%d" % i, [DC, 128, DFF]).ap() for i in (1, 2)]
    wfm_d = dt("wfm", [42, 128, 1024]).ap()
    wzi_d = dt("wzi", [4, 128, 1024]).ap()
    wv_d = dt("wv", [128, 1024]).ap()
    wua_d = dt("wua", [4, 128, 1024]).ap()
    wub_d = dt("wub", [4, 128, 1024]).ap()
    wo_d = dt("wo", [DC, 128, 1024]).ap()
    outT_d = dt("outT", [D, NT], F32, "ExternalOutput").ap()
    kt_loc_d = nc.dram_tensor("kt_loc", [128, NT], BF16)
    kt_all_d = nc.dram_tensor("kt_all", [NCORE * 128, NT], BF16)
    v_loc_d = nc.dram_tensor("v_loc", [128, 16 * 130], BF16)
    v_all_d = nc.dram_tensor("v_all", [NCORE * 128, 16 * 130], BF16)
    hg_loc_d = nc.dram_tensor("hg_loc", [128, HGW], F32)
    hg_all_d = nc.dram_tensor("hg_all", [NCORE * 128, HGW], F32)

    with ExitStack() as ctx:
        S = Sched(nc, ctx)

        uniq = [0]

        def sbt(c, name, shape, dtype=F32, side="left"):
            uniq[0] += 1
            return c.enter_context(nc.sbuf_tensor("s%d_%s" % (uniq[0], name), shape, dtype, side=side))

        def sbr(c, name, shape, dtype=F32):
            return sbt(c, name, shape, dtype, side="right")

        PSbig = ctx.enter_context(nc.psum_tensor("psbig", [128, 2048], F32))
        PS = [PSbig[:, i * 512:(i + 1) * 512] for i in range(4)] + \
             [ctx.enter_context(nc.psum_tensor("ps%d" % i, [128, 512], F32)) for i in range(4, 7)]
        pb = [Buf("ps%d" % i) for i in range(7)]
        PT = ctx.enter_context(nc.psum_tensor("pt", [128, 1024], BF16))
        ptb = Buf("pt")

        hT = sbr(ctx, "hT", [128, DC, NA])
        hb = [[Buf("h") for _ in CG] for _ in range(DC)]
        vecs = sbr(ctx, "vecs", [128, NV]); vecb = Buf("vecs")
        corec = sbr(ctx, "corec", [128, 17]); coreb = Buf("corec")
        der = sbr(ctx, "der", [128, 32]); derb = Buf("der")
        cmat = sbr(ctx, "cmat", [128, 384], BF16); cmatb = Buf("cmat")
        mask4 = sbr(ctx, "mask4", [128, 2, 512], BF16); maskb = Buf("mask4")
        ones_bf = sbr(ctx, "ones_bf", [128, 128], BF16); onesb = Buf("ones")
        ones_f = sbr(ctx, "ones_f", [128, 128]); onesfb = Buf("onesf")
        blk_f = sbr(ctx, "blk_f", [128, 128]); blkb = Buf("blk")
        rmask = sbr(ctx, "rmask", [128, 512]); rmaskb = Buf("rmask")
        KT_meta = sbr(ctx, "KT_meta", [128, NM], BF16); ktmb = Buf("ktm")
        V_meta = sbr(ctx, "V_meta", [128, 130], BF16); vmb = Buf("vm")

        def vcol(i, n=1):
            return vecs[:, i:i + n]
        D_LBF, D_OMLF, D_NOMLF, D_LBB, D_OMLB, D_NOMLB = 0, 4, 8, 12, 16, 20
        D_INVR, D_SINSC = 24, 25

        act = lambda fn, r=(), w=(): S.op("act", fn, r, w)
        dve = lambda fn, r=(), w=(): S.op("dve", fn, r, w)
        pool = lambda fn, r=(), w=(): S.op("pool", fn, r, w)

        def mm(out, lhsT, rhs, start, stop, r, w):
            S.op("pe", lambda h: h.matmul(out, lhsT, rhs, start=start, stop=stop), r, w, self_sync=False)

        ch_in = [S.chan() for _ in range(8)]
        S.dma("sp", ch_in[0], lambda h: h.dma_start(out=vecs[:], in_=vecs_d), writes=[vecb])
        S.dma("sp", ch_in[1], lambda h: h.dma_start(out=corec[:], in_=corec_d), writes=[coreb])
        S.dma("pool", ch_in[2], lambda h: h.dma_start(out=cmat[:], in_=cmat_d), writes=[cmatb])
        xT_v = xT_d.rearrange("(c p) t -> p c t", p=128)
        for g in range(4):
            c0, n = CG[g]
            S.dma("sp", ch_in[3 + g], lambda h, c0=c0, n=n: h.dma_start(out=hT[:, :, c0:c0 + n], in_=xT_v[:, :, c0:c0 + n]),
                  writes=[hb[c][g] for c in range(DC)])
        S.dma("sp", ch_in[7], lambda h: h.dma_start(out=hT[:, :, NT:NA], in_=metaT_d.rearrange("(c p) t -> p c t", p=128)),
              writes=[hb[c][4] for c in range(DC)])
        dve(lambda h: h.memset(ones_bf[:], 1.0), w=[onesb])
        dve(lambda h: h.memset(ones_f[:], 1.0), w=[onesfb])
        dve(lambda h: h.memset(rmask[:], 1.0), w=[rmaskb])
        dve(lambda h: h.memset(rmask[:, 0:512:128], 0.0), w=[rmaskb])
        dve(lambda h: h.memset(blk_f[:], 0.0), w=[blkb])
        dve(lambda h: h.memset(blk_f[0:64, 0:64], 1.0), w=[blkb])
        dve(lambda h: h.memset(blk_f[64:128, 64:128], 1.0), w=[blkb])
        for k in range(4):
            dve(lambda h, k=k: h.tensor_copy(out=mask4[:, 0, k * 128:(k + 1) * 128], in_=cmat[:, 128:256]), r=[cmatb], w=[maskb])
            dve(lambda h, k=k: h.tensor_copy(out=mask4[:, 1, k * 128:(k + 1) * 128], in_=cmat[:, 256:384]), r=[cmatb], w=[maskb])
        for (src0, dl, do, dn) in ((28, D_LBF, D_OMLF, D_NOMLF), (36, D_LBB, D_OMLB, D_NOMLB)):
            dve(lambda h, s=src0, dl=dl: h.tensor_tensor(out=der[:, dl:dl + 4], in0=vecs[:, s:s + 4], in1=vecs[:, s + 4:s + 8], op=ALU.subtract),
                r=[vecb], w=[derb])
            act(lambda h, dl=dl, do=do: h.activation(out=der[:, do:do + 4], in_=der[:, dl:dl + 4], func=AF.Sigmoid, scale=-1.0), r=[derb], w=[derb])
            act(lambda h, dl=dl: h.activation(out=der[:, dl:dl + 4], in_=der[:, dl:dl + 4], func=AF.Sigmoid), r=[derb], w=[derb])
            dve(lambda h, do=do, dn=dn: h.tensor_scalar(out=der[:, dn:dn + 4], in0=der[:, do:do + 4], scalar1=-1.0, scalar2=None, op0=ALU.mult),
                r=[derb], w=[derb])
        act(lambda h: h.activation(out=der[:, D_INVR:D_INVR + 1], in_=vecs[:, 48:49], func=AF.Exp, scale=-float(np.log(10000.0) / 16.0)),
            r=[vecb], w=[derb])
        dve(lambda h: h.tensor_scalar(out=der[:, D_INVR:D_INVR + 1], in0=der[:, D_INVR:D_INVR + 1], scalar1=float(1.0 / (2 * np.pi)), scalar2=None, op0=ALU.mult),
            r=[derb], w=[derb])
        dve(lambda h: h.tensor_scalar(out=der[:, D_SINSC:D_SINSC + 1], in0=vecs[:, 51:52], scalar1=-TWO_PI, scalar2=None, op0=ALU.mult),
            r=[vecb], w=[derb])

        class Slots:
            def __init__(self, c, n, width):
                self.t = [sbt(c, "slot%d_%d" % (id(self) % 1000, i), [128, width], BF16) for i in range(n)]
                self.b = [Buf("slot") for _ in range(n)]
                self.ch = [S.chan() for _ in range(n)]
                self.n = n
                self.i = 0

            def load(self, src, width):
                k = self.i % self.n
                self.i += 1
                t, b = self.t[k], self.b[k]
                S.dma("pool", self.ch[k], lambda h: h.dma_start(out=t[:, 0:width], in_=src), writes=[b])
                return t, b

        def rmsnorm_all(c, wcol0, uT, ub, groups, coff=0):
            with ExitStack() as lc:
                sq = [sbt(lc, "nsq%d" % i, [128, DC, 512], BF16) for i in range(2)]
                sqb = [Buf("sq") for _ in range(2)]
                t1 = sbt(lc, "nt1", [128, 512]); t1b = Buf("t1")
                rs = sbt(lc, "nrs", [128, 512]); rsb = Buf("rs")
                for gi, g in enumerate(groups):
                    c0, n = CG[g]
                    q = gi % 2
                    for cc in range(DC):
                        act(lambda h, cc=cc: h.activation(out=sq[q][:, cc, 0:n], in_=hT[:, cc, c0:c0 + n], func=AF.Square),
                            r=[hb[cc][g]], w=[sqb[q]])
                    for cc in range(DC):
                        mm(PS[6][:, 0:n], ones_bf[:, :], sq[q][:, cc, 0:n], cc == 0, cc == DC - 1, [sqb[q], onesb], [pb[6]])
                    act(lambda h: h.activation(out=t1[:, 0:n], in_=PS[6][:, 0:n], func=AF.Ln, scale=1.0 / D, bias=EPS), r=[pb[6]], w=[t1b])
                    act(lambda h: h.activation(out=rs[:, 0:n], in_=t1[:, 0:n], func=AF.Exp, scale=-0.5), r=[t1b], w=[rsb])
                    for cc in range(DC):
                        dve(lambda h, cc=cc: h.scalar_tensor_tensor(out=uT[:, cc, c0 - coff:c0 - coff + n], in0=hT[:, cc, c0:c0 + n],
                                                                    scalar=vecs[:, wcol0 + cc:wcol0 + cc + 1], in1=rs[:, 0:n],
                                                                    op0=ALU.mult, op1=ALU.mult),
                            r=[hb[cc][g], rsb, vecb], w=[ub[g]])
                S.barrier()

        def ffn(c, slots, wg, wu, wd, uT, ub, groups):
            with ExitStack() as lc:
                aT = sbt(lc, "aT", [128, 6, NA], BF16)
                ab = [[Buf("a") for _ in CG] for _ in range(6)]
                sg = [sbt(lc, "fsg%d" % i, [128, 512]) for i in range(2)]
                sgb = [Buf("sg") for _ in range(2)]
                rot = 0
                for (j0, nj) in ((0, 6), (6, 6), (12, 5), (17, 5)):
                    for jj in range(nj):
                        j = j0 + jj
                        tg, bg = slots.load(wg[j], 1024)
                        tu, bu = slots.load(wu[j], 1024)
                        for g in groups:
                            c0, n = CG[g]
                            r = rot % 2
                            rot += 1
                            Gp, Up = PS[r], PS[2 + r]
                            for kc in range(DC):
                                mm(Gp[:, 0:n], tg[:, kc * 128:(kc + 1) * 128], uT[:, kc, c0:c0 + n], kc == 0, kc == DC - 1, [bg, ub[g]], [pb[r]])
                            for kc in range(DC):
                                mm(Up[:, 0:n], tu[:, kc * 128:(kc + 1) * 128], uT[:, kc, c0:c0 + n], kc == 0, kc == DC - 1, [bu, ub[g]], [pb[2 + r]])
                            act(lambda h, r=r, Gp=Gp: h.activation(out=sg[r][:, 0:n], in_=Gp[:, 0:n], func=AF.Silu), r=[pb[r]], w=[sgb[r]])
                            dve(lambda h, r=r, Up=Up, jj=jj: h.tensor_tensor(out=aT[:, jj, c0:c0 + n], in0=sg[r][:, 0:n], in1=Up[:, 0:n], op=ALU.mult),
                                r=[sgb[r], pb[2 + r]], w=[ab[jj][g]])
                    for o in range(DC):
                        td, bd = slots.load(wd[o][:, j0 * 128:(j0 + nj) * 128], nj * 128)
                        for g in groups:
                            c0, n = CG[g]
                            r = rot % 2
                            rot += 1
                            Yp = PS[4 + r]
                            for jj in range(nj):
                                mm(Yp[:, 0:n], td[:, jj * 128:(jj + 1) * 128], aT[:, jj, c0:c0 + n], jj == 0, jj == nj - 1, [bd, ab[jj][g]], [pb[4 + r]])
                            dve(lambda h, Yp=Yp, o=o: h.scalar_tensor_tensor(out=hT[:, o, c0:c0 + n], in0=Yp[:, 0:n], scalar=0.5,
                                                                             in1=hT[:, o, c0:c0 + n], op0=ALU.mult, op1=ALU.add),
                                r=[pb[4 + r], hb[o][g]], w=[hb[o][g]])
                S.barrier()

        def rope_tables(lc_t, tdiv, tmod, tdb, g, want):
            (Ct, Cb, St, Sb, tmp, tmpb, tmi, tmib, rowoff, rowoffb) = lc_t
            c0, n = CG[g]
            dve(lambda h: h.tensor_scalar(out=rowoff[:, 0:1], in0=corec[:, 0:1], scalar1=float(c0 // 64), scalar2=vecs[:, 49:50],
                                          op0=ALU.add, op1=ALU.mult), r=[coreb, vecb], w=[rowoffb])
            dve(lambda h: h.tensor_scalar(out=tmp[0][:, :], in0=tdiv[:, :], scalar1=vecs[:, 49:50], scalar2=rowoff[:, 0:1],
                                          op0=ALU.mult, op1=ALU.add), r=[tdb, vecb, rowoffb], w=[tmpb[0]])
            dve(lambda h: h.scalar_tensor_tensor(out=tmp[1][:, :], in0=tmod[:, :], scalar=vecs[:, 50:51], in1=tmp[0][:, :],
                                                 op0=ALU.mult, op1=ALU.add), r=[tdb, vecb, tmpb[0]], w=[tmpb[1]])
            for (shift, T, Tb, sc) in ((0.0, St, Sb, der[:, D_SINSC:D_SINSC + 1]), (0.25, Ct, Cb, -TWO_PI)):
                dve(lambda h, shift=shift: h.tensor_scalar(out=tmp[0][:, :], in0=tmp[1][:, :], scalar1=der[:, D_INVR:D_INVR + 1], scalar2=shift,
                                                           op0=ALU.mult, op1=ALU.add), r=[tmpb[1], derb], w=[tmpb[0]])
                dve(lambda h: h.tensor_copy(out=tmi[:, :], in_=tmp[0][:, :]), r=[tmpb[0]], w=[tmib])
                dve(lambda h: h.tensor_copy(out=tmp[2][:, :], in_=tmi[:, :]), r=[tmib], w=[tmpb[2]])
                dve(lambda h: h.tensor_tensor(out=tmp[0][:, :], in0=tmp[0][:, :], in1=tmp[2][:, :], op=ALU.subtract), r=[tmpb[0], tmpb[2]], w=[tmpb[0]])
                dve(lambda h: h.scalar_tensor_tensor(out=tmp[2][:, :], in0=tmp[0][:, :], scalar=0.5, in1=tmp[0][:, :],
                                                     op0=ALU.is_gt, op1=ALU.subtract), r=[tmpb[0]], w=[tmpb[2]])
                act(lambda h, T=T, sc=sc: h.activation(out=T[:, :], in_=tmp[2][:, :], func=AF.Sin, scale=sc), r=[tmpb[2], derb], w=[Tb])

        def make_rope_ctx(lc):
            Ct = sbt(lc, "ropeC", [128, 512]); St = sbt(lc, "ropeS", [128, 512])
            tmp = [sbt(lc, "ropet%d" % i, [128, 512]) for i in range(3)]
            tmi = sbt(lc, "ropei", [128, 512], I32)
            rowoff = sbt(lc, "rowoff", [128, 1])
            tdiv = sbt(lc, "tdiv", [128, 512]); tmod = sbt(lc, "tmod", [128, 512]); tdb = Buf("td")
            pool(lambda h: h.iota(tdiv[:, :], [[1, 8], [0, 64]], base=0, channel_multiplier=0, allow_small_or_imprecise_dtypes=True), w=[tdb])
            pool(lambda h: h.iota(tmod[:, :], [[0, 8], [1, 64]], base=0, channel_multiplier=0, allow_small_or_imprecise_dtypes=True), w=[tdb])
            return (Ct, Buf("C"), St, Buf("S"), tmp, [Buf("t") for _ in range(3)], tmi, Buf("ti"), rowoff, Buf("ro")), tdiv, tmod, tdb

        def qk_post(lcw, Zp, zpb, Zsp, zspb, n, gcol, gscol, rope, out_ap, outb):
            (sqf, sqfb, t1, t1b, rs, rsb, qn, qnb, qs, qsb) = lcw
            act(lambda h: h.activation(out=sqf[:, 0:n], in_=Zp[:, 0:n], func=AF.Square), r=[zpb], w=[sqfb])
            mm(PS[6][:, 0:n], blk_f[:, :], sqf[:, 0:n], True, True, [blkb, sqfb], [pb[6]])
            act(lambda h: h.activation(out=t1[:, 0:n], in_=PS[6][:, 0:n], func=AF.Ln, scale=1.0 / 64, bias=EPS), r=[pb[6]], w=[t1b])
            act(lambda h: h.activation(out=rs[:, 0:n], in_=t1[:, 0:n], func=AF.Exp, scale=-0.5), r=[t1b], w=[rsb])
            if rope is None:
                dve(lambda h: h.scalar_tensor_tensor(out=out_ap, in0=Zp[:, 0:n], scalar=vecs[:, gcol:gcol + 1], in1=rs[:, 0:n],
                                                     op0=ALU.mult, op1=ALU.mult), r=[zpb, rsb, vecb], w=outb)
                return
            Ct, Cb, St, Sb = rope
            dve(lambda h: h.scalar_tensor_tensor(out=qn[:, 0:n], in0=Zp[:, 0:n], scalar=vecs[:, gcol:gcol + 1], in1=rs[:, 0:n],
                                                 op0=ALU.mult, op1=ALU.mult), r=[zpb, rsb, vecb], w=[qnb])
            dve(lambda h: h.scalar_tensor_tensor(out=qs[:, 0:n], in0=Zsp[:, 0:n], scalar=vecs[:, gscol:gscol + 1], in1=rs[:, 0:n],
                                                 op0=ALU.mult, op1=ALU.mult), r=[zspb, rsb, vecb], w=[qsb])
            dve(lambda h: h.tensor_tensor(out=qn[:, 0:n], in0=qn[:, 0:n], in1=Ct[:, 0:n], op=ALU.mult), r=[qnb, Cb], w=[qnb])
            dve(lambda h: h.tensor_tensor(out=qs[:, 0:n], in0=qs[:, 0:n], in1=St[:, 0:n], op=ALU.mult), r=[qsb, Sb], w=[qsb])
            dve(lambda h: h.tensor_tensor(out=out_ap, in0=qn[:, 0:n], in1=qs[:, 0:n], op=ALU.add), r=[qnb, qsb], w=outb)

        def make_qk_ctx(lc):
            names = ["sqf", "t1", "rs", "qn", "qs"]
            out = []
            for nm in names:
                out.append(sbt(lc, "qk_" + nm, [128, 512]))
                out.append(Buf(nm))
            return tuple(out)

        def hg_gates(lcw, Zf, zfb, n, L, d, h, full, qsil, qsilb, khT_ap, khb, dcol_ap, dcolb, blc_ap,
                     qt_ap=None, kt_ap=None, qkb=None, ecol_ap=None):
            (sgt, sgb_, lf, lfb, kk, kkb, cum, cumb, bb, bbb, X, Xb, E, Eb) = lcw
            nch = n // L
            v3 = lambda t: t[:, 0:n].rearrange("p (c l) -> p c l", l=L)
            col = lambda t, off: t[:, off:n:L]
            bc = lambda t, off: col(t, off).unsqueeze(2).to_broadcast([128, nch, L])
            lb_c, oml_c, noml_c = (D_LBF, D_OMLF, D_NOMLF) if d == 0 else (D_LBB, D_OMLB, D_NOMLB)
            act(lambda hh: hh.activation(out=sgt[:, 0:n], in_=Zf[:, 0:n], func=AF.Sigmoid), r=[zfb], w=[sgb_])
            act(lambda hh: hh.activation(out=lf[:, 0:n], in_=sgt[:, 0:n], func=AF.Ln, scale=der[:, oml_c + h:oml_c + h + 1],
                                         bias=der[:, lb_c + h:lb_c + h + 1]), r=[sgb_, derb], w=[lfb])
            dve(lambda hh: hh.tensor_scalar(out=kk[:, 0:n], in0=sgt[:, 0:n], scalar1=der[:, noml_c + h:noml_c + h + 1],
                                            scalar2=der[:, oml_c + h:oml_c + h + 1], op0=ALU.mult, op1=ALU.add), r=[sgb_, derb], w=[kkb])
            dve(lambda hh: hh.tensor_tensor_scan(out=cum[:, 0:n], data0=rmask[:, 0:n], data1=lf[:, 0:n], initial=0.0, op0=ALU.mult, op1=ALU.add),
                r=[rmaskb, lfb], w=[cumb])
            if d == 0:
                b, bbuf = cum, cumb
                last_off, mid_off = L - 1, L // 2 - 1
            else:
                dve(lambda hh: hh.tensor_tensor(out=bb[:, 0:n], in0=lf[:, 0:n], in1=cum[:, 0:n], op=ALU.subtract), r=[lfb, cumb], w=[bbb])
                dve(lambda hh: hh.tensor_tensor(out=v3(bb), in0=v3(bb), in1=bc(cum, L - 1), op=ALU.add), r=[bbb, cumb], w=[bbb])
                b, bbuf = bb, bbb
                last_off, mid_off = 0, L // 2
            dve(lambda hh: hh.tensor_tensor(out=v3(X), in0=v3(b), in1=bc(b, last_off), op=ALU.subtract), r=[bbuf], w=[Xb])
            act(lambda hh: hh.activation(out=E[:, 0:n], in_=X[:, 0:n], func=AF.Exp, scale=-1.0), r=[Xb], w=[Eb])
            dve(lambda hh: hh.tensor_tensor(out=khT_ap, in0=kk[:, 0:n], in1=E[:, 0:n], op=ALU.mult), r=[kkb, Eb], w=[khb])
            act(lambda hh: hh.activation(out=dcol_ap[:, 0:nch], in_=col(b, last_off), func=AF.Exp), r=[bbuf], w=[dcolb])
            dve(lambda hh: hh.tensor_copy(out=blc_ap[:, 0:nch], in_=col(b, last_off)), r=[bbuf], w=[dcolb])
            if not full:
                return
            dve(lambda hh: hh.tensor_tensor(out=v3(X), in0=v3(b), in1=bc(b, mid_off), op=ALU.subtract), r=[bbuf, Xb], w=[Xb])
            act(lambda hh: hh.activation(out=E[:, 0:n], in_=X[:, 0:n], func=AF.Exp, scale=1.0), r=[Xb, khb], w=[Eb])
            dve(lambda hh: hh.tensor_tensor(out=qt_ap, in0=qsil[:, 0:n], in1=E[:, 0:n], op=ALU.mult), r=[qsilb, Eb], w=[qkb])
            act(lambda hh: hh.activation(out=E[:, 0:n], in_=X[:, 0:n], func=AF.Exp, scale=-1.0), r=[Xb, qkb], w=[Eb])
            dve(lambda hh: hh.tensor_tensor(out=kt_ap, in0=kk[:, 0:n], in1=E[:, 0:n], op=ALU.mult), r=[kkb, Eb], w=[qkb])
            act(lambda hh: hh.activation(out=ecol_ap[:, 0:nch], in_=col(b, mid_off), func=AF.Exp), r=[bbuf], w=[dcolb])

        def make_gate_ctx(lc):
            out = []
            for nm in ["sg", "lf", "kk", "cum", "bb", "X", "E"]:
                out.append(sbt(lc, "hg_" + nm, [128, 512]))
                out.append(Buf(nm))
            return tuple(out)

        cY = ExitStack()
        with ExitStack() as cA:
            slots = Slots(cA, 8, 1024)
            uT = sbt(cA, "uT", [128, DC, NA], BF16)
            ub = [Buf("u") for _ in CG]
            sbin0 = sbt(cA, "sbin0", [128, 4, 16, 128], BF16); sbin0b = [Buf("sbin0") for _ in range(4)]
            pbcol = sbt(cA, "pbcol", [128, 4, 16]); pbcolb = [Buf("pbcol") for _ in range(4)]
            s0f = sbt(cA, "s0f", [128, 4, 128]); s0fb = Buf("s0f")
            s0b = sbt(cA, "s0b", [128, 4, 128]); s0bb = Buf("s0b")
            cX = ExitStack()
            smeta = sbt(cX, "smeta", [128, 4, 128]); smetab = Buf("smeta")
            hgpay = sbt(cX, "hgpay", [128, HGW]); hgpayb = Buf("hgpay")
            ALLG = [0, 1, 2, 3, 4]
            rmsnorm_all(cA, 0, uT, ub, ALLG)
            ffn(cA, slots, wg_d[0], wu_d[0], wd_d[0], uT, ub, ALLG)
            rmsnorm_all(cA, 8, uT, ub, ALLG)

            with ExitStack() as c4:
                KT_loc = sbt(c4, "KT_loc", [128, NA], BF16); ktb = Buf("ktloc")
                V_loc = sbt(c4, "V_loc", [128, 17, 130], BF16); vlb = Buf("vloc")
                ropec, tdiv, tmod, tdb = make_rope_ctx(c4)
                qkc = make_qk_ctx(c4)
                dve(lambda h: h.memset(V_loc[:, :, 64:65], 1.0), w=[vlb])
                dve(lambda h: h.memset(V_loc[:, :, 129:130], 1.0), w=[vlb])
                tk, bk = slots.load(wfm_d[24], 1024)
                tks, bks = slots.load(wfm_d[25], 1024)
                for g in ALLG:
                    c0, n = CG[g]
                    r = g % 2
                    for kc in range(DC):
                        mm(PS[r][:, 0:n], tk[:, kc * 128:(kc + 1) * 128], uT[:, kc, c0:c0 + n], kc == 0, kc == DC - 1, [bk, ub[g]], [pb[r]])
                    for kc in range(DC):
                        mm(PS[2 + r][:, 0:n], tks[:, kc * 128:(kc + 1) * 128], uT[:, kc, c0:c0 + n], kc == 0, kc == DC - 1, [bks, ub[g]], [pb[2 + r]])
                    if g < 4:
                        rope_tables(ropec, tdiv, tmod, tdb, g, None)
                        rp = (ropec[0], ropec[1], ropec[2], ropec[3])
                    else:
                        rp = None
                    qk_post(qkc, PS[r], pb[r], PS[2 + r], pb[2 + r], n, 45, 47, rp, KT_loc[:, c0:c0 + n], [ktb])
                tv, bv = slots.load(wv_d, 1024)
                for blk in range(17):
                    nb_ = 128 if blk < 16 else NM
                    t0 = blk * 128
                    r = blk % 2
                    for kc in range(DC):
                        mm(PS[4 + r][0:nb_, 0:128], uT[:, kc, t0:t0 + nb_], tv[:, kc * 128:(kc + 1) * 128], kc == 0, kc == DC - 1,
                           [bv, ub[min(blk // 4, 4)]], [pb[4 + r]])
                    act(lambda h, r=r, blk=blk, nb_=nb_: h.copy(out=V_loc[0:nb_, blk, 0:64], in_=PS[4 + r][0:nb_, 0:64]), r=[pb[4 + r]], w=[vlb])
                    dve(lambda h, r=r, blk=blk, nb_=nb_: h.tensor_copy(out=V_loc[0:nb_, blk, 65:129], in_=PS[4 + r][0:nb_, 64:128]), r=[pb[4 + r]], w=[vlb])
                ch_k, ch_v = S.chan(), S.chan()
                ktlb, ktab, vlob, vab = Buf("ktl_d"), Buf("kta_d"), Buf("vl_d"), Buf("va_d")
                S.dma("sp", ch_k, lambda h: h.dma_start(out=kt_loc_d.ap(), in_=KT_loc[:, 0:NT]), reads=[ktb], writes=[ktlb])
                S.dma("sp", ch_v, lambda h: h.dma_start(out=v_loc_d.ap().rearrange("p (b n) -> p b n", n=130), in_=V_loc[:, 0:16, :]),
                      reads=[vlb], writes=[vlob])
                ch_cc = [S.chan() for _ in range(3)]
                S.wait_bufs("pool", slots.b)
                S.dma("pool", ch_cc[0], lambda h: h.collective_compute("AllGather", ALU.bypass, replica_groups=[list(range(NCORE))],
                                                                        ins=[kt_loc_d.ap().opt()], outs=[kt_all_d.ap().opt()]),
                      reads=[ktlb], writes=[ktab], inc=1)
                S.wait_bufs("pool", [ktab])
                S.dma("pool", ch_cc[1], lambda h: h.collective_compute("AllGather", ALU.bypass, replica_groups=[list(range(NCORE))],
                                                                        ins=[v_loc_d.ap().opt()], outs=[v_all_d.ap().opt()]),
                      reads=[vlob], writes=[vab], inc=1)
                S.wait_bufs("pool", [vab])
                dve(lambda h: h.tensor_copy(out=KT_meta[:, :], in_=KT_loc[:, NT:NA]), r=[ktb], w=[ktmb])
                dve(lambda h: h.tensor_copy(out=V_meta[0:NM, :], in_=V_loc[0:NM, 16, :]), r=[vlb], w=[vmb])
                S.barrier()

            with ExitStack() as c5:
                gatecs = [make_gate_ctx(c5), make_gate_ctx(c5)]
                gcall = [0]
                v_h = sbt(c5, "v_h", [128, 17, 128], BF16); vhb = Buf("vh")
                khT = sbt(c5, "khT", [128, NA], BF16); khb = Buf("khT")
                khat = sbt(c5, "khat", [128, 17, 128], BF16); khatb = Buf("khat")
                dcol = sbt(c5, "dcol", [128, 17]); dcolb = Buf("dcol")
                blc = sbt(c5, "blc", [128, 17])
                Sst = sbt(c5, "Sst", [128, 128]); Sstb = Buf("Sst")
                prun = sbt(c5, "prun", [128, 2]); prunb = Buf("prun")
                for h in range(4):
                    tzi, bzi = slots.load(wzi_d[h], 1024)
                    for blk in range(17):
                        nb_ = 128 if blk < 16 else NM
                        t0 = blk * 128
                        r = blk % 2
                        for kc in range(DC):
                            mm(PS[2 + r][0:nb_, 0:128], uT[:, kc, t0:t0 + nb_], tzi[:, kc * 128:(kc + 1) * 128], kc == 0, kc == DC - 1,
                               [bzi, ub[min(blk // 4, 4)]], [pb[2 + r]])
                        if blk % 2 == 0:
                            act(lambda hh, r=r, blk=blk, nb_=nb_: hh.copy(out=v_h[0:nb_, blk, :], in_=PS[2 + r][0:nb_, 0:128]), r=[pb[2 + r]], w=[vhb])
                        else:
                            dve(lambda hh, r=r, blk=blk, nb_=nb_: hh.tensor_copy(out=v_h[0:nb_, blk, :], in_=PS[2 + r][0:nb_, 0:128]), r=[pb[2 + r]], w=[vhb])
                    for d in range(2):
                        tf, bf_ = slots.load(wfm_d[4 + 4 * d + h], 1024)
                        groups = ALLG if d == 0 else [0, 1, 2, 3]
                        for g in groups:
                            c0, n = CG[g]
                            r = g % 2
                            for kc in range(DC):
                                mm(PS[r][:, 0:n], tf[:, kc * 128:(kc + 1) * 128], uT[:, kc, c0:c0 + n], kc == 0, kc == DC - 1, [bf_, ub[g]], [pb[r]])
                            L = 128 if g < 4 else NM
                            gcall[0] += 1
                            hg_gates(gatecs[gcall[0] % 2], PS[r], pb[r], n, L, d, h, False, None, None, khT[:, c0:c0 + n], khb,
                                     dcol[:, 4 * g:4 * g + n // L], dcolb, blc[:, 4 * g:4 * g + n // L])
                        nblk = 17 if d == 0 else 16
                        for b0 in range(0, nblk, 4):
                            bl = list(range(b0, min(b0 + 4, nblk)))
                            for i, blk in enumerate(bl):
                                nb_ = 128 if blk < 16 else NM
                                S.op("pe", lambda hh, i=i, blk=blk, nb_=nb_: hh.transpose(PT[0:nb_, i * 128:(i + 1) * 128], khT[:, blk * 128:blk * 128 + nb_], cmat[:, 0:128]),
                                     [khb, cmatb], [ptb], self_sync=False)
                            for i, blk in enumerate(bl):
                                nb_ = 128 if blk < 16 else NM
                                dve(lambda hh, i=i, blk=blk, nb_=nb_: hh.tensor_copy(out=khat[0:nb_, blk, :], in_=PT[0:nb_, i * 128:(i + 1) * 128]), r=[ptb], w=[khatb])
                        if d == 0:
                            mm(PS[4][:, 0:128], khat[0:NM, 16, :], v_h[0:NM, 16, :], True, True, [khatb, vhb], [pb[4]])
                            act(lambda hh, h=h: hh.copy(out=smeta[:, h, :], in_=PS[4][:, 0:128]), r=[pb[4]], w=[smetab])
                        order = list(range(16)) if d == 0 else list(range(15, -1, -1))
                        for idx, j in enumerate(order):
                            r = idx % 2
                            mm(PS[4 + r][:, 0:128], khat[:, j, :], v_h[:, j, :], True, True, [khatb, vhb], [pb[4 + r]])
                            if d == 1:
                                if idx == 0:
                                    dve(lambda hh, h=h, j=j: hh.memset(sbin0[:, h, j, :], 0.0), w=[sbin0b[h]])
                                    dve(lambda hh, h=h, j=j: hh.memset(pbcol[:, h, j:j + 1], 1.0), w=[pbcolb[h]])
                                    dve(lambda hh: hh.memset(prun[:, 0:1], 1.0), w=[prunb])
                                else:
                                    dve(lambda hh, h=h, j=j: hh.tensor_copy(out=sbin0[:, h, j, :], in_=Sst[:, :]), r=[Sstb], w=[sbin0b[h]])
                                    dve(lambda hh, h=h, j=j: hh.tensor_copy(out=pbcol[:, h, j:j + 1], in_=prun[:, 0:1]), r=[prunb], w=[pbcolb[h]])
                                dve(lambda hh, j=j: hh.tensor_tensor(out=prun[:, 0:1], in0=prun[:, 0:1], in1=dcol[:, j:j + 1], op=ALU.mult),
                                    r=[prunb, dcolb], w=[prunb])
                            if idx == 0:
                                act(lambda hh, r=r: hh.copy(out=Sst[:, :], in_=PS[4 + r][:, 0:128]), r=[pb[4 + r]], w=[Sstb])
                            else:
                                dve(lambda hh, r=r, j=j: hh.scalar_tensor_tensor(out=Sst[:, :], in0=Sst[:, :], scalar=dcol[:, j:j + 1], in1=PS[4 + r][:, 0:128],
                                                                                 op0=ALU.mult, op1=ALU.add), r=[Sstb, dcolb, pb[4 + r]], w=[Sstb])
                        off = (0 if d == 0 else 512) + h * 128
                        dve(lambda hh, off=off: hh.tensor_copy(out=hgpay[:, off:off + 128], in_=Sst[:, :]), r=[Sstb], w=[hgpayb])
                        dve(lambda hh: hh.tensor_reduce(out=prun[:, 1:2], in_=blc[:, 0:16], op=ALU.add, axis=mybir.AxisListType.X), r=[dcolb], w=[prunb])
                        ac = 1024 + 4 * d + h
                        act(lambda hh, ac=ac: hh.activation(out=hgpay[:, ac:ac + 1], in_=prun[:, 1:2], func=AF.Exp), r=[prunb], w=[hgpayb])
                S.barrier()
            ch_h = S.chan()
            hglb, hgab = Buf("hgl"), Buf("hga")
            S.dma("sp", ch_h, lambda h: h.dma_start(out=hg_loc_d.ap(), in_=hgpay[:, :]), reads=[hgpayb], writes=[hglb])
            S.wait_bufs("pool", slots.b)
            S.dma("pool", ch_cc[2], lambda h: h.collective_compute("AllGather", ALU.bypass, replica_groups=[list(range(NCORE))],
                                                                    ins=[hg_loc_d.ap().opt()], outs=[hg_all_d.ap().opt()]),
                  reads=[hglb], writes=[hgab], inc=1)
            S.wait_bufs("pool", [hgab])
            with ExitStack() as c6:
                stg = [sbt(c6, "hgstg%d" % i, [128, HGW]) for i in range(2)]
                stgb = [Buf("stg") for _ in range(2)]
                stch = [S.chan() for _ in range(2)]
                aeff = sbt(c6, "aeff", [128, 8]); aeffb = Buf("aeff")
                tmpB = sbt(c6, "tmpB", [128, 128]); tmpBb = Buf("tmpB")
                dve(lambda hh: hh.tensor_copy(out=s0f[:, :, :], in_=smeta[:, :, :]), r=[smetab], w=[s0fb])
                dve(lambda hh: hh.memset(s0b[:, :, :], 0.0), w=[s0bb])
                hg_v = hg_all_d.ap().rearrange("(r p) n -> r p n", p=128)
                k = 0
                for (d, order) in ((0, list(range(NCORE))), (1, list(range(NCORE - 1, -1, -1)))):
                    st, stb_ = (s0f, s0fb) if d == 0 else (s0b, s0bb)
                    for j in order:
                        q = k % 2
                        k += 1
                        S.dma("sp", stch[q], lambda h, q=q, j=j: h.dma_start(out=stg[q][:, :], in_=hg_v[j]), reads=[hgab], writes=[stgb[q]])
                        mcol = (1 + j) if d == 0 else (9 + j)
                        dve(lambda hh, q=q, d=d, mcol=mcol: hh.tensor_scalar(out=aeff[:, 0:4], in0=stg[q][:, 1024 + 4 * d:1028 + 4 * d], scalar1=-1.0,
                                                                            scalar2=corec[:, mcol:mcol + 1], op0=ALU.add, op1=ALU.mult),
                            r=[stgb[q], coreb], w=[aeffb])
                        dve(lambda hh: hh.tensor_scalar(out=aeff[:, 0:4], in0=aeff[:, 0:4], scalar1=1.0, scalar2=None, op0=ALU.add), r=[aeffb], w=[aeffb])
                        for h in range(4):
                            off = 512 * d + h * 128
                            dve(lambda hh, q=q, off=off, mcol=mcol: hh.tensor_scalar(out=tmpB[:, :], in0=stg[q][:, off:off + 128],
                                                                                    scalar1=corec[:, mcol:mcol + 1], scalar2=None, op0=ALU.mult),
                                r=[stgb[q], coreb], w=[tmpBb])
                            dve(lambda hh, h=h, st=st: hh.scalar_tensor_tensor(out=st[:, h, :], in0=st[:, h, :], scalar=aeff[:, h:h + 1], in1=tmpB[:, :],
                                                                               op0=ALU.mult, op1=ALU.add), r=[aeffb, tmpBb, stb_], w=[stb_])
                S.barrier()
            cX.close()

            yaT = sbr(cY, "yaT", [128, 4, NT], BF16)
            yab = [[Buf("ya") for _ in range(4)] for _ in range(4)]
            QT = sbr(cY, "QT", [128, 4, NT], BF16)
            qtb = [[[Buf("qt") for _ in range(4)] for _ in range(4)] for _ in range(4)]
            with ExitStack() as c7:
                gatec = make_gate_ctx(c7)
                qsil = sbt(c7, "qsil", [128, 512]); qsilb = Buf("qsil")
                zgs = sbt(c7, "zgs", [128, 512]); zgsb = Buf("zgs")
                v_h = sbt(c7, "v_h2", [128, 4, 128], BF16); vhb = Buf("vh")
                khT = sbt(c7, "khT2", [128, 512], BF16); khb = Buf("khT")
                khat = sbt(c7, "khat2", [128, 4, 128], BF16); khatb = Buf("khat")
                qtl = [sbt(c7, "qtl%d" % i, [128, 512], BF16) for i in range(2)]
                ktl = [sbt(c7, "ktl%d" % i, [128, 512], BF16) for i in range(2)]
                qkb = [Buf("qk") for _ in range(2)]
                dcol = sbt(c7, "dcol2", [128, 2, 4]); dcolb = Buf("dcol")
                ecol = sbt(c7, "ecol2", [128, 2, 4])
                blc = sbt(c7, "blc2", [128, 8])
                AT = [sbt(c7, "AT%d" % i, [128, 512], BF16) for i in range(2)]
                ATb = [Buf("AT") for _ in range(2)]
                Sf = sbt(c7, "Sf", [128, 128]); Sfb = Buf("Sf")
                Sp = [sbt(c7, "Sp%d" % i, [128, 128], BF16) for i in range(2)]
                Spb = [Buf("Sp") for _ in range(2)]
                stmp = sbt(c7, "stmp", [128, 128]); stmpb = Buf("stmp")
                osq = sbt(c7, "osq", [128, 512], BF16); osqb = Buf("osq")
                ot1, ot1b, ors, orsb, oy, oyb = gatec[0], gatec[1], gatec[2], gatec[3], gatec[4], gatec[5]
                for h in range(4):
                    tq, bq = slots.load(wfm_d[0 + h], 1024)
                    tff, bff = slots.load(wfm_d[4 + h], 1024)
                    tfb, bfb = slots.load(wfm_d[8 + h], 1024)
                    tg_, bg_ = slots.load(wfm_d[12 + h], 1024)
                    tzi, bzi = slots.load(wzi_d[h], 1024)
                    dve(lambda hh, h=h: hh.tensor_copy(out=Sf[:, :], in_=s0f[:, h, :]), r=[s0fb], w=[Sfb])
                    for g in range(4):
                        c0, n = CG[g]
                        for (pi, tw, bw) in ((0, tq, bq), (1, tff, bff), (2, tfb, bfb), (3, tg_, bg_)):
                            for kc in range(DC):
                                mm(PS[pi][:, 0:n], tw[:, kc * 128:(kc + 1) * 128], uT[:, kc, c0:c0 + n], kc == 0, kc == DC - 1, [bw, ub[g]], [pb[pi]])
                        for bi in range(4):
                            t0 = c0 + bi * 128
                            for kc in range(DC):
                                mm(PS[4][:, bi * 128:(bi + 1) * 128], uT[:, kc, t0:t0 + 128], tzi[:, kc * 128:(kc + 1) * 128], kc == 0, kc == DC - 1,
                                   [bzi, ub[g]], [pb[4]])
                        act(lambda hh: hh.activation(out=qsil[:, :], in_=PS[0][:, :], func=AF.Silu), r=[pb[0]], w=[qsilb])
                        act(lambda hh: hh.activation(out=zgs[:, :], in_=PS[3][:, :], func=AF.Silu), r=[pb[3]], w=[zgsb])
                        dve(lambda hh: hh.tensor_copy(out=v_h[:, :, :], in_=PS[4][:, :].rearrange("p (b n) -> p b n", n=128)), r=[pb[4]], w=[vhb])
                        for d in range(2):
                            hg_gates(gatec, PS[1 + d], pb[1 + d], n, 128, d, h, True, qsil, qsilb, khT[:, :], khb,
                                     dcol[:, d, :], dcolb, blc[:, 4 * d:4 * d + 4], qtl[d][:, :], ktl[d][:, :], qkb[d], ecol[:, d, :])
                            if d == 0:
                                for i in range(4):
                                    S.op("pe", lambda hh, i=i: hh.transpose(PT[:, i * 128:(i + 1) * 128], khT[:, i * 128:(i + 1) * 128], cmat[:, 0:128]),
                                         [khb, cmatb], [ptb], self_sync=False)
                                dve(lambda hh: hh.tensor_copy(out=khat[:, :, :], in_=PT[:, 0:512].rearrange("p (b n) -> p b n", n=128)), r=[ptb], w=[khatb])
                        for d in range(2):
                            for i in range(4):
                                mm(PS[d][:, i * 128:(i + 1) * 128], ktl[d][:, i * 128:(i + 1) * 128], qtl[d][:, i * 128:(i + 1) * 128], True, True,
                                   [qkb[d]], [pb[d]])
                            dve(lambda hh, d=d: hh.tensor_tensor(out=AT[d][:, :], in0=PS[d][:, :], in1=mask4[:, d, :], op=ALU.mult),
                                r=[pb[d], maskb], w=[ATb[d]])
                        for i in range(4):
                            j = g * 4 + i
                            dve(lambda hh, i=i: hh.tensor_scalar(out=Sp[0][:, :], in0=Sf[:, :], scalar1=ecol[:, 0, i:i + 1], scalar2=None, op0=ALU.mult),
                                r=[Sfb, dcolb], w=[Spb[0]])
                            dve(lambda hh, h=h, j=j: hh.scalar_tensor_tensor(out=stmp[:, :], in0=s0b[:, h, :], scalar=pbcol[:, h, j:j + 1], in1=sbin0[:, h, j, :],
                                                                             op0=ALU.mult, op1=ALU.add), r=[s0bb, pbcolb[h], sbin0b[h]], w=[stmpb])
                            dve(lambda hh, i=i: hh.tensor_scalar(out=Sp[1][:, :], in0=stmp[:, :], scalar1=ecol[:, 1, i:i + 1], scalar2=None, op0=ALU.mult),
                                r=[stmpb, dcolb], w=[Spb[1]])
                            osl = PS[2][:, i * 128:(i + 1) * 128]
                            mm(osl, v_h[:, i, :], AT[0][:, i * 128:(i + 1) * 128], True, False, [vhb, ATb[0]], [pb[2]])
                            mm(osl, v_h[:, i, :], AT[1][:, i * 128:(i + 1) * 128], False, False, [vhb, ATb[1]], [pb[2]])
                            mm(osl, Sp[0][:, :], qtl[0][:, i * 128:(i + 1) * 128], False, False, [Spb[0], qkb[0]], [pb[2]])
                            mm(osl, Sp[1][:, :], qtl[1][:, i * 128:(i + 1) * 128], False, True, [Spb[1], qkb[1]], [pb[2]])
                            mm(PS[5][:, 0:128], khat[:, i, :], v_h[:, i, :], True, True, [khatb, vhb], [pb[5]])
                            dve(lambda hh, i=i: hh.scalar_tensor_tensor(out=Sf[:, :], in0=Sf[:, :], scalar=dcol[:, 0, i:i + 1], in1=PS[5][:, 0:128],
                                                                        op0=ALU.mult, op1=ALU.add), r=[Sfb, dcolb, pb[5]], w=[Sfb])
                        act(lambda hh: hh.activation(out=osq[:, :], in_=PS[2][:, :], func=AF.Square), r=[pb[2]], w=[osqb])
                        mm(PS[3][:, :], ones_bf[:, :], osq[:, :], True, True, [onesb, osqb], [pb[3]])
                        act(lambda hh: hh.activation(out=ot1[:, :], in_=PS[3][:, :], func=AF.Ln, scale=1.0 / 128, bias=EPS), r=[pb[3]], w=[ot1b])
                        act(lambda hh: hh.activation(out=ors[:, :], in_=ot1[:, :], func=AF.Exp, scale=-0.5), r=[ot1b], w=[orsb])
                        dve(lambda hh, h=h: hh.scalar_tensor_tensor(out=oy[:, :], in0=PS[2][:, :], scalar=vecs[:, 24 + h:25 + h], in1=ors[:, :],
                                                                    op0=ALU.mult, op1=ALU.mult), r=[pb[2], orsb, vecb], w=[oyb])
                        dve(lambda hh, h=h, c0=c0: hh.tensor_tensor(out=yaT[:, h, c0:c0 + 512], in0=oy[:, :], in1=zgs[:, :], op=ALU.mult),
                            r=[oyb, zgsb], w=[yab[h][g]])
                S.barrier()
            with ExitStack() as c8:
                ropec, tdiv, tmod, tdb = make_rope_ctx(c8)
                qkc = make_qk_ctx(c8)
                wq = [slots.load(wfm_d[16 + c], 1024) for c in range(4)]
                wqs = [slots.load(wfm_d[20 + c], 1024) for c in range(4)]
                for g in range(4):
                    c0, n = CG[g]
                    rope_tables(ropec, tdiv, tmod, tdb, g, None)
                    rp = (ropec[0], ropec[1], ropec[2], ropec[3])
                    for c in range(4):
                        r = c % 2
                        for kc in range(DC):
                            mm(PS[r][:, 0:n], wq[c][0][:, kc * 128:(kc + 1) * 128], uT[:, kc, c0:c0 + n], kc == 0, kc == DC - 1, [wq[c][1], ub[g]], [pb[r]])
                        for kc in range(DC):
                            mm(PS[2 + r][:, 0:n], wqs[c][0][:, kc * 128:(kc + 1) * 128], uT[:, kc, c0:c0 + n], kc == 0, kc == DC - 1, [wqs[c][1], ub[g]], [pb[2 + r]])
                        qk_post(qkc, PS[r], pb[r], PS[2 + r], pb[2 + r], n, 44, 46, rp, QT[:, c, c0:c0 + n], qtb[c][g])
                S.barrier()
        S.barrier()

        with ExitStack() as cB:
            KT_all = sbt(cB, "KT_all", [128, NCORE * NT + NM], BF16); ktallb = Buf("ktall")
            V_all = sbt(cB, "V_all", [128, NCORE * 16 + 1, 130], BF16); vallb = Buf("vall")
            PTs = [sbt(cB, "PTs%d" % i, [128, 1024], BF16) for i in range(2)]
            PTb = [Buf("pts") for _ in range(2)]
            rinv = sbt(cB, "rinv", [128, 512]); rinvb = Buf("rinv")
            rbc = sbt(cB, "rbc", [128, 512]); rbcb = Buf("rbc")
            ch_l = [S.chan() for _ in range(2)]
            S.dma("sp", ch_l[0], lambda h: h.dma_start(out=KT_all[:, 0:NCORE * NT].rearrange("p (r n) -> p r n", n=NT),
                                                        in_=kt_all_d.ap().rearrange("(r p) n -> p r n", p=128)), reads=[ktab], writes=[ktallb])
            S.dma("sp", ch_l[1], lambda h: h.dma_start(out=V_all[:, 0:NCORE * 16, :].rearrange("p (r b) n -> p r (b n)", b=16),
                                                        in_=v_all_d.ap().rearrange("(r p) n -> p r n", p=128)), reads=[vab], writes=[vallb])
            dve(lambda h: h.tensor_copy(out=KT_all[:, NCORE * NT:NCORE * NT + NM], in_=KT_meta[:, :]), r=[ktmb], w=[ktallb])
            dve(lambda h: h.tensor_copy(out=V_all[0:NM, NCORE * 16, :], in_=V_meta[0:NM, :]), r=[vmb], w=[vallb])
            NKT = NCORE * 16 + 1
            NP = (NKT + 1) // 2
            grp = 0
            pending = [None]
            QZ = [[sbt(cB, "QZ%d_%d" % (k, i), [128, 512], BF16) for i in range(2)] for k in range(2)]
            QZb = [[Buf("qz") for _ in range(2)] for _ in range(2)]
            for k in range(2):
                for i in range(2):
                    pool(lambda hh, k=k, i=i: hh.memset(QZ[k][i][:, :], 0.0), w=[QZb[k][i]])
            qzi = [0, 0]
            pglob = 0
            for g in range(4):
                c0 = CG[g][0]
                for kvh in range(2):
                    pbase = kvh * 64
                    for half in range(2):
                        t0 = c0 + half * 256
                        for cp in range(2):
                            ob = 4 + grp % 2
                            grp += 1
                            Op = PS[ob]
                            qsl = QT[pbase:pbase + 64, 2 * cp:2 * cp + 2, t0:t0 + 256]
                            qbufs = [qtb[2 * cp][g][kvh * 2 + half], qtb[2 * cp + 1][g][kvh * 2 + half]]
                            zi = qzi[kvh] % 2
                            qzi[kvh] += 1
                            qz, qzb_ = QZ[kvh][zi], QZb[kvh][zi]
                            pool(lambda hh, qz=qz, qsl=qsl, pbase=pbase: hh.tensor_copy(out=qz[pbase:pbase + 64, :].rearrange("p (c n) -> p c n", n=256), in_=qsl),
                                 r=qbufs, w=[qzb_])
                            pg0 = pglob
                            pglob += NP

                            def qk(pp, qz=qz, qzb_=qzb_, pg0=pg0):
                                sp = (pg0 + pp) % 2
                                for hf in range(2):
                                    kt = 2 * pp + hf
                                    if kt >= NKT:
                                        continue
                                    nk = 128 if kt < NKT - 1 else NM
                                    S.op("pe", lambda hh, kt=kt, nk=nk, hf=hf: hh.matmul(PSbig[0:nk, sp * 1024 + hf * 512:sp * 1024 + hf * 512 + 512],
                                                                                       KT_all[:, kt * 128:kt * 128 + nk], qz[:, :], start=True, stop=True),
                                         [ktallb, qzb_], [pb[2 * sp + hf]], self_sync=False)

                            qk(0)
                            for pp in range(NP):
                                sp = (pg0 + pp) % 2
                                full = (2 * pp + 1 < NKT - 1) and WIDE_EXP
                                if full:
                                    act(lambda hh, sp=sp: hh.activation(out=PTs[sp][:, 0:1024], in_=PSbig[:, sp * 1024:sp * 1024 + 1024], func=AF.Exp, scale=0.125),
                                        r=[pb[2 * sp], pb[2 * sp + 1]], w=[PTb[sp]])
                                else:
                                    for hf in range(2):
                                        kt = 2 * pp + hf
                                        if kt >= NKT:
                                            continue
                                        nk = 128 if kt < NKT - 1 else NM
                                        act(lambda hh, sp=sp, hf=hf, nk=nk: hh.activation(out=PTs[sp][0:nk, hf * 512:hf * 512 + 512],
                                                                                        in_=PSbig[0:nk, sp * 1024 + hf * 512:sp * 1024 + hf * 512 + 512], func=AF.Exp, scale=0.125),
                                            r=[pb[2 * sp + hf]], w=[PTb[sp]])
                                if pp + 1 < NP:
                                    qk(pp + 1)
                                for hf in range(2):
                                    kt = 2 * pp + hf
                                    if kt >= NKT:
                                        continue
                                    nk = 128 if kt < NKT - 1 else NM
                                    S.op("pe", lambda hh, Op=Op, kt=kt, nk=nk, sp=sp, hf=hf, kvh=kvh: hh.matmul(Op[0:65, :], V_all[0:nk, kt, 65 * kvh:65 * kvh + 65],
                                                                                                            PTs[sp][0:nk, hf * 512:hf * 512 + 512],
                                                                                                            start=(kt == 0), stop=(kt == NKT - 1)),
                                         [vallb, PTb[sp]], [pb[ob]], self_sync=False)
                                if pp == 6 and pending[0] is not None:
                                    pending[0]()
                                    pending[0] = None

                            def finalize(Op=Op, ob=ob, qsl=qsl, qbufs=qbufs):
                                dve(lambda hh: hh.reciprocal(out=rinv[64:65, :], in_=Op[64:65, :]), r=[pb[ob]], w=[rinvb])
                                mm(PS[6][0:64, :], ones_f[64:65, 0:64], rinv[64:65, :], True, True, [onesfb, rinvb], [pb[6]])
                                act(lambda hh: hh.copy(out=rbc[0:64, :], in_=PS[6][0:64, :]), r=[pb[6]], w=[rbcb])
                                dve(lambda hh: hh.tensor_tensor(out=qsl, in0=Op[0:64, :].rearrange("p (c n) -> p c n", n=256),
                                                                in1=rbc[0:64, :].rearrange("p (c n) -> p c n", n=256), op=ALU.mult),
                                    r=[pb[ob], rbcb], w=qbufs)
                            pending[0] = finalize
            pending[0]()
            S.barrier()
        S.barrier()

        with ExitStack() as cC:
            slots = Slots(cC, 8, 1024)
            for half in range(2):
                HG = [2 * half, 2 * half + 1]
                hoff = half * 1024
                with ExitStack() as c9:
                    uT = sbt(c9, "uTm", [128, DC, 1024], BF16)
                    ub = [Buf("u") for _ in CG]
                    mixT = sbt(c9, "mixT", [128, DC, 1024], BF16)
                    mixb = [[Buf("mix") for _ in range(4)] for _ in range(DC)]
                    sga = [sbt(c9, "sga%d" % i, [128, 512]) for i in range(2)]; sgab = [Buf("sga") for _ in range(2)]
                    sgb2 = [sbt(c9, "sgb%d" % i, [128, 512]) for i in range(2)]; sgbb = [Buf("sgb") for _ in range(2)]
                    m1 = [sbt(c9, "m1%d" % i, [128, 512]) for i in range(2)]; m1b = [Buf("m1") for _ in range(2)]
                    m2 = [sbt(c9, "m2%d" % i, [128, 512]) for i in range(2)]; m2b = [Buf("m2") for _ in range(2)]
                    rmsnorm_all(c9, 8, uT, ub, HG, hoff)
                    rot = 0
                    for s in range(4):
                        tua, bua = slots.load(wua_d[s], 1024)
                        tub, bub = slots.load(wub_d[s], 1024)
                        for oo in range(2):
                            o = 2 * s + oo
                            tga, bga = slots.load(wfm_d[26 + o], 1024)
                            tgb, bgb = slots.load(wfm_d[34 + o], 1024)
                            for g in HG:
                                c0, n = CG[g]
                                r = rot % 2
                                rot += 1
                                za, zb, pa, pbk = 0, 1, 2 + r, 4 + r
                                for kc in range(DC):
                                    mm(PS[za][:, :], tga[:, kc * 128:(kc + 1) * 128], uT[:, kc, c0 - hoff:c0 - hoff + n], kc == 0, kc == DC - 1, [bga, ub[g]], [pb[za]])
                                for kc in range(DC):
                                    mm(PS[zb][:, :], tgb[:, kc * 128:(kc + 1) * 128], uT[:, kc, c0 - hoff:c0 - hoff + n], kc == 0, kc == DC - 1, [bgb, ub[g]], [pb[zb]])
                                for kc in range(4):
                                    mm(PS[pa][:, :], tua[:, (oo * 4 + kc) * 128:(oo * 4 + kc + 1) * 128], yaT[:, kc, c0:c0 + n], kc == 0, kc == 3,
                                       [bua, yab[kc][g]], [pb[pa]])
                                for kc in range(4):
                                    mm(PS[pbk][:, :], tub[:, (oo * 4 + kc) * 128:(oo * 4 + kc + 1) * 128], QT[:, kc, c0:c0 + n], kc == 0, kc == 3,
                                       [bub] + qtb[kc][g], [pb[pbk]])
                                act(lambda hh, r=r, za=za: hh.activation(out=sga[r][:, :], in_=PS[za][:, :], func=AF.Sigmoid), r=[pb[za]], w=[sgab[r]])
                                act(lambda hh, r=r, zb=zb: hh.activation(out=sgb2[r][:, :], in_=PS[zb][:, :], func=AF.Sigmoid), r=[pb[zb]], w=[sgbb[r]])
                                dve(lambda hh, r=r, pa=pa: hh.tensor_tensor(out=m1[r][:, :], in0=sga[r][:, :], in1=PS[pa][:, :], op=ALU.mult),
                                    r=[sgab[r], pb[pa]], w=[m1b[r]])
                                dve(lambda hh, r=r, pbk=pbk: hh.tensor_tensor(out=m2[r][:, :], in0=sgb2[r][:, :], in1=PS[pbk][:, :], op=ALU.mult),
                                    r=[sgbb[r], pb[pbk]], w=[m2b[r]])
                                dve(lambda hh, r=r, o=o, c0=c0: hh.tensor_tensor(out=mixT[:, o, c0 - hoff:c0 - hoff + 512], in0=m1[r][:, :], in1=m2[r][:, :], op=ALU.add),
                                    r=[m1b[r], m2b[r]], w=[mixb[o][g]])
                    rot = 0
                    for o in range(DC):
                        two, bwo = slots.load(wo_d[o], 1024)
                        for g in HG:
                            c0, n = CG[g]
                            r = rot % 2
                            rot += 1
                            for kc in range(DC):
                                mm(PS[4 + r][:, :], two[:, kc * 128:(kc + 1) * 128], mixT[:, kc, c0 - hoff:c0 - hoff + n], kc == 0, kc == DC - 1,
                                   [bwo, mixb[kc][g]], [pb[4 + r]])
                            dve(lambda hh, r=r, o=o, c0=c0: hh.tensor_tensor(out=hT[:, o, c0:c0 + 512], in0=PS[4 + r][:, :], in1=hT[:, o, c0:c0 + 512], op=ALU.add),
                                r=[pb[4 + r], hb[o][g]], w=[hb[o][g]])
                    S.barrier()
            cY.close()
            with ExitStack() as c10:
                uT = sbt(c10, "uT2", [128, DC, NA], BF16)
                ub = [Buf("u") for _ in CG]
                rmsnorm_all(c10, 16, uT, ub, [0, 1, 2, 3])
                ffn(c10, slots, wg_d[1], wu_d[1], wd_d[1], uT, ub, [0, 1, 2, 3])
            outT_v = outT_d.rearrange("(c p) t -> p c t", p=128)
            obufs = []
            for g in range(4):
                c0, n = CG[g]
                cho = S.chan()
                ob_ = Buf("out")
                S.dma("sp", cho, lambda h, c0=c0, n=n: h.dma_start(out=outT_v[:, :, c0:c0 + n], in_=hT[:, :, c0:c0 + n]),
                      reads=[hb[c][g] for c in range(DC)], writes=[ob_])
                obufs.append(ob_)
            S.wait_bufs("sp", obufs)
            S.barrier()
    return nc


def _fm_layout(w_cols):
    return np.ascontiguousarray(w_cols.reshape(8, 128, 128).transpose(1, 0, 2).reshape(128, 1024))


def _host_inputs(x, meta_tokens, ffn1_norm, ffn1_w_gate, ffn1_w_up, ffn1_w_down, mix_norm, w_in, hg_lb_fwd, hg_lb_bwd,
                 hg_out_norm, q_norm, k_norm, w_up_a, w_up_b, w_out, ffn2_norm, ffn2_w_gate, ffn2_w_up, ffn2_w_down):
    f32 = lambda a: np.asarray(a, dtype=np.float32)
    shared = {}
    p = np.arange(128)

    def ffn_pack(idx, wg, wu, wd):
        wg, wu, wd = f32(wg)[0], f32(wu)[0], f32(wd)[0]
        shared["wg%d" % idx] = np.stack([_fm_layout(wg[:, j * 128:(j + 1) * 128]) for j in range(JC)])
        shared["wu%d" % idx] = np.stack([_fm_layout(wu[:, j * 128:(j + 1) * 128]) for j in range(JC)])
        shared["wd%d" % idx] = np.ascontiguousarray(wd.reshape(JC, 128, DC, 128).transpose(2, 1, 0, 3).reshape(DC, 128, DFF))

    ffn_pack(1, ffn1_w_gate, ffn1_w_up, ffn1_w_down)
    ffn_pack(2, ffn2_w_gate, ffn2_w_up, ffn2_w_down)
    W = f32(w_in)[0]
    cols = []
    for base in (0, 1024, 1536, 2048):
        for h in range(4):
            cols.append(base + h * 128 + p)
    dd = np.arange(64)
    for swap in (0, 1):
        for c in range(4):
            d_ = dd ^ 1 if swap else dd
            cols.append(np.concatenate([2560 + c * 64 + d_, 2560 + (4 + c) * 64 + d_]))
    for swap in (0, 1):
        d_ = dd ^ 1 if swap else dd
        cols.append(np.concatenate([3072 + d_, 3072 + 64 + d_]))
    for base in (3328, 4352):
        for o in range(8):
            cols.append(base + o * 128 + p)
    assert len(cols) == 42
    shared["wfm"] = np.stack([_fm_layout(W[:, c]) for c in cols])
    shared["wzi"] = np.stack([_fm_layout(W[:, 512 + h * 128:512 + (h + 1) * 128]) for h in range(4)])
    shared["wv"] = _fm_layout(W[:, 3200:3328])

    def up_pack(wu_, rowperm):
        wu_ = f32(wu_)[0][rowperm]
        out = np.zeros((4, 128, 1024), np.float32)
        for s in range(4):
            for oo in range(2):
                o = 2 * s + oo
                blk = wu_[:, o * 128:(o + 1) * 128].reshape(4, 128, 128).transpose(1, 0, 2).reshape(128, 512)
                out[s, :, oo * 512:(oo + 1) * 512] = blk
        return out

    shared["wua"] = up_pack(w_up_a, np.arange(512))
    permb = np.concatenate([np.concatenate([c * 64 + dd, (4 + c) * 64 + dd]) for c in range(4)])
    shared["wub"] = up_pack(w_up_b, permb)
    Wo = f32(w_out)[0]
    shared["wo"] = np.stack([_fm_layout(Wo[:, o * 128:(o + 1) * 128]) for o in range(DC)])
    vecs = np.zeros((128, NV), np.float32)
    vecs[:, 0:8] = f32(ffn1_norm)[0].reshape(8, 128).T
    vecs[:, 8:16] = f32(mix_norm)[0].reshape(8, 128).T
    vecs[:, 16:24] = f32(ffn2_norm)[0].reshape(8, 128).T
    vecs[:, 24:28] = f32(hg_out_norm)[0].reshape(4, 128).T
    vecs[:, 28:32] = f32(hg_lb_fwd)[0].reshape(4, 128).T
    vecs[:, 32:36] = f32(hg_lb_fwd)[1].reshape(4, 128).T
    vecs[:, 36:40] = f32(hg_lb_bwd)[0].reshape(4, 128).T
    vecs[:, 40:44] = f32(hg_lb_bwd)[1].reshape(4, 128).T
    qn, kn = f32(q_norm)[0], f32(k_norm)[0]
    vecs[:, 44] = qn[p % 64]
    vecs[:, 45] = kn[p % 64]
    vecs[:, 46] = qn[(p % 64) ^ 1]
    vecs[:, 47] = kn[(p % 64) ^ 1]
    pi = (p % 64) // 2
    vecs[:, 48] = (pi % 16).astype(np.float32)
    vecs[:, 49] = (pi < 16).astype(np.float32)
    vecs[:, 50] = (pi >= 16).astype(np.float32)
    vecs[:, 51] = np.where(p % 2 == 0, -1.0, 1.0)
    shared["vecs"] = vecs
    cm = np.zeros((128, 384), np.float32)
    cm[:, 0:128] = np.eye(128, dtype=np.float32)
    cm[:, 128:256] = np.triu(np.ones((128, 128), np.float32))
    cm[:, 256:384] = np.tril(np.ones((128, 128), np.float32))
    shared["cmat"] = cm
    shared["metaT"] = np.ascontiguousarray(f32(meta_tokens).T)
    xs = f32(x)[0]
    in_maps = []
    for r in range(NCORE):
        m = dict(shared)
        m["xT"] = np.ascontiguousarray(xs[r * NT:(r + 1) * NT].T)
        cc = np.zeros((128, 17), np.float32)
        cc[:, 0] = r * (NT // 64)
        for j in range(NCORE):
            cc[:, 1 + j] = 1.0 if j < r else 0.0
            cc[:, 9 + j] = 1.0 if j > r else 0.0
        m["corec"] = cc
        in_maps.append(m)
    return in_maps


_NC_CACHE = {}


def kernel(**inputs):
    in_maps = _host_inputs(**inputs)
    if "nc" not in _NC_CACHE:
        _NC_CACHE["nc"] = build_program()
    res = run_bass_kernel_spmd(_NC_CACHE["nc"], in_maps, core_ids=list(range(NCORE)))
    outs = [np.asarray(res.results[r]["outT"]).T for r in range(NCORE)]
    return np.ascontiguousarray(np.concatenate(outs, axis=0)[None].astype(np.float32))
```

```python
import numpy as np
from contextlib import ExitStack
import concourse.bass as bass
import concourse.mybir as mybir
from concourse.bass_utils import run_bass_kernel_spmd

F32 = mybir.dt.float32
BF16 = mybir.dt.bfloat16
I32 = mybir.dt.int32
AF = mybir.ActivationFunctionType
ALU = mybir.AluOpType

NCORE = 8
D = 1024
DC = 8
NT = 2048
NM = 16
NA = NT + NM
DFF = 2816
JC = 22
EPS = 1e-6
CG = [(0, 512), (512, 512), (1024, 512), (1536, 512), (2048, 16)]
NV = 52
WIDE_EXP = False
TWO_PI = 6.28318
HGW = 1032


class Buf:
    __slots__ = ("name", "w", "r")

    def __init__(self, name=""):
        self.name = name
        self.w = None
        self.r = {}


class Sched:
    ENG = ("pe", "act", "dve", "pool", "sp")

    def __init__(self, nc, ctx):
        self.nc = nc
        self.ctx = ctx
        self.h = {"pe": nc.tensor, "act": nc.scalar, "dve": nc.vector, "pool": nc.gpsimd, "sp": nc.sync}
        self.sems = {}
        self.count = {}
        self.waited = {e: {} for e in self.ENG}
        for e in self.ENG:
            self.sems[e] = ctx.enter_context(nc.semaphore("sem_" + e))
            self.count[e] = 0
        self.nchan = 0
        self.ninstr = 0

    def chan(self):
        key = "ch%d" % self.nchan
        self.nchan += 1
        self.sems[key] = self.ctx.enter_context(self.nc.semaphore("sem_" + key))
        self.count[key] = 0
        return key

    def _waits(self, eng, reads, writes, self_sync):
        need = {}
        for b in reads:
            if b.w is not None:
                k, v = b.w
                if need.get(k, 0) < v:
                    need[k] = v
        for b in writes:
            if b.w is not None:
                k, v = b.w
                if need.get(k, 0) < v:
                    need[k] = v
            for k, v in b.r.items():
                if need.get(k, 0) < v:
                    need[k] = v
        wd = self.waited[eng]
        hnd = self.h[eng]
        for k, v in need.items():
            if (not self_sync) and k == eng:
                continue
            if wd.get(k, 0) < v:
                wd[k] = v
                hnd.wait_ge(self.sems[k], v)

    def _mark(self, key, val, reads, writes):
        for b in reads:
            if b.r.get(key, 0) < val:
                b.r[key] = val
        for b in writes:
            b.w = (key, val)
            b.r = {}

    def op(self, eng, fn, reads=(), writes=(), self_sync=True):
        self._waits(eng, reads, writes, self_sync)
        self.count[eng] += 1
        fn(self.h[eng]).then_inc(self.sems[eng], 1)
        self._mark(eng, self.count[eng], reads, writes)
        self.ninstr += 1

    def dma(self, q, ch, fn, reads=(), writes=(), inc=16):
        self._waits(q, reads, writes, True)
        self.count[ch] += inc
        fn(self.h[q]).then_inc(self.sems[ch], inc)
        self._mark(ch, self.count[ch], reads, writes)
        self.ninstr += 1

    def wait_bufs(self, eng, bufs):
        self._waits(eng, bufs, (), True)

    def barrier(self):
        for e in self.ENG:
            wd = self.waited[e]
            for k, v in self.count.items():
                if k == e or v == 0:
                    continue
                if wd.get(k, 0) < v:
                    wd[k] = v
                    self.h[e].wait_ge(self.sems[k], v)


def build_program():
    nc = bass.Bass("TRN2", target_bir_lowering=False)
    dt = lambda name, shape, dtype=F32, kind="ExternalInput": nc.dram_tensor(name, shape, dtype, kind=kind)
    xT_d = dt("xT", [D, NT]).ap()
    metaT_d = dt("metaT", [D, NM]).ap()
    vecs_d = dt("vecs", [128, NV]).ap()
    corec_d = dt("corec", [128, 17]).ap()
    cmat_d = dt("cmat", [128, 3 * 128]).ap()
    wg_d = [dt("wg%d" % i, [JC, 128, 1024]).ap() for i in (1, 2)]
    wu_d = [dt("wu%d" % i, [JC, 128, 1024]).ap() for i in (1, 2)]
    wd_d = [dt("wd%d" % i, [DC, 128, DFF]).ap() for i in (1, 2)]
    wfm_d = dt("wfm", [42, 128, 1024]).ap()
    wzi_d = dt("wzi", [4, 128, 1024]).ap()
    wv_d = dt("wv", [128, 1024]).ap()
    wua_d = dt("wua", [4, 128, 1024]).ap()
    wub_d = dt("wub", [4, 128, 1024]).ap()
    wo_d = dt("wo", [DC, 128, 1024]).ap()
    outT_d = dt("outT", [D, NT], F32, "ExternalOutput").ap()
    kt_loc_d = nc.dram_tensor("kt_loc", [128, NT], BF16)
    kt_all_d = nc.dram_tensor("kt_all", [NCORE * 128, NT], BF16)
    v_loc_d = nc.dram_tensor("v_loc", [128, 16 * 130], BF16)
    v_all_d = nc.dram_tensor("v_all", [NCORE * 128, 16 * 130], BF16)
    hg_loc_d = nc.dram_tensor("hg_loc", [128, HGW], F32)
    hg_all_d = nc.dram_tensor("hg_all", [NCORE * 128, HGW], F32)

    with ExitStack() as ctx:
        S = Sched(nc, ctx)

        uniq = [0]

        def sbt(c, name, shape, dtype=F32, side="left"):
            uniq[0] += 1
            return c.enter_context(nc.sbuf_tensor("s%d_%s" % (uniq[0], name), shape, dtype, side=side))

        def sbr(c, name, shape, dtype=F32):
            return sbt(c, name, shape, dtype, side="right")

        PSbig = ctx.enter_context(nc.psum_tensor("psbig", [128, 2048], F32))
        PS = [PSbig[:, i * 512:(i + 1) * 512] for i in range(4)] + \
             [ctx.enter_context(nc.psum_tensor("ps%d" % i, [128, 512], F32)) for i in range(4, 7)]
        pb = [Buf("ps%d" % i) for i in range(7)]
        PT = ctx.enter_context(nc.psum_tensor("pt", [128, 1024], BF16))
        ptb = Buf("pt")

        hT = sbr(ctx, "hT", [128, DC, NA])
        hb = [[Buf("h") for _ in CG] for _ in range(DC)]
        vecs = sbr(ctx, "vecs", [128, NV]); vecb = Buf("vecs")
        corec = sbr(ctx, "corec", [128, 17]); coreb = Buf("corec")
        der = sbr(ctx, "der", [128, 32]); derb = Buf("der")
        cmat = sbr(ctx, "cmat", [128, 384], BF16); cmatb = Buf("cmat")
        mask4 = sbr(ctx, "mask4", [128, 2, 512], BF16); maskb = Buf("mask4")
        ones_bf = sbr(ctx, "ones_bf", [128, 128], BF16); onesb = Buf("ones")
        ones_f = sbr(ctx, "ones_f", [128, 128]); onesfb = Buf("onesf")
        blk_f = sbr(ctx, "blk_f", [128, 128]); blkb = Buf("blk")
        rmask = sbr(ctx, "rmask", [128, 512]); rmaskb = Buf("rmask")
        KT_meta = sbr(ctx, "KT_meta", [128, NM], BF16); ktmb = Buf("ktm")
        V_meta = sbr(ctx, "V_meta", [128, 130], BF16); vmb = Buf("vm")

        def vcol(i, n=1):
            return vecs[:, i:i + n]
        D_LBF, D_OMLF, D_NOMLF, D_LBB, D_OMLB, D_NOMLB = 0, 4, 8, 12, 16, 20
        D_INVR, D_SINSC = 24, 25

        act = lambda fn, r=(), w=(): S.op("act", fn, r, w)
        dve = lambda fn, r=(), w=(): S.op("dve", fn, r, w)
        pool = lambda fn, r=(), w=(): S.op("pool", fn, r, w)

        def mm(out, lhsT, rhs, start, stop, r, w):
            S.op("pe", lambda h: h.matmul(out, lhsT, rhs, start=start, stop=stop), r, w, self_sync=False)

        ch_in = [S.chan() for _ in range(8)]
        S.dma("sp", ch_in[0], lambda h: h.dma_start(out=vecs[:], in_=vecs_d), writes=[vecb])
        S.dma("sp", ch_in[1], lambda h: h.dma_start(out=corec[:], in_=corec_d), writes=[coreb])
        S.dma("pool", ch_in[2], lambda h: h.dma_start(out=cmat[:], in_=cmat_d), writes=[cmatb])
        xT_v = xT_d.rearrange("(c p) t -> p c t", p=128)
        for g in range(4):
            c0, n = CG[g]
            S.dma("sp", ch_in[3 + g], lambda h, c0=c0, n=n: h.dma_start(out=hT[:, :, c0:c0 + n], in_=xT_v[:, :, c0:c0 + n]),
                  writes=[hb[c][g] for c in range(DC)])
        S.dma("sp", ch_in[7], lambda h: h.dma_start(out=hT[:, :, NT:NA], in_=metaT_d.rearrange("(c p) t -> p c t", p=128)),
              writes=[hb[c][4] for c in range(DC)])
        dve(lambda h: h.memset(ones_bf[:], 1.0), w=[onesb])
        dve(lambda h: h.memset(ones_f[:], 1.0), w=[onesfb])
        dve(lambda h: h.memset(rmask[:], 1.0), w=[rmaskb])
        dve(lambda h: h.memset(rmask[:, 0:512:128], 0.0), w=[rmaskb])
        dve(lambda h: h.memset(blk_f[:], 0.0), w=[blkb])
        dve(lambda h: h.memset(blk_f[0:64, 0:64], 1.0), w=[blkb])
        dve(lambda h: h.memset(blk_f[64:128, 64:128], 1.0), w=[blkb])
        for k in range(4):
            dve(lambda h, k=k: h.tensor_copy(out=mask4[:, 0, k * 128:(k + 1) * 128], in_=cmat[:, 128:256]), r=[cmatb], w=[maskb])
            dve(lambda h, k=k: h.tensor_copy(out=mask4[:, 1, k * 128:(k + 1) * 128], in_=cmat[:, 256:384]), r=[cmatb], w=[maskb])
        for (src0, dl, do, dn) in ((28, D_LBF, D_OMLF, D_NOMLF), (36, D_LBB, D_OMLB, D_NOMLB)):
            dve(lambda h, s=src0, dl=dl: h.tensor_tensor(out=der[:, dl:dl + 4], in0=vecs[:, s:s + 4], in1=vecs[:, s + 4:s + 8], op=ALU.subtract),
                r=[vecb], w=[derb])
            act(lambda h, dl=dl, do=do: h.activation(out=der[:, do:do + 4], in_=der[:, dl:dl + 4], func=AF.Sigmoid, scale=-1.0), r=[derb], w=[derb])
            act(lambda h, dl=dl: h.activation(out=der[:, dl:dl + 4], in_=der[:, dl:dl + 4], func=AF.Sigmoid), r=[derb], w=[derb])
            dve(lambda h, do=do, dn=dn: h.tensor_scalar(out=der[:, dn:dn + 4], in0=der[:, do:do + 4], scalar1=-1.0, scalar2=None, op0=ALU.mult),
                r=[derb], w=[derb])
        act(lambda h: h.activation(out=der[:, D_INVR:D_INVR + 1], in_=vecs[:, 48:49], func=AF.Exp, scale=-float(np.log(10000.0) / 16.0)),
            r=[vecb], w=[derb])
        dve(lambda h: h.tensor_scalar(out=der[:, D_INVR:D_INVR + 1], in0=der[:, D_INVR:D_INVR + 1], scalar1=float(1.0 / (2 * np.pi)), scalar2=None, op0=ALU.mult),
            r=[derb], w=[derb])
        dve(lambda h: h.tensor_scalar(out=der[:, D_SINSC:D_SINSC + 1], in0=vecs[:, 51:52], scalar1=-TWO_PI, scalar2=None, op0=ALU.mult),
            r=[vecb], w=[derb])

        class Slots:
            def __init__(self, c, n, width):
                self.t = [sbt(c, "slot%d_%d" % (id(self) % 1000, i), [128, width], BF16) for i in range(n)]
                self.b = [Buf("slot") for _ in range(n)]
                self.ch = [S.chan() for _ in range(n)]
                self.n = n
                self.i = 0

            def load(self, src, width):
                k = self.i % self.n
                self.i += 1
                t, b = self.t[k], self.b[k]
                S.dma("pool", self.ch[k], lambda h: h.dma_start(out=t[:, 0:width], in_=src), writes=[b])
                return t, b

        def rmsnorm_all(c, wcol0, uT, ub, groups, coff=0):
            with ExitStack() as lc:
                sq = [sbt(lc, "nsq%d" % i, [128, DC, 512], BF16) for i in range(2)]
                sqb = [Buf("sq") for _ in range(2)]
                t1 = sbt(lc, "nt1", [128, 512]); t1b = Buf("t1")
                rs = sbt(lc, "nrs", [128, 512]); rsb = Buf("rs")
                for gi, g in enumerate(groups):
                    c0, n = CG[g]
                    q = gi % 2
                    for cc in range(DC):
                        act(lambda h, cc=cc: h.activation(out=sq[q][:, cc, 0:n], in_=hT[:, cc, c0:c0 + n], func=AF.Square),
                            r=[hb[cc][g]], w=[sqb[q]])
                    for cc in range(DC):
                        mm(PS[6][:, 0:n], ones_bf[:, :], sq[q][:, cc, 0:n], cc == 0, cc == DC - 1, [sqb[q], onesb], [pb[6]])
                    act(lambda h: h.activation(out=t1[:, 0:n], in_=PS[6][:, 0:n], func=AF.Ln, scale=1.0 / D, bias=EPS), r=[pb[6]], w=[t1b])
                    act(lambda h: h.activation(out=rs[:, 0:n], in_=t1[:, 0:n], func=AF.Exp, scale=-0.5), r=[t1b], w=[rsb])
                    for cc in range(DC):
                        dve(lambda h, cc=cc: h.scalar_tensor_tensor(out=uT[:, cc, c0 - coff:c0 - coff + n], in0=hT[:, cc, c0:c0 + n],
                                                                    scalar=vecs[:, wcol0 + cc:wcol0 + cc + 1], in1=rs[:, 0:n],
                                                                    op0=ALU.mult, op1=ALU.mult),
                            r=[hb[cc][g], rsb, vecb], w=[ub[g]])
                S.barrier()

        def ffn(c, slots, wg, wu, wd, uT, ub, groups):
            with ExitStack() as lc:
                aT = sbt(lc, "aT", [128, 6, NA], BF16)
                ab = [[Buf("a") for _ in CG] for _ in range(6)]
                sg = [sbt(lc, "fsg%d" % i, [128, 512]) for i in range(2)]
                sgb = [Buf("sg") for _ in range(2)]
                rot = 0
                for (j0, nj) in ((0, 6), (6, 6), (12, 5), (17, 5)):
                    for jj in range(nj):
                        j = j0 + jj
                        tg, bg = slots.load(wg[j], 1024)
                        tu, bu = slots.load(wu[j], 1024)
                        for g in groups:
                            c0, n = CG[g]
                            r = rot % 2
                            rot += 1
                            Gp, Up = PS[r], PS[2 + r]
                            for kc in range(DC):
                                mm(Gp[:, 0:n], tg[:, kc * 128:(kc + 1) * 128], uT[:, kc, c0:c0 + n], kc == 0, kc == DC - 1, [bg, ub[g]], [pb[r]])
                            for kc in range(DC):
                                mm(Up[:, 0:n], tu[:, kc * 128:(kc + 1) * 128], uT[:, kc, c0:c0 + n], kc == 0, kc == DC - 1, [bu, ub[g]], [pb[2 + r]])
                            act(lambda h, r=r, Gp=Gp: h.activation(out=sg[r][:, 0:n], in_=Gp[:, 0:n], func=AF.Silu), r=[pb[r]], w=[sgb[r]])
                            dve(lambda h, r=r, Up=Up, jj=jj: h.tensor_tensor(out=aT[:, jj, c0:c0 + n], in0=sg[r][:, 0:n], in1=Up[:, 0:n], op=ALU.mult),
                                r=[sgb[r], pb[2 + r]], w=[ab[jj][g]])
                    for o in range(DC):
                        td, bd = slots.load(wd[o][:, j0 * 128:(j0 + nj) * 128], nj * 128)
                        for g in groups:
                            c0, n = CG[g]
                            r = rot % 2
                            rot += 1
                            Yp = PS[4 + r]
                            for jj in range(nj):
                                mm(Yp[:, 0:n], td[:, jj * 128:(jj + 1) * 128], aT[:, jj, c0:c0 + n], jj == 0, jj == nj - 1, [bd, ab[jj][g]], [pb[4 + r]])
                            dve(lambda h, Yp=Yp, o=o: h.scalar_tensor_tensor(out=hT[:, o, c0:c0 + n], in0=Yp[:, 0:n], scalar=0.5,
                                                                             in1=hT[:, o, c0:c0 + n], op0=ALU.mult, op1=ALU.add),
                                r=[pb[4 + r], hb[o][g]], w=[hb[o][g]])
                S.barrier()

        def rope_tables(lc_t, tdiv, tmod, tdb, g, want):
            (Ct, Cb, St, Sb, tmp, tmpb, tmi, tmib, rowoff, rowoffb) = lc_t
            c0, n = CG[g]
            dve(lambda h: h.tensor_scalar(out=rowoff[:, 0:1], in0=corec[:, 0:1], scalar1=float(c0 // 64), scalar2=vecs[:, 49:50],
                                          op0=ALU.add, op1=ALU.mult), r=[coreb, vecb], w=[rowoffb])
            dve(lambda h: h.tensor_scalar(out=tmp[0][:, :], in0=tdiv[:, :], scalar1=vecs[:, 49:50], scalar2=rowoff[:, 0:1],
                                          op0=ALU.mult, op1=ALU.add), r=[tdb, vecb, rowoffb], w=[tmpb[0]])
            dve(lambda h: h.scalar_tensor_tensor(out=tmp[1][:, :], in0=tmod[:, :], scalar=vecs[:, 50:51], in1=tmp[0][:, :],
                                                 op0=ALU.mult, op1=ALU.add), r=[tdb, vecb, tmpb[0]], w=[tmpb[1]])
            for (shift, T, Tb, sc) in ((0.0, St, Sb, der[:, D_SINSC:D_SINSC + 1]), (0.25, Ct, Cb, -TWO_PI)):
                dve(lambda h, shift=shift: h.tensor_scalar(out=tmp[0][:, :], in0=tmp[1][:, :], scalar1=der[:, D_INVR:D_INVR + 1], scalar2=shift,
                                                           op0=ALU.mult, op1=ALU.add), r=[tmpb[1], derb], w=[tmpb[0]])
                dve(lambda h: h.tensor_copy(out=tmi[:, :], in_=tmp[0][:, :]), r=[tmpb[0]], w=[tmib])
                dve(lambda h: h.tensor_copy(out=tmp[2][:, :], in_=tmi[:, :]), r=[tmib], w=[tmpb[2]])
                dve(lambda h: h.tensor_tensor(out=tmp[0][:, :], in0=tmp[0][:, :], in1=tmp[2][:, :], op=ALU.subtract), r=[tmpb[0], tmpb[2]], w=[tmpb[0]])
                dve(lambda h: h.scalar_tensor_tensor(out=tmp[2][:, :], in0=tmp[0][:, :], scalar=0.5, in1=tmp[0][:, :],
                                                     op0=ALU.is_gt, op1=ALU.subtract), r=[tmpb[0]], w=[tmpb[2]])
                act(lambda h, T=T, sc=sc: h.activation(out=T[:, :], in_=tmp[2][:, :], func=AF.Sin, scale=sc), r=[tmpb[2], derb], w=[Tb])

        def make_rope_ctx(lc):
            Ct = sbt(lc, "ropeC", [128, 512]); St = sbt(lc, "ropeS", [128, 512])
            tmp = [sbt(lc, "ropet%d" % i, [128, 512]) for i in range(3)]
            tmi = sbt(lc, "ropei", [128, 512], I32)
            rowoff = sbt(lc, "rowoff", [128, 1])
            tdiv = sbt(lc, "tdiv", [128, 512]); tmod = sbt(lc, "tmod", [128, 512]); tdb = Buf("td")
            pool(lambda h: h.iota(tdiv[:, :], [[1, 8], [0, 64]], base=0, channel_multiplier=0, allow_small_or_imprecise_dtypes=True), w=[tdb])
            pool(lambda h: h.iota(tmod[:, :], [[0, 8], [1, 64]], base=0, channel_multiplier=0, allow_small_or_imprecise_dtypes=True), w=[tdb])
            return (Ct, Buf("C"), St, Buf("S"), tmp, [Buf("t") for _ in range(3)], tmi, Buf("ti"), rowoff, Buf("ro")), tdiv, tmod, tdb

        def qk_post(lcw, Zp, zpb, Zsp, zspb, n, gcol, gscol, rope, out_ap, outb):
            (sqf, sqfb, t1, t1b, rs, rsb, qn, qnb, qs, qsb) = lcw
            act(lambda h: h.activation(out=sqf[:, 0:n], in_=Zp[:, 0:n], func=AF.Square), r=[zpb], w=[sqfb])
            mm(PS[6][:, 0:n], blk_f[:, :], sqf[:, 0:n], True, True, [blkb, sqfb], [pb[6]])
            act(lambda h: h.activation(out=t1[:, 0:n], in_=PS[6][:, 0:n], func=AF.Ln, scale=1.0 / 64, bias=EPS), r=[pb[6]], w=[t1b])
            act(lambda h: h.activation(out=rs[:, 0:n], in_=t1[:, 0:n], func=AF.Exp, scale=-0.5), r=[t1b], w=[rsb])
            if rope is None:
                dve(lambda h: h.scalar_tensor_tensor(out=out_ap, in0=Zp[:, 0:n], scalar=vecs[:, gcol:gcol + 1], in1=rs[:, 0:n],
                                                     op0=ALU.mult, op1=ALU.mult), r=[zpb, rsb, vecb], w=outb)
                return
            Ct, Cb, St, Sb = rope
            dve(lambda h: h.scalar_tensor_tensor(out=qn[:, 0:n], in0=Zp[:, 0:n], scalar=vecs[:, gcol:gcol + 1], in1=rs[:, 0:n],
                                                 op0=ALU.mult, op1=ALU.mult), r=[zpb, rsb, vecb], w=[qnb])
            dve(lambda h: h.scalar_tensor_tensor(out=qs[:, 0:n], in0=Zsp[:, 0:n], scalar=vecs[:, gscol:gscol + 1], in1=rs[:, 0:n],
                                                 op0=ALU.mult, op1=ALU.mult), r=[zspb, rsb, vecb], w=[qsb])
            dve(lambda h: h.tensor_tensor(out=qn[:, 0:n], in0=qn[:, 0:n], in1=Ct[:, 0:n], op=ALU.mult), r=[qnb, Cb], w=[qnb])
            dve(lambda h: h.tensor_tensor(out=qs[:, 0:n], in0=qs[:, 0:n], in1=St[:, 0:n], op=ALU.mult), r=[qsb, Sb], w=[qsb])
            dve(lambda h: h.tensor_tensor(out=out_ap, in0=qn[:, 0:n], in1=qs[:, 0:n], op=ALU.add), r=[qnb, qsb], w=outb)

        def make_qk_ctx(lc):
            names = ["sqf", "t1", "rs", "qn", "qs"]
            out = []
            for nm in names:
                out.append(sbt(lc, "qk_" + nm, [128, 512]))
                out.append(Buf(nm))
            return tuple(out)

        def hg_gates(lcw, Zf, zfb, n, L, d, h, full, qsil, qsilb, khT_ap, khb, dcol_ap, dcolb, blc_ap,
                     qt_ap=None, kt_ap=None, qkb=None, ecol_ap=None, need_khat=True):
            (sgt, sgb_, lf, lfb, kk, kkb, cum, cumb, bb, bbb, X, Xb, E, Eb) = lcw
            nch = n // L
            v3 = lambda t: t[:, 0:n].rearrange("p (c l) -> p c l", l=L)
            col = lambda t, off: t[:, off:n:L]
            bc = lambda t, off: col(t, off).unsqueeze(2).to_broadcast([128, nch, L])
            lb_c, oml_c, noml_c = (D_LBF, D_OMLF, D_NOMLF) if d == 0 else (D_LBB, D_OMLB, D_NOMLB)
            act(lambda hh: hh.activation(out=sgt[:, 0:n], in_=Zf[:, 0:n], func=AF.Sigmoid), r=[zfb], w=[sgb_])
            yield
            act(lambda hh: hh.activation(out=lf[:, 0:n], in_=sgt[:, 0:n], func=AF.Ln, scale=der[:, oml_c + h:oml_c + h + 1],
                                         bias=der[:, lb_c + h:lb_c + h + 1]), r=[sgb_, derb], w=[lfb])
            yield
            dve(lambda hh: hh.tensor_scalar(out=kk[:, 0:n], in0=sgt[:, 0:n], scalar1=der[:, noml_c + h:noml_c + h + 1],
                                            scalar2=der[:, oml_c + h:oml_c + h + 1], op0=ALU.mult, op1=ALU.add), r=[sgb_, derb], w=[kkb])
            yield
            dve(lambda hh: hh.tensor_tensor_scan(out=cum[:, 0:n], data0=rmask[:, 0:n], data1=lf[:, 0:n], initial=0.0, op0=ALU.mult, op1=ALU.add),
                r=[rmaskb, lfb], w=[cumb])
            yield
            if d == 0:
                b, bbuf = cum, cumb
                last_off, mid_off = L - 1, L // 2 - 1
            else:
                dve(lambda hh: hh.tensor_tensor(out=bb[:, 0:n], in0=lf[:, 0:n], in1=cum[:, 0:n], op=ALU.subtract), r=[lfb, cumb], w=[bbb])
                yield
                dve(lambda hh: hh.tensor_tensor(out=v3(bb), in0=v3(bb), in1=bc(cum, L - 1), op=ALU.add), r=[bbb, cumb], w=[bbb])
                yield
                b, bbuf = bb, bbb
                last_off, mid_off = 0, L // 2
            if need_khat:
                dve(lambda hh: hh.tensor_tensor(out=v3(X), in0=v3(b), in1=bc(b, last_off), op=ALU.subtract), r=[bbuf], w=[Xb])
                yield
                act(lambda hh: hh.activation(out=E[:, 0:n], in_=X[:, 0:n], func=AF.Exp, scale=-1.0), r=[Xb], w=[Eb])
                yield
                dve(lambda hh: hh.tensor_tensor(out=khT_ap, in0=kk[:, 0:n], in1=E[:, 0:n], op=ALU.mult), r=[kkb, Eb], w=[khb])
                yield
                act(lambda hh: hh.activation(out=dcol_ap[:, 0:nch], in_=col(b, last_off), func=AF.Exp), r=[bbuf], w=[dcolb])
                yield
                dve(lambda hh: hh.tensor_copy(out=blc_ap[:, 0:nch], in_=col(b, last_off)), r=[bbuf], w=[dcolb])
                yield
            if not full:
                return
            dve(lambda hh: hh.tensor_tensor(out=v3(X), in0=v3(b), in1=bc(b, mid_off), op=ALU.subtract), r=[bbuf, Xb], w=[Xb])
            yield
            act(lambda hh: hh.activation(out=E[:, 0:n], in_=X[:, 0:n], func=AF.Exp, scale=1.0), r=[Xb] + ([khb] if need_khat else []), w=[Eb])
            yield
            dve(lambda hh: hh.tensor_tensor(out=qt_ap, in0=qsil[:, 0:n], in1=E[:, 0:n], op=ALU.mult), r=[qsilb, Eb], w=[qkb])
            yield
            act(lambda hh: hh.activation(out=E[:, 0:n], in_=X[:, 0:n], func=AF.Exp, scale=-1.0), r=[Xb, qkb], w=[Eb])
            yield
            dve(lambda hh: hh.tensor_tensor(out=kt_ap, in0=kk[:, 0:n], in1=E[:, 0:n], op=ALU.mult), r=[kkb, Eb], w=[qkb])
            yield
            act(lambda hh: hh.activation(out=ecol_ap[:, 0:nch], in_=col(b, mid_off), func=AF.Exp), r=[bbuf], w=[dcolb])
            yield

        def run_interleaved(gens):
            gens = list(gens)
            while gens:
                nxt = []
                for g_ in gens:
                    try:
                        next(g_)
                        nxt.append(g_)
                    except StopIteration:
                        pass
                gens = nxt

        def make_gate_ctx(lc):
            out = []
            for nm in ["sg", "lf", "kk", "cum", "bb", "X", "E"]:
                out.append(sbt(lc, "hg_" + nm, [128, 512]))
                out.append(Buf(nm))
            return tuple(out)

        cY = ExitStack()
        with ExitStack() as cA:
            slots = Slots(cA, 8, 1024)
            uT = sbt(cA, "uT", [128, DC, NA], BF16)
            ub = [Buf("u") for _ in CG]
            sbin0 = sbt(cA, "sbin0", [128, 4, 16, 128], BF16); sbin0b = [Buf("sbin0") for _ in range(4)]
            pbcol = sbt(cA, "pbcol", [128, 4, 16]); pbcolb = [Buf("pbcol") for _ in range(4)]
            s0f = sbt(cA, "s0f", [128, 4, 128]); s0fb = Buf("s0f")
            s0b = sbt(cA, "s0b", [128, 4, 128]); s0bb = Buf("s0b")
            cX = ExitStack()
            smeta = sbt(cX, "smeta", [128, 4, 128]); smetab = Buf("smeta")
            hgpay = sbt(cX, "hgpay", [128, HGW]); hgpayb = Buf("hgpay")
            ALLG = [0, 1, 2, 3, 4]
            rmsnorm_all(cA, 0, uT, ub, ALLG)
            ffn(cA, slots, wg_d[0], wu_d[0], wd_d[0], uT, ub, ALLG)
            rmsnorm_all(cA, 8, uT, ub, ALLG)

            with ExitStack() as c4:
                KT_loc = sbt(c4, "KT_loc", [128, NA], BF16); ktb = Buf("ktloc")
                V_loc = sbt(c4, "V_loc", [128, 17, 130], BF16); vlb = Buf("vloc")
                ropec, tdiv, tmod, tdb = make_rope_ctx(c4)
                qkc = make_qk_ctx(c4)
                dve(lambda h: h.memset(V_loc[:, :, 64:65], 1.0), w=[vlb])
                dve(lambda h: h.memset(V_loc[:, :, 129:130], 1.0), w=[vlb])
                tk, bk = slots.load(wfm_d[24], 1024)
                tks, bks = slots.load(wfm_d[25], 1024)
                for g in ALLG:
                    c0, n = CG[g]
                    r = g % 2
                    for kc in range(DC):
                        mm(PS[r][:, 0:n], tk[:, kc * 128:(kc + 1) * 128], uT[:, kc, c0:c0 + n], kc == 0, kc == DC - 1, [bk, ub[g]], [pb[r]])
                    for kc in range(DC):
                        mm(PS[2 + r][:, 0:n], tks[:, kc * 128:(kc + 1) * 128], uT[:, kc, c0:c0 + n], kc == 0, kc == DC - 1, [bks, ub[g]], [pb[2 + r]])
                    if g < 4:
                        rope_tables(ropec, tdiv, tmod, tdb, g, None)
                        rp = (ropec[0], ropec[1], ropec[2], ropec[3])
                    else:
                        rp = None
                    qk_post(qkc, PS[r], pb[r], PS[2 + r], pb[2 + r], n, 45, 47, rp, KT_loc[:, c0:c0 + n], [ktb])
                tv, bv = slots.load(wv_d, 1024)
                for blk in range(17):
                    nb_ = 128 if blk < 16 else NM
                    t0 = blk * 128
                    r = blk % 2
                    for kc in range(DC):
                        mm(PS[4 + r][0:nb_, 0:128], uT[:, kc, t0:t0 + nb_], tv[:, kc * 128:(kc + 1) * 128], kc == 0, kc == DC - 1,
                           [bv, ub[min(blk // 4, 4)]], [pb[4 + r]])
                    act(lambda h, r=r, blk=blk, nb_=nb_: h.copy(out=V_loc[0:nb_, blk, 0:64], in_=PS[4 + r][0:nb_, 0:64]), r=[pb[4 + r]], w=[vlb])
                    dve(lambda h, r=r, blk=blk, nb_=nb_: h.tensor_copy(out=V_loc[0:nb_, blk, 65:129], in_=PS[4 + r][0:nb_, 64:128]), r=[pb[4 + r]], w=[vlb])
                ch_k, ch_v = S.chan(), S.chan()
                ktlb, ktab, vlob, vab = Buf("ktl_d"), Buf("kta_d"), Buf("vl_d"), Buf("va_d")
                S.dma("sp", ch_k, lambda h: h.dma_start(out=kt_loc_d.ap(), in_=KT_loc[:, 0:NT]), reads=[ktb], writes=[ktlb])
                S.dma("sp", ch_v, lambda h: h.dma_start(out=v_loc_d.ap().rearrange("p (b n) -> p b n", n=130), in_=V_loc[:, 0:16, :]),
                      reads=[vlb], writes=[vlob])
                ch_cc = [S.chan() for _ in range(3)]
                S.wait_bufs("pool", slots.b)
                S.dma("pool", ch_cc[0], lambda h: h.collective_compute("AllGather", ALU.bypass, replica_groups=[list(range(NCORE))],
                                                                        ins=[kt_loc_d.ap().opt()], outs=[kt_all_d.ap().opt()]),
                      reads=[ktlb], writes=[ktab], inc=1)
                S.wait_bufs("pool", [ktab])
                S.dma("pool", ch_cc[1], lambda h: h.collective_compute("AllGather", ALU.bypass, replica_groups=[list(range(NCORE))],
                                                                        ins=[v_loc_d.ap().opt()], outs=[v_all_d.ap().opt()]),
                      reads=[vlob], writes=[vab], inc=1)
                S.wait_bufs("pool", [vab])
                dve(lambda h: h.tensor_copy(out=KT_meta[:, :], in_=KT_loc[:, NT:NA]), r=[ktb], w=[ktmb])
                dve(lambda h: h.tensor_copy(out=V_meta[0:NM, :], in_=V_loc[0:NM, 16, :]), r=[vlb], w=[vmb])
                S.barrier()

            with ExitStack() as c5:
                gatecs = [make_gate_ctx(c5), make_gate_ctx(c5)]
                v_h = sbt(c5, "v_h", [128, 17, 128], BF16); vhb = Buf("vh")
                khT = [sbt(c5, "khT%d" % d, [128, NA], BF16) for d in range(2)]; khb = [Buf("khT") for _ in range(2)]
                khat = [sbt(c5, "khat%d" % d, [128, 17, 128], BF16) for d in range(2)]; khatb = [Buf("khat") for _ in range(2)]
                dcol = [sbt(c5, "dcol%d" % d, [128, 17]) for d in range(2)]; dcolb = [Buf("dcol") for _ in range(2)]
                blc = [sbt(c5, "blc%d" % d, [128, 17]) for d in range(2)]
                Sst = [sbt(c5, "Sst%d" % d, [128, 128]) for d in range(2)]; Sstb = [Buf("Sst") for _ in range(2)]
                prun = [sbt(c5, "prun%d" % d, [128, 2]) for d in range(2)]; prunb = [Buf("prun") for _ in range(2)]
                for h in range(4):
                    tzi, bzi = slots.load(wzi_d[h], 1024)
                    tfs = [slots.load(wfm_d[4 + 4 * d + h], 1024) for d in range(2)]
                    for blk in range(17):
                        nb_ = 128 if blk < 16 else NM
                        t0 = blk * 128
                        r = blk % 2
                        for kc in range(DC):
                            mm(PS[2 + r][0:nb_, 0:128], uT[:, kc, t0:t0 + nb_], tzi[:, kc * 128:(kc + 1) * 128], kc == 0, kc == DC - 1,
                               [bzi, ub[min(blk // 4, 4)]], [pb[2 + r]])
                        if blk % 2 == 0:
                            act(lambda hh, r=r, blk=blk, nb_=nb_: hh.copy(out=v_h[0:nb_, blk, :], in_=PS[2 + r][0:nb_, 0:128]), r=[pb[2 + r]], w=[vhb])
                        else:
                            dve(lambda hh, r=r, blk=blk, nb_=nb_: hh.tensor_copy(out=v_h[0:nb_, blk, :], in_=PS[2 + r][0:nb_, 0:128]), r=[pb[2 + r]], w=[vhb])
                    for g in ALLG:
                        c0, n = CG[g]
                        L = 128 if g < 4 else NM
                        gens = []
                        for d in range(2):
                            if d == 1 and g == 4:
                                continue
                            tf, bf_ = tfs[d]
                            for kc in range(DC):
                                mm(PS[d][:, 0:n], tf[:, kc * 128:(kc + 1) * 128], uT[:, kc, c0:c0 + n], kc == 0, kc == DC - 1, [bf_, ub[g]], [pb[d]])
                            gens.append(hg_gates(gatecs[d], PS[d], pb[d], n, L, d, h, False, None, None, khT[d][:, c0:c0 + n], khb[d],
                                                 dcol[d][:, 4 * g:4 * g + n // L], dcolb[d], blc[d][:, 4 * g:4 * g + n // L]))
                        run_interleaved(gens)
                    for d in range(2):
                        nblk = 17 if d == 0 else 16
                        for b0 in range(0, nblk, 4):
                            bl = list(range(b0, min(b0 + 4, nblk)))
                            for i, blk in enumerate(bl):
                                nb_ = 128 if blk < 16 else NM
                                S.op("pe", lambda hh, i=i, blk=blk, nb_=nb_, d=d: hh.transpose(PT[0:nb_, i * 128:(i + 1) * 128], khT[d][:, blk * 128:blk * 128 + nb_], cmat[:, 0:128]),
                                     [khb[d], cmatb], [ptb], self_sync=False)
                            for i, blk in enumerate(bl):
                                nb_ = 128 if blk < 16 else NM
                                dve(lambda hh, i=i, blk=blk, nb_=nb_, d=d: hh.tensor_copy(out=khat[d][0:nb_, blk, :], in_=PT[0:nb_, i * 128:(i + 1) * 128]), r=[ptb], w=[khatb[d]])
                    mm(PS[6][:, 0:128], khat[0][0:NM, 16, :], v_h[0:NM, 16, :], True, True, [khatb[0], vhb], [pb[6]])
                    act(lambda hh, h=h: hh.copy(out=smeta[:, h, :], in_=PS[6][:, 0:128]), r=[pb[6]], w=[smetab])
                    for idx in range(16):
                        for d in range(2):
                            j = idx if d == 0 else 15 - idx
                            mm(PS[4 + d][:, 0:128], khat[d][:, j, :], v_h[:, j, :], True, True, [khatb[d], vhb], [pb[4 + d]])
                            if d == 1:
                                if idx == 0:
                                    dve(lambda hh, h=h, j=j: hh.memset(sbin0[:, h, j, :], 0.0), w=[sbin0b[h]])
                                    dve(lambda hh, h=h, j=j: hh.memset(pbcol[:, h, j:j + 1], 1.0), w=[pbcolb[h]])
                                    dve(lambda hh: hh.memset(prun[1][:, 0:1], 1.0), w=[prunb[1]])
                                else:
                                    dve(lambda hh, h=h, j=j: hh.tensor_copy(out=sbin0[:, h, j, :], in_=Sst[1][:, :]), r=[Sstb[1]], w=[sbin0b[h]])
                                    dve(lambda hh, h=h, j=j: hh.tensor_copy(out=pbcol[:, h, j:j + 1], in_=prun[1][:, 0:1]), r=[prunb[1]], w=[pbcolb[h]])
                                dve(lambda hh, j=j: hh.tensor_tensor(out=prun[1][:, 0:1], in0=prun[1][:, 0:1], in1=dcol[1][:, j:j + 1], op=ALU.mult),
                                    r=[prunb[1], dcolb[1]], w=[prunb[1]])
                            if idx == 0:
                                act(lambda hh, d=d: hh.copy(out=Sst[d][:, :], in_=PS[4 + d][:, 0:128]), r=[pb[4 + d]], w=[Sstb[d]])
                            else:
                                dve(lambda hh, d=d, j=j: hh.scalar_tensor_tensor(out=Sst[d][:, :], in0=Sst[d][:, :], scalar=dcol[d][:, j:j + 1], in1=PS[4 + d][:, 0:128],
                                                                                 op0=ALU.mult, op1=ALU.add), r=[Sstb[d], dcolb[d], pb[4 + d]], w=[Sstb[d]])
                    for d in range(2):
                        off = (0 if d == 0 else 512) + h * 128
                        dve(lambda hh, off=off, d=d: hh.tensor_copy(out=hgpay[:, off:off + 128], in_=Sst[d][:, :]), r=[Sstb[d]], w=[hgpayb])
                        dve(lambda hh, d=d: hh.tensor_reduce(out=prun[d][:, 1:2], in_=blc[d][:, 0:16], op=ALU.add, axis=mybir.AxisListType.X), r=[dcolb[d]], w=[prunb[d]])
                        ac = 1024 + 4 * d + h
                        act(lambda hh, ac=ac, d=d: hh.activation(out=hgpay[:, ac:ac + 1], in_=prun[d][:, 1:2], func=AF.Exp), r=[prunb[d]], w=[hgpayb])
                S.barrier()
            ch_h = S.chan()
            hglb, hgab = Buf("hgl"), Buf("hga")
            S.dma("sp", ch_h, lambda h: h.dma_start(out=hg_loc_d.ap(), in_=hgpay[:, :]), reads=[hgpayb], writes=[hglb])
            S.wait_bufs("pool", slots.b)
            S.dma("pool", ch_cc[2], lambda h: h.collective_compute("AllGather", ALU.bypass, replica_groups=[list(range(NCORE))],
                                                                    ins=[hg_loc_d.ap().opt()], outs=[hg_all_d.ap().opt()]),
                  reads=[hglb], writes=[hgab], inc=1)
            S.wait_bufs("pool", [hgab])
            with ExitStack() as c6:
                stg = [sbt(c6, "hgstg%d" % i, [128, HGW]) for i in range(2)]
                stgb = [Buf("stg") for _ in range(2)]
                stch = [S.chan() for _ in range(2)]
                aeff = sbt(c6, "aeff", [128, 8]); aeffb = Buf("aeff")
                tmpB = sbt(c6, "tmpB", [128, 128]); tmpBb = Buf("tmpB")
                dve(lambda hh: hh.tensor_copy(out=s0f[:, :, :], in_=smeta[:, :, :]), r=[smetab], w=[s0fb])
                dve(lambda hh: hh.memset(s0b[:, :, :], 0.0), w=[s0bb])
                hg_v = hg_all_d.ap().rearrange("(r p) n -> r p n", p=128)
                k = 0
                for (d, order) in ((0, list(range(NCORE))), (1, list(range(NCORE - 1, -1, -1)))):
                    st, stb_ = (s0f, s0fb) if d == 0 else (s0b, s0bb)
                    for j in order:
                        q = k % 2
                        k += 1
                        S.dma("sp", stch[q], lambda h, q=q, j=j: h.dma_start(out=stg[q][:, :], in_=hg_v[j]), reads=[hgab], writes=[stgb[q]])
                        mcol = (1 + j) if d == 0 else (9 + j)
                        dve(lambda hh, q=q, d=d, mcol=mcol: hh.tensor_scalar(out=aeff[:, 0:4], in0=stg[q][:, 1024 + 4 * d:1028 + 4 * d], scalar1=-1.0,
                                                                            scalar2=corec[:, mcol:mcol + 1], op0=ALU.add, op1=ALU.mult),
                            r=[stgb[q], coreb], w=[aeffb])
                        dve(lambda hh: hh.tensor_scalar(out=aeff[:, 0:4], in0=aeff[:, 0:4], scalar1=1.0, scalar2=None, op0=ALU.add), r=[aeffb], w=[aeffb])
                        for h in range(4):
                            off = 512 * d + h * 128
                            dve(lambda hh, q=q, off=off, mcol=mcol: hh.tensor_scalar(out=tmpB[:, :], in0=stg[q][:, off:off + 128],
                                                                                    scalar1=corec[:, mcol:mcol + 1], scalar2=None, op0=ALU.mult),
                                r=[stgb[q], coreb], w=[tmpBb])
                            dve(lambda hh, h=h, st=st: hh.scalar_tensor_tensor(out=st[:, h, :], in0=st[:, h, :], scalar=aeff[:, h:h + 1], in1=tmpB[:, :],
                                                                               op0=ALU.mult, op1=ALU.add), r=[aeffb, tmpBb, stb_], w=[stb_])
                S.barrier()
            cX.close()

            yaT = sbr(cY, "yaT", [128, 4, NT], BF16)
            yab = [[Buf("ya") for _ in range(4)] for _ in range(4)]
            with ExitStack() as c7:
                gsets = [make_gate_ctx(c7), make_gate_ctx(c7)]
                qsil = sbt(c7, "qsil", [128, 512]); qsilb = Buf("qsil")
                zgs = sbt(c7, "zgs", [128, 512]); zgsb = Buf("zgs")
                v_h = sbt(c7, "v_h2", [128, 4, 128], BF16); vhb = Buf("vh")
                khT = sbt(c7, "khT2", [128, 512], BF16); khb = Buf("khT")
                khat = sbt(c7, "khat2", [128, 4, 128], BF16); khatb = Buf("khat")
                qtl = [sbt(c7, "qtl%d" % i, [128, 512], BF16) for i in range(2)]
                ktl = [sbt(c7, "ktl%d" % i, [128, 512], BF16) for i in range(2)]
                qkb = [Buf("qk") for _ in range(2)]
                dcol = sbt(c7, "dcol2", [128, 2, 4]); dcolb = [Buf("dcol") for _ in range(2)]
                ecol = sbt(c7, "ecol2", [128, 2, 4])
                blc = sbt(c7, "blc2", [128, 8])
                AT = [sbt(c7, "AT%d" % i, [128, 512], BF16) for i in range(2)]
                ATb = [Buf("AT") for _ in range(2)]
                Sf = sbt(c7, "Sf", [128, 128]); Sfb = Buf("Sf")
                Sp0 = [sbt(c7, "Sp0_%d" % i, [128, 128], BF16) for i in range(2)]
                Sp0b = [Buf("Sp0") for _ in range(2)]
                stmp4 = sbt(c7, "stmp4", [128, 4, 128]); stmp4b = Buf("stmp4")
                Sp1 = sbt(c7, "Sp1", [128, 4, 128], BF16); Sp1b = Buf("Sp1")
                osq = sbt(c7, "osq", [128, 512], BF16); osqb = Buf("osq")
                for h in range(4):
                    tq, bq = slots.load(wfm_d[0 + h], 1024)
                    tff, bff = slots.load(wfm_d[4 + h], 1024)
                    tfb, bfb = slots.load(wfm_d[8 + h], 1024)
                    tg_, bg_ = slots.load(wfm_d[12 + h], 1024)
                    tzi, bzi = slots.load(wzi_d[h], 1024)
                    dve(lambda hh, h=h: hh.tensor_copy(out=Sf[:, :], in_=s0f[:, h, :]), r=[s0fb], w=[Sfb])
                    for g in range(4):
                        c0, n = CG[g]
                        ot1, ot1b, ors, orsb, oy, oyb = gsets[0][0], gsets[0][1], gsets[0][2], gsets[0][3], gsets[0][4], gsets[0][5]
                        for (pi, tw, bw) in ((0, tq, bq), (1, tff, bff), (2, tfb, bfb), (3, tg_, bg_)):
                            for kc in range(DC):
                                mm(PS[pi][:, 0:n], tw[:, kc * 128:(kc + 1) * 128], uT[:, kc, c0:c0 + n], kc == 0, kc == DC - 1, [bw, ub[g]], [pb[pi]])
                        for bi in range(4):
                            t0 = c0 + bi * 128
                            for kc in range(DC):
                                mm(PS[4][:, bi * 128:(bi + 1) * 128], uT[:, kc, t0:t0 + 128], tzi[:, kc * 128:(kc + 1) * 128], kc == 0, kc == DC - 1,
                                   [bzi, ub[g]], [pb[4]])
                        act(lambda hh: hh.activation(out=qsil[:, :], in_=PS[0][:, :], func=AF.Sigmoid), r=[pb[0]], w=[qsilb])
                        act(lambda hh: hh.activation(out=zgs[:, :], in_=PS[3][:, :], func=AF.Sigmoid), r=[pb[3]], w=[zgsb])
                        dve(lambda hh: hh.tensor_tensor(out=qsil[:, :], in0=PS[0][:, :], in1=qsil[:, :], op=ALU.mult), r=[pb[0], qsilb], w=[qsilb])
                        dve(lambda hh: hh.tensor_tensor(out=zgs[:, :], in0=PS[3][:, :], in1=zgs[:, :], op=ALU.mult), r=[pb[3], zgsb], w=[zgsb])
                        dve(lambda hh: hh.tensor_copy(out=v_h[:, :, :], in_=PS[4][:, :].rearrange("p (b n) -> p b n", n=128)), r=[pb[4]], w=[vhb])
                        run_interleaved([
                            hg_gates(gsets[d], PS[1 + d], pb[1 + d], n, 128, d, h, True, qsil, qsilb, khT[:, :], khb,
                                     dcol[:, d, :], dcolb[d], blc[:, 4 * d:4 * d + 4], qtl[d][:, :], ktl[d][:, :], qkb[d], ecol[:, d, :],
                                     need_khat=(d == 0))
                            for d in range(2)])
                        for i in range(4):
                            S.op("pe", lambda hh, i=i: hh.transpose(PT[:, i * 128:(i + 1) * 128], khT[:, i * 128:(i + 1) * 128], cmat[:, 0:128]),
                                 [khb, cmatb], [ptb], self_sync=False)
                        dve(lambda hh: hh.tensor_copy(out=khat[:, :, :], in_=PT[:, 0:512].rearrange("p (b n) -> p b n", n=128)), r=[ptb], w=[khatb])
                        for d in range(2):
                            for i in range(4):
                                mm(PS[d][:, i * 128:(i + 1) * 128], ktl[d][:, i * 128:(i + 1) * 128], qtl[d][:, i * 128:(i + 1) * 128], True, True,
                                   [qkb[d]], [pb[d]])
                            dve(lambda hh, d=d: hh.tensor_tensor(out=AT[d][:, :], in0=PS[d][:, :], in1=mask4[:, d, :], op=ALU.mult),
                                r=[pb[d], maskb], w=[ATb[d]])
                        j0 = g * 4
                        dve(lambda hh, h=h, j0=j0: hh.tensor_tensor(out=stmp4[:, :, :], in0=s0b[:, h, :].unsqueeze(1).to_broadcast([128, 4, 128]),
                                                                    in1=pbcol[:, h, j0:j0 + 4].unsqueeze(2).to_broadcast([128, 4, 128]), op=ALU.mult),
                            r=[s0bb, pbcolb[h]], w=[stmp4b])
                        dve(lambda hh, h=h, j0=j0: hh.tensor_tensor(out=stmp4[:, :, :], in0=stmp4[:, :, :], in1=sbin0[:, h, j0:j0 + 4, :], op=ALU.add),
                            r=[stmp4b, sbin0b[h]], w=[stmp4b])
                        dve(lambda hh: hh.tensor_tensor(out=Sp1[:, :, :], in0=stmp4[:, :, :], in1=ecol[:, 1, :].unsqueeze(2).to_broadcast([128, 4, 128]), op=ALU.mult),
                            r=[stmp4b, dcolb[1]], w=[Sp1b])
                        for i in range(4):
                            q_ = i % 2
                            dve(lambda hh, i=i, q_=q_: hh.tensor_scalar(out=Sp0[q_][:, :], in0=Sf[:, :], scalar1=ecol[:, 0, i:i + 1], scalar2=None, op0=ALU.mult),
                                r=[Sfb, dcolb[0]], w=[Sp0b[q_]])
                            osl = PS[2][:, i * 128:(i + 1) * 128]
                            mm(osl, v_h[:, i, :], AT[0][:, i * 128:(i + 1) * 128], True, False, [vhb, ATb[0]], [pb[2]])
                            mm(osl, v_h[:, i, :], AT[1][:, i * 128:(i + 1) * 128], False, False, [vhb, ATb[1]], [pb[2]])
                            mm(osl, Sp1[:, i, :], qtl[1][:, i * 128:(i + 1) * 128], False, False, [Sp1b, qkb[1]], [pb[2]])
                            mm(osl, Sp0[q_][:, :], qtl[0][:, i * 128:(i + 1) * 128], False, True, [Sp0b[q_], qkb[0]], [pb[2]])
                            mm(PS[5][:, i * 128:(i + 1) * 128], khat[:, i, :], v_h[:, i, :], True, True, [khatb, vhb], [pb[5]])
                            dve(lambda hh, i=i: hh.scalar_tensor_tensor(out=Sf[:, :], in0=Sf[:, :], scalar=dcol[:, 0, i:i + 1], in1=PS[5][:, i * 128:(i + 1) * 128],
                                                                        op0=ALU.mult, op1=ALU.add), r=[Sfb, dcolb[0], pb[5]], w=[Sfb])
                        act(lambda hh: hh.activation(out=osq[:, :], in_=PS[2][:, :], func=AF.Square), r=[pb[2]], w=[osqb])
                        mm(PS[3][:, :], ones_bf[:, :], osq[:, :], True, True, [onesb, osqb], [pb[3]])
                        act(lambda hh: hh.activation(out=ot1[:, :], in_=PS[3][:, :], func=AF.Ln, scale=1.0 / 128, bias=EPS), r=[pb[3]], w=[ot1b])
                        act(lambda hh: hh.activation(out=ors[:, :], in_=ot1[:, :], func=AF.Exp, scale=-0.5), r=[ot1b], w=[orsb])
                        dve(lambda hh, h=h: hh.scalar_tensor_tensor(out=oy[:, :], in0=PS[2][:, :], scalar=vecs[:, 24 + h:25 + h], in1=ors[:, :],
                                                                    op0=ALU.mult, op1=ALU.mult), r=[pb[2], orsb, vecb], w=[oyb])
                        dve(lambda hh, h=h, c0=c0: hh.tensor_tensor(out=yaT[:, h, c0:c0 + 512], in0=oy[:, :], in1=zgs[:, :], op=ALU.mult),
                            r=[oyb, zgsb], w=[yab[h][g]])
                S.barrier()
            QT = sbr(cY, "QT", [128, 4, NT], BF16)
            qtb = [[[Buf("qt") for _ in range(4)] for _ in range(4)] for _ in range(4)]
            with ExitStack() as c8:
                ropec, tdiv, tmod, tdb = make_rope_ctx(c8)
                qkc = make_qk_ctx(c8)
                wq = [slots.load(wfm_d[16 + c], 1024) for c in range(4)]
                wqs = [slots.load(wfm_d[20 + c], 1024) for c in range(4)]
                for g in range(4):
                    c0, n = CG[g]
                    rope_tables(ropec, tdiv, tmod, tdb, g, None)
                    rp = (ropec[0], ropec[1], ropec[2], ropec[3])
                    for c in range(4):
                        r = c % 2
                        for kc in range(DC):
                            mm(PS[r][:, 0:n], wq[c][0][:, kc * 128:(kc + 1) * 128], uT[:, kc, c0:c0 + n], kc == 0, kc == DC - 1, [wq[c][1], ub[g]], [pb[r]])
                        for kc in range(DC):
                            mm(PS[2 + r][:, 0:n], wqs[c][0][:, kc * 128:(kc + 1) * 128], uT[:, kc, c0:c0 + n], kc == 0, kc == DC - 1, [wqs[c][1], ub[g]], [pb[2 + r]])
                        qk_post(qkc, PS[r], pb[r], PS[2 + r], pb[2 + r], n, 44, 46, rp, QT[:, c, c0:c0 + n], qtb[c][g])
                S.barrier()
        S.barrier()

        with ExitStack() as cB:
            KT_all = sbt(cB, "KT_all", [128, NCORE * NT + NM], BF16); ktallb = Buf("ktall")
            V_all = sbt(cB, "V_all", [128, NCORE * 16 + 1, 130], BF16); vallb = Buf("vall")
            PTs = [sbt(cB, "PTs%d" % i, [128, 1024], BF16) for i in range(2)]
            PTb = [Buf("pts") for _ in range(2)]
            rinv = sbt(cB, "rinv", [128, 512]); rinvb = Buf("rinv")
            rbc = sbt(cB, "rbc", [128, 512]); rbcb = Buf("rbc")
            ch_l = [S.chan() for _ in range(2)]
            S.dma("sp", ch_l[0], lambda h: h.dma_start(out=KT_all[:, 0:NCORE * NT].rearrange("p (r n) -> p r n", n=NT),
                                                        in_=kt_all_d.ap().rearrange("(r p) n -> p r n", p=128)), reads=[ktab], writes=[ktallb])
            S.dma("sp", ch_l[1], lambda h: h.dma_start(out=V_all[:, 0:NCORE * 16, :].rearrange("p (r b) n -> p r (b n)", b=16),
                                                        in_=v_all_d.ap().rearrange("(r p) n -> p r n", p=128)), reads=[vab], writes=[vallb])
            dve(lambda h: h.tensor_copy(out=KT_all[:, NCORE * NT:NCORE * NT + NM], in_=KT_meta[:, :]), r=[ktmb], w=[ktallb])
            dve(lambda h: h.tensor_copy(out=V_all[0:NM, NCORE * 16, :], in_=V_meta[0:NM, :]), r=[vmb], w=[vallb])
            NKT = NCORE * 16 + 1
            NP = (NKT + 1) // 2
            grp = 0
            pending = [None]
            QZ = [[sbt(cB, "QZ%d_%d" % (k, i), [128, 512], BF16) for i in range(2)] for k in range(2)]
            QZb = [[Buf("qz") for _ in range(2)] for _ in range(2)]
            for k in range(2):
                for i in range(2):
                    pool(lambda hh, k=k, i=i: hh.memset(QZ[k][i][:, :], 0.0), w=[QZb[k][i]])
            qzi = [0, 0]
            pglob = 0
            for g in range(4):
                c0 = CG[g][0]
                for kvh in range(2):
                    pbase = kvh * 64
                    for half in range(2):
                        t0 = c0 + half * 256
                        for cp in range(2):
                            ob = 4 + grp % 2
                            grp += 1
                            Op = PS[ob]
                            qsl = QT[pbase:pbase + 64, 2 * cp:2 * cp + 2, t0:t0 + 256]
                            qbufs = [qtb[2 * cp][g][kvh * 2 + half], qtb[2 * cp + 1][g][kvh * 2 + half]]
                            zi = qzi[kvh] % 2
                            qzi[kvh] += 1
                            qz, qzb_ = QZ[kvh][zi], QZb[kvh][zi]
                            pool(lambda hh, qz=qz, qsl=qsl, pbase=pbase: hh.tensor_copy(out=qz[pbase:pbase + 64, :].rearrange("p (c n) -> p c n", n=256), in_=qsl),
                                 r=qbufs, w=[qzb_])
                            pg0 = pglob
                            pglob += NP

                            def qk(pp, qz=qz, qzb_=qzb_, pg0=pg0):
                                sp = (pg0 + pp) % 2
                                for hf in range(2):
                                    kt = 2 * pp + hf
                                    if kt >= NKT:
                                        continue
                                    nk = 128 if kt < NKT - 1 else NM
                                    S.op("pe", lambda hh, kt=kt, nk=nk, hf=hf: hh.matmul(PSbig[0:nk, sp * 1024 + hf * 512:sp * 1024 + hf * 512 + 512],
                                                                                       KT_all[:, kt * 128:kt * 128 + nk], qz[:, :], start=True, stop=True),
                                         [ktallb, qzb_], [pb[2 * sp + hf]], self_sync=False)

                            qk(0)
                            for pp in range(NP):
                                sp = (pg0 + pp) % 2
                                full = (2 * pp + 1 < NKT - 1) and WIDE_EXP
                                if full:
                                    act(lambda hh, sp=sp: hh.activation(out=PTs[sp][:, 0:1024], in_=PSbig[:, sp * 1024:sp * 1024 + 1024], func=AF.Exp, scale=0.125),
                                        r=[pb[2 * sp], pb[2 * sp + 1]], w=[PTb[sp]])
                                else:
                                    for hf in range(2):
                                        kt = 2 * pp + hf
                                        if kt >= NKT:
                                            continue
                                        nk = 128 if kt < NKT - 1 else NM
                                        act(lambda hh, sp=sp, hf=hf, nk=nk: hh.activation(out=PTs[sp][0:nk, hf * 512:hf * 512 + 512],
                                                                                        in_=PSbig[0:nk, sp * 1024 + hf * 512:sp * 1024 + hf * 512 + 512], func=AF.Exp, scale=0.125),
                                            r=[pb[2 * sp + hf]], w=[PTb[sp]])
                                if pp + 1 < NP:
                                    qk(pp + 1)
                                for hf in range(2):
                                    kt = 2 * pp + hf
                                    if kt >= NKT:
                                        continue
                                    nk = 128 if kt < NKT - 1 else NM
                                    S.op("pe", lambda hh, Op=Op, kt=kt, nk=nk, sp=sp, hf=hf, kvh=kvh: hh.matmul(Op[0:65, :], V_all[0:nk, kt, 65 * kvh:65 * kvh + 65],
                                                                                                            PTs[sp][0:nk, hf * 512:hf * 512 + 512],
                                                                                                            start=(kt == 0), stop=(kt == NKT - 1)),
                                         [vallb, PTb[sp]], [pb[ob]], self_sync=False)
                                if pp == 6 and pending[0] is not None:
                                    pending[0]()
                                    pending[0] = None

                            def finalize(Op=Op, ob=ob, qsl=qsl, qbufs=qbufs):
                                dve(lambda hh: hh.reciprocal(out=rinv[64:65, :], in_=Op[64:65, :]), r=[pb[ob]], w=[rinvb])
                                mm(PS[6][0:64, :], ones_f[64:65, 0:64], rinv[64:65, :], True, True, [onesfb, rinvb], [pb[6]])
                                act(lambda hh: hh.copy(out=rbc[0:64, :], in_=PS[6][0:64, :]), r=[pb[6]], w=[rbcb])
                                dve(lambda hh: hh.tensor_tensor(out=qsl, in0=Op[0:64, :].rearrange("p (c n) -> p c n", n=256),
                                                                in1=rbc[0:64, :].rearrange("p (c n) -> p c n", n=256), op=ALU.mult),
                                    r=[pb[ob], rbcb], w=qbufs)
                            pending[0] = finalize
            pending[0]()
            S.barrier()
        S.barrier()

        with ExitStack() as cC:
            slots = Slots(cC, 8, 1024)
            for half in range(2):
                HG = [2 * half, 2 * half + 1]
                hoff = half * 1024
                with ExitStack() as c9:
                    uT = sbt(c9, "uTm", [128, DC, 1024], BF16)
                    ub = [Buf("u") for _ in CG]
                    mixT = sbt(c9, "mixT", [128, DC, 1024], BF16)
                    mixb = [[Buf("mix") for _ in range(4)] for _ in range(DC)]
                    sga = [sbt(c9, "sga%d" % i, [128, 512]) for i in range(2)]; sgab = [Buf("sga") for _ in range(2)]
                    sgb2 = [sbt(c9, "sgb%d" % i, [128, 512]) for i in range(2)]; sgbb = [Buf("sgb") for _ in range(2)]
                    m1 = [sbt(c9, "m1%d" % i, [128, 512]) for i in range(2)]; m1b = [Buf("m1") for _ in range(2)]
                    m2 = [sbt(c9, "m2%d" % i, [128, 512]) for i in range(2)]; m2b = [Buf("m2") for _ in range(2)]
                    rmsnorm_all(c9, 8, uT, ub, HG, hoff)
                    rot = 0
                    for s in range(4):
                        tua, bua = slots.load(wua_d[s], 1024)
                        tub, bub = slots.load(wub_d[s], 1024)
                        for oo in range(2):
                            o = 2 * s + oo
                            tga, bga = slots.load(wfm_d[26 + o], 1024)
                            tgb, bgb = slots.load(wfm_d[34 + o], 1024)
                            for g in HG:
                                c0, n = CG[g]
                                r = rot % 2
                                rot += 1
                                za, zb, pa, pbk = 0, 1, 2 + r, 4 + r
                                for kc in range(DC):
                                    mm(PS[za][:, :], tga[:, kc * 128:(kc + 1) * 128], uT[:, kc, c0 - hoff:c0 - hoff + n], kc == 0, kc == DC - 1, [bga, ub[g]], [pb[za]])
                                for kc in range(DC):
                                    mm(PS[zb][:, :], tgb[:, kc * 128:(kc + 1) * 128], uT[:, kc, c0 - hoff:c0 - hoff + n], kc == 0, kc == DC - 1, [bgb, ub[g]], [pb[zb]])
                                for kc in range(4):
                                    mm(PS[pa][:, :], tua[:, (oo * 4 + kc) * 128:(oo * 4 + kc + 1) * 128], yaT[:, kc, c0:c0 + n], kc == 0, kc == 3,
                                       [bua, yab[kc][g]], [pb[pa]])
                                for kc in range(4):
                                    mm(PS[pbk][:, :], tub[:, (oo * 4 + kc) * 128:(oo * 4 + kc + 1) * 128], QT[:, kc, c0:c0 + n], kc == 0, kc == 3,
                                       [bub] + qtb[kc][g], [pb[pbk]])
                                act(lambda hh, r=r, za=za: hh.activation(out=sga[r][:, :], in_=PS[za][:, :], func=AF.Sigmoid), r=[pb[za]], w=[sgab[r]])
                                act(lambda hh, r=r, zb=zb: hh.activation(out=sgb2[r][:, :], in_=PS[zb][:, :], func=AF.Sigmoid), r=[pb[zb]], w=[sgbb[r]])
                                dve(lambda hh, r=r, pa=pa: hh.tensor_tensor(out=m1[r][:, :], in0=sga[r][:, :], in1=PS[pa][:, :], op=ALU.mult),
                                    r=[sgab[r], pb[pa]], w=[m1b[r]])
                                dve(lambda hh, r=r, pbk=pbk: hh.tensor_tensor(out=m2[r][:, :], in0=sgb2[r][:, :], in1=PS[pbk][:, :], op=ALU.mult),
                                    r=[sgbb[r], pb[pbk]], w=[m2b[r]])
                                dve(lambda hh, r=r, o=o, c0=c0: hh.tensor_tensor(out=mixT[:, o, c0 - hoff:c0 - hoff + 512], in0=m1[r][:, :], in1=m2[r][:, :], op=ALU.add),
                                    r=[m1b[r], m2b[r]], w=[mixb[o][g]])
                    rot = 0
                    for o in range(DC):
                        two, bwo = slots.load(wo_d[o], 1024)
                        for g in HG:
                            c0, n = CG[g]
                            r = rot % 2
                            rot += 1
                            for kc in range(DC):
                                mm(PS[4 + r][:, :], two[:, kc * 128:(kc + 1) * 128], mixT[:, kc, c0 - hoff:c0 - hoff + n], kc == 0, kc == DC - 1,
                                   [bwo, mixb[kc][g]], [pb[4 + r]])
                            dve(lambda hh, r=r, o=o, c0=c0: hh.tensor_tensor(out=hT[:, o, c0:c0 + 512], in0=PS[4 + r][:, :], in1=hT[:, o, c0:c0 + 512], op=ALU.add),
                                r=[pb[4 + r], hb[o][g]], w=[hb[o][g]])
                    S.barrier()
            cY.close()
            with ExitStack() as c10:
                uT = sbt(c10, "uT2", [128, DC, NA], BF16)
                ub = [Buf("u") for _ in CG]
                rmsnorm_all(c10, 16, uT, ub, [0, 1, 2, 3])
                ffn(c10, slots, wg_d[1], wu_d[1], wd_d[1], uT, ub, [0, 1, 2, 3])
            outT_v = outT_d.rearrange("(c p) t -> p c t", p=128)
            obufs = []
            for g in range(4):
                c0, n = CG[g]
                cho = S.chan()
                ob_ = Buf("out")
                S.dma("sp", cho, lambda h, c0=c0, n=n: h.dma_start(out=outT_v[:, :, c0:c0 + n], in_=hT[:, :, c0:c0 + n]),
                      reads=[hb[c][g] for c in range(DC)], writes=[ob_])
                obufs.append(ob_)
            S.wait_bufs("sp", obufs)
            S.barrier()
    return nc


def _fm_layout(w_cols):
    return np.ascontiguousarray(w_cols.reshape(8, 128, 128).transpose(1, 0, 2).reshape(128, 1024))


def _host_inputs(x, meta_tokens, ffn1_norm, ffn1_w_gate, ffn1_w_up, ffn1_w_down, mix_norm, w_in, hg_lb_fwd, hg_lb_bwd,
                 hg_out_norm, q_norm, k_norm, w_up_a, w_up_b, w_out, ffn2_norm, ffn2_w_gate, ffn2_w_up, ffn2_w_down):
    f32 = lambda a: np.asarray(a, dtype=np.float32)
    shared = {}
    p = np.arange(128)

    def ffn_pack(idx, wg, wu, wd):
        wg, wu, wd = f32(wg)[0], f32(wu)[0], f32(wd)[0]
        shared["wg%d" % idx] = np.stack([_fm_layout(wg[:, j * 128:(j + 1) * 128]) for j in range(JC)])
        shared["wu%d" % idx] = np.stack([_fm_layout(wu[:, j * 128:(j + 1) * 128]) for j in range(JC)])
        shared["wd%d" % idx] = np.ascontiguousarray(wd.reshape(JC, 128, DC, 128).transpose(2, 1, 0, 3).reshape(DC, 128, DFF))

    ffn_pack(1, ffn1_w_gate, ffn1_w_up, ffn1_w_down)
    ffn_pack(2, ffn2_w_gate, ffn2_w_up, ffn2_w_down)
    W = f32(w_in)[0]
    cols = []
    for base in (0, 1024, 1536, 2048):
        for h in range(4):
            cols.append(base + h * 128 + p)
    dd = np.arange(64)
    for swap in (0, 1):
        for c in range(4):
            d_ = dd ^ 1 if swap else dd
            cols.append(np.concatenate([2560 + c * 64 + d_, 2560 + (4 + c) * 64 + d_]))
    for swap in (0, 1):
        d_ = dd ^ 1 if swap else dd
        cols.append(np.concatenate([3072 + d_, 3072 + 64 + d_]))
    for base in (3328, 4352):
        for o in range(8):
            cols.append(base + o * 128 + p)
    assert len(cols) == 42
    shared["wfm"] = np.stack([_fm_layout(W[:, c]) for c in cols])
    shared["wzi"] = np.stack([_fm_layout(W[:, 512 + h * 128:512 + (h + 1) * 128]) for h in range(4)])
    shared["wv"] = _fm_layout(W[:, 3200:3328])

    def up_pack(wu_, rowperm):
        wu_ = f32(wu_)[0][rowperm]
        out = np.zeros((4, 128, 1024), np.float32)
        for s in range(4):
            for oo in range(2):
                o = 2 * s + oo
                blk = wu_[:, o * 128:(o + 1) * 128].reshape(4, 128, 128).transpose(1, 0, 2).reshape(128, 512)
                out[s, :, oo * 512:(oo + 1) * 512] = blk
        return out

    shared["wua"] = up_pack(w_up_a, np.arange(512))
    permb = np.concatenate([np.concatenate([c * 64 + dd, (4 + c) * 64 + dd]) for c in range(4)])
    shared["wub"] = up_pack(w_up_b, permb)
    Wo = f32(w_out)[0]
    shared["wo"] = np.stack([_fm_layout(Wo[:, o * 128:(o + 1) * 128]) for o in range(DC)])
    vecs = np.zeros((128, NV), np.float32)
    vecs[:, 0:8] = f32(ffn1_norm)[0].reshape(8, 128).T
    vecs[:, 8:16] = f32(mix_norm)[0].reshape(8, 128).T
    vecs[:, 16:24] = f32(ffn2_norm)[0].reshape(8, 128).T
    vecs[:, 24:28] = f32(hg_out_norm)[0].reshape(4, 128).T
    vecs[:, 28:32] = f32(hg_lb_fwd)[0].reshape(4, 128).T
    vecs[:, 32:36] = f32(hg_lb_fwd)[1].reshape(4, 128).T
    vecs[:, 36:40] = f32(hg_lb_bwd)[0].reshape(4, 128).T
    vecs[:, 40:44] = f32(hg_lb_bwd)[1].reshape(4, 128).T
    qn, kn = f32(q_norm)[0], f32(k_norm)[0]
    vecs[:, 44] = qn[p % 64]
    vecs[:, 45] = kn[p % 64]
    vecs[:, 46] = qn[(p % 64) ^ 1]
    vecs[:, 47] = kn[(p % 64) ^ 1]
    pi = (p % 64) // 2
    vecs[:, 48] = (pi % 16).astype(np.float32)
    vecs[:, 49] = (pi < 16).astype(np.float32)
    vecs[:, 50] = (pi >= 16).astype(np.float32)
    vecs[:, 51] = np.where(p % 2 == 0, -1.0, 1.0)
    shared["vecs"] = vecs
    cm = np.zeros((128, 384), np.float32)
    cm[:, 0:128] = np.eye(128, dtype=np.float32)
    cm[:, 128:256] = np.triu(np.ones((128, 128), np.float32))
    cm[:, 256:384] = np.tril(np.ones((128, 128), np.float32))
    shared["cmat"] = cm
    shared["metaT"] = np.ascontiguousarray(f32(meta_tokens).T)
    xs = f32(x)[0]
    in_maps = []
    for r in range(NCORE):
        m = dict(shared)
        m["xT"] = np.ascontiguousarray(xs[r * NT:(r + 1) * NT].T)
        cc = np.zeros((128, 17), np.float32)
        cc[:, 0] = r * (NT // 64)
        for j in range(NCORE):
            cc[:, 1 + j] = 1.0 if j < r else 0.0
            cc[:, 9 + j] = 1.0 if j > r else 0.0
        m["corec"] = cc
        in_maps.append(m)
    return in_maps


_NC_CACHE = {}


def kernel(**inputs):
    in_maps = _host_inputs(**inputs)
    if "nc" not in _NC_CACHE:
        _NC_CACHE["nc"] = build_program()
    res = run_bass_kernel_spmd(_NC_CACHE["nc"], in_maps, core_ids=list(range(NCORE)))
    outs = [np.asarray(res.results[r]["outT"]).T for r in range(NCORE)]
    return np.ascontiguousarray(np.concatenate(outs, axis=0)[None].astype(np.float32))
```

```python
import numpy as np
from contextlib import ExitStack
import concourse.bass as bass
import concourse.mybir as mybir
from concourse.bass_utils import run_bass_kernel_spmd

F32 = mybir.dt.float32
BF16 = mybir.dt.bfloat16
I32 = mybir.dt.int32
AF = mybir.ActivationFunctionType
ALU = mybir.AluOpType

NCORE = 8
D = 1024
DC = 8
NT = 2048
NM = 16
NA = NT + NM
DFF = 2816
JC = 22
EPS = 1e-6
CG = [(0, 512), (512, 512), (1024, 512), (1536, 512), (2048, 16)]
NV = 52
WIDE_EXP = False
TWO_PI = 6.28318
HGW = 1032


class Buf:
    __slots__ = ("name", "w", "r")

    def __init__(self, name=""):
        self.name = name
        self.w = None
        self.r = {}


class Sched:
    ENG = ("pe", "act", "dve", "pool", "sp")

    def __init__(self, nc, ctx):
        self.nc = nc
        self.ctx = ctx
        self.h = {"pe": nc.tensor, "act": nc.scalar, "dve": nc.vector, "pool": nc.gpsimd, "sp": nc.sync}
        self.sems = {}
        self.count = {}
        self.waited = {e: {} for e in self.ENG}
        for e in self.ENG:
            self.sems[e] = ctx.enter_context(nc.semaphore("sem_" + e))
            self.count[e] = 0
        self.nchan = 0
        self.ninstr = 0

    def chan(self):
        key = "ch%d" % self.nchan
        self.nchan += 1
        self.sems[key] = self.ctx.enter_context(self.nc.semaphore("sem_" + key))
        self.count[key] = 0
        return key

    def _waits(self, eng, reads, writes, self_sync):
        need = {}
        for b in reads:
            if b.w is not None:
                k, v = b.w
                if need.get(k, 0) < v:
                    need[k] = v
        for b in writes:
            if b.w is not None:
                k, v = b.w
                if need.get(k, 0) < v:
                    need[k] = v
            for k, v in b.r.items():
                if need.get(k, 0) < v:
                    need[k] = v
        wd = self.waited[eng]
        hnd = self.h[eng]
        for k, v in need.items():
            if (not self_sync) and k == eng:
                continue
            if wd.get(k, 0) < v:
                wd[k] = v
                hnd.wait_ge(self.sems[k], v)

    def _mark(self, key, val, reads, writes):
        for b in reads:
            if b.r.get(key, 0) < val:
                b.r[key] = val
        for b in writes:
            b.w = (key, val)
            b.r = {}

    def op(self, eng, fn, reads=(), writes=(), self_sync=True):
        self._waits(eng, reads, writes, self_sync)
        self.count[eng] += 1
        fn(self.h[eng]).then_inc(self.sems[eng], 1)
        self._mark(eng, self.count[eng], reads, writes)
        self.ninstr += 1

    def dma(self, q, ch, fn, reads=(), writes=(), inc=16):
        self._waits(q, reads, writes, True)
        self.count[ch] += inc
        fn(self.h[q]).then_inc(self.sems[ch], inc)
        self._mark(ch, self.count[ch], reads, writes)
        self.ninstr += 1

    def wait_bufs(self, eng, bufs):
        self._waits(eng, bufs, (), True)

    def barrier(self):
        for e in self.ENG:
            wd = self.waited[e]
            for k, v in self.count.items():
                if k == e or v == 0:
                    continue
                if wd.get(k, 0) < v:
                    wd[k] = v
                    self.h[e].wait_ge(self.sems[k], v)


def build_program():
    nc = bass.Bass("TRN2", target_bir_lowering=False)
    dt = lambda name, shape, dtype=F32, kind="ExternalInput": nc.dram_tensor(name, shape, dtype, kind=kind)
    xT_d = dt("xT", [D, NT]).ap()
    metaT_d = dt("metaT", [D, NM]).ap()
    vecs_d = dt("vecs", [128, NV]).ap()
    corec_d = dt("corec", [128, 17]).ap()
    cmat_d = dt("cmat", [128, 3 * 128]).ap()
    wg_d = [dt("wg%d" % i, [JC, 128, 1024]).ap() for i in (1, 2)]
    wu_d = [dt("wu%d" % i, [JC, 128, 1024]).ap() for i in (1, 2)]
    wd_d = [dt("wd%d" % i, [DC, 128, DFF]).ap() for i in (1, 2)]
    wfm_d = dt("wfm", [42, 128, 1024]).ap()
    wzi_d = dt("wzi", [4, 128, 1024]).ap()
    wv_d = dt("wv", [128, 1024]).ap()
    wua_d = dt("wua", [4, 128, 1024]).ap()
    wub_d = dt("wub", [4, 128, 1024]).ap()
    wo_d = dt("wo", [DC, 128, 1024]).ap()
    outT_d = dt("outT", [D, NT], F32, "ExternalOutput").ap()
    kt_loc_d = nc.dram_tensor("kt_loc", [128, NT], BF16)
    kt_all_d = nc.dram_tensor("kt_all", [NCORE * 128, NT], BF16)
    v_loc_d = nc.dram_tensor("v_loc", [128, 16 * 130], BF16)
    v_all_d = nc.dram_tensor("v_all", [NCORE * 128, 16 * 130], BF16)
    hg_loc_d = nc.dram_tensor("hg_loc", [128, HGW], F32)
    hg_all_d = nc.dram_tensor("hg_all", [NCORE * 128, HGW], F32)

    with ExitStack() as ctx:
        S = Sched(nc, ctx)

        uniq = [0]

        def sbt(c, name, shape, dtype=F32, side="left"):
            uniq[0] += 1
            return c.enter_context(nc.sbuf_tensor("s%d_%s" % (uniq[0], name), shape, dtype, side=side))

        def sbr(c, name, shape, dtype=F32):
            return sbt(c, name, shape, dtype, side="right")

        PSbig = ctx.enter_context(nc.psum_tensor("psbig", [128, 2048], F32))
        PS = [PSbig[:, i * 512:(i + 1) * 512] for i in range(4)] + \
             [ctx.enter_context(nc.psum_tensor("ps%d" % i, [128, 512], F32)) for i in range(4, 7)]
        pb = [Buf("ps%d" % i) for i in range(7)]
        PT = ctx.enter_context(nc.psum_tensor("pt", [128, 1024], BF16))
        ptb = Buf("pt")

        hT = sbr(ctx, "hT", [128, DC, NA])
        hb = [[Buf("h") for _ in CG] for _ in range(DC)]
        vecs = sbr(ctx, "vecs", [128, NV]); vecb = Buf("vecs")
        corec = sbr(ctx, "corec", [128, 17]); coreb = Buf("corec")
        der = sbr(ctx, "der", [128, 32]); derb = Buf("der")
        cmat = sbr(ctx, "cmat", [128, 384], BF16); cmatb = Buf("cmat")
        mask4 = sbr(ctx, "mask4", [128, 2, 512], BF16); maskb = Buf("mask4")
        ones_bf = sbr(ctx, "ones_bf", [128, 128], BF16); onesb = Buf("ones")
        ones_f = sbr(ctx, "ones_f", [128, 128]); onesfb = Buf("onesf")
        blk_f = sbr(ctx, "blk_f", [128, 128]); blkb = Buf("blk")
        rmask = sbr(ctx, "rmask", [128, 512]); rmaskb = Buf("rmask")
        KT_meta = sbr(ctx, "KT_meta", [128, NM], BF16); ktmb = Buf("ktm")
        V_meta = sbr(ctx, "V_meta", [128, 130], BF16); vmb = Buf("vm")

        def vcol(i, n=1):
            return vecs[:, i:i + n]
        D_LBF, D_OMLF, D_NOMLF, D_LBB, D_OMLB, D_NOMLB = 0, 4, 8, 12, 16, 20
        D_INVR, D_SINSC = 24, 25

        act = lambda fn, r=(), w=(): S.op("act", fn, r, w)
        dve = lambda fn, r=(), w=(): S.op("dve", fn, r, w)
        pool = lambda fn, r=(), w=(): S.op("pool", fn, r, w)

        def mm(out, lhsT, rhs, start, stop, r, w):
            S.op("pe", lambda h: h.matmul(out, lhsT, rhs, start=start, stop=stop), r, w, self_sync=False)

        ch_in = [S.chan() for _ in range(8)]
        S.dma("sp", ch_in[0], lambda h: h.dma_start(out=vecs[:], in_=vecs_d), writes=[vecb])
        S.dma("sp", ch_in[1], lambda h: h.dma_start(out=corec[:], in_=corec_d), writes=[coreb])
        S.dma("pool", ch_in[2], lambda h: h.dma_start(out=cmat[:], in_=cmat_d), writes=[cmatb])
        xT_v = xT_d.rearrange("(c p) t -> p c t", p=128)
        for g in range(4):
            c0, n = CG[g]
            S.dma("sp", ch_in[3 + g], lambda h, c0=c0, n=n: h.dma_start(out=hT[:, :, c0:c0 + n], in_=xT_v[:, :, c0:c0 + n]),
                  writes=[hb[c][g] for c in range(DC)])
        S.dma("sp", ch_in[7], lambda h: h.dma_start(out=hT[:, :, NT:NA], in_=metaT_d.rearrange("(c p) t -> p c t", p=128)),
              writes=[hb[c][4] for c in range(DC)])
        dve(lambda h: h.memset(ones_bf[:], 1.0), w=[onesb])
        dve(lambda h: h.memset(ones_f[:], 1.0), w=[onesfb])
        dve(lambda h: h.memset(rmask[:], 1.0), w=[rmaskb])
        dve(lambda h: h.memset(rmask[:, 0:512:128], 0.0), w=[rmaskb])
        dve(lambda h: h.memset(blk_f[:], 0.0), w=[blkb])
        dve(lambda h: h.memset(blk_f[0:64, 0:64], 1.0), w=[blkb])
        dve(lambda h: h.memset(blk_f[64:128, 64:128], 1.0), w=[blkb])
        for k in range(4):
            dve(lambda h, k=k: h.tensor_copy(out=mask4[:, 0, k * 128:(k + 1) * 128], in_=cmat[:, 128:256]), r=[cmatb], w=[maskb])
            dve(lambda h, k=k: h.tensor_copy(out=mask4[:, 1, k * 128:(k + 1) * 128], in_=cmat[:, 256:384]), r=[cmatb], w=[maskb])
        for (src0, dl, do, dn) in ((28, D_LBF, D_OMLF, D_NOMLF), (36, D_LBB, D_OMLB, D_NOMLB)):
            dve(lambda h, s=src0, dl=dl: h.tensor_tensor(out=der[:, dl:dl + 4], in0=vecs[:, s:s + 4], in1=vecs[:, s + 4:s + 8], op=ALU.subtract),
                r=[vecb], w=[derb])
            act(lambda h, dl=dl, do=do: h.activation(out=der[:, do:do + 4], in_=der[:, dl:dl + 4], func=AF.Sigmoid, scale=-1.0), r=[derb], w=[derb])
            act(lambda h, dl=dl: h.activation(out=der[:, dl:dl + 4], in_=der[:, dl:dl + 4], func=AF.Sigmoid), r=[derb], w=[derb])
            dve(lambda h, do=do, dn=dn: h.tensor_scalar(out=der[:, dn:dn + 4], in0=der[:, do:do + 4], scalar1=-1.0, scalar2=None, op0=ALU.mult),
                r=[derb], w=[derb])
        act(lambda h: h.activation(out=der[:, D_INVR:D_INVR + 1], in_=vecs[:, 48:49], func=AF.Exp, scale=-float(np.log(10000.0) / 16.0)),
            r=[vecb], w=[derb])
        dve(lambda h: h.tensor_scalar(out=der[:, D_INVR:D_INVR + 1], in0=der[:, D_INVR:D_INVR + 1], scalar1=float(1.0 / (2 * np.pi)), scalar2=None, op0=ALU.mult),
            r=[derb], w=[derb])
        dve(lambda h: h.tensor_scalar(out=der[:, D_SINSC:D_SINSC + 1], in0=vecs[:, 51:52], scalar1=-TWO_PI, scalar2=None, op0=ALU.mult),
            r=[vecb], w=[derb])

        class Slots:
            def __init__(self, c, n, width):
                self.t = [sbt(c, "slot%d_%d" % (id(self) % 1000, i), [128, width], BF16) for i in range(n)]
                self.b = [Buf("slot") for _ in range(n)]
                self.ch = [S.chan() for _ in range(n)]
                self.n = n
                self.i = 0

            def load(self, src, width):
                k = self.i % self.n
                self.i += 1
                t, b = self.t[k], self.b[k]
                S.dma("pool", self.ch[k], lambda h: h.dma_start(out=t[:, 0:width], in_=src), writes=[b])
                return t, b

        def make_norm_ctx(lc):
            sq = [sbt(lc, "nsq%d" % i, [128, DC, 512], BF16) for i in range(2)]
            sqb = [Buf("sq") for _ in range(2)]
            t1 = sbt(lc, "nt1", [128, 512]); t1b = Buf("t1")
            rs = [sbt(lc, "nrs%d" % i, [128, 512]) for i in range(2)]; rsb = [Buf("rs") for _ in range(2)]
            return (sq, sqb, t1, t1b, rs, rsb)

        def rmsnorm_all(nctx, wcol0, uT, ub, groups, coff=0):
            (sq, sqb, t1, t1b, rs, rsb) = nctx
            for gi, g in enumerate(groups):
                c0, n = CG[g]
                q = gi % 2
                for cc in range(DC):
                    act(lambda h, cc=cc: h.activation(out=sq[q][:, cc, 0:n], in_=hT[:, cc, c0:c0 + n], func=AF.Square),
                        r=[hb[cc][g]], w=[sqb[q]])
                for cc in range(DC):
                    mm(PS[6][:, 0:n], ones_bf[:, :], sq[q][:, cc, 0:n], cc == 0, cc == DC - 1, [sqb[q], onesb], [pb[6]])
                act(lambda h: h.activation(out=t1[:, 0:n], in_=PS[6][:, 0:n], func=AF.Ln, scale=1.0 / D, bias=EPS), r=[pb[6]], w=[t1b])
                act(lambda h, q=q: h.activation(out=rs[q][:, 0:n], in_=t1[:, 0:n], func=AF.Exp, scale=-0.5), r=[t1b], w=[rsb[q]])
                for cc in range(DC):
                    dve(lambda h, cc=cc, q=q: h.scalar_tensor_tensor(out=uT[:, cc, c0 - coff:c0 - coff + n], in0=hT[:, cc, c0:c0 + n],
                                                                    scalar=vecs[:, wcol0 + cc:wcol0 + cc + 1], in1=rs[q][:, 0:n],
                                                                    op0=ALU.mult, op1=ALU.mult),
                        r=[hb[cc][g], rsb[q], vecb], w=[ub[g]])

        def ffn(c, slots, wg, wu, wd, uT, ub, groups):
            with ExitStack() as lc:
                aT = sbt(lc, "aT", [128, 6, NA], BF16)
                ab = [[Buf("a") for _ in CG] for _ in range(6)]
                sg = [sbt(lc, "fsg%d" % i, [128, 512]) for i in range(2)]
                sgb = [Buf("sg") for _ in range(2)]
                rot = 0
                for (j0, nj) in ((0, 6), (6, 6), (12, 5), (17, 5)):
                    for jj in range(nj):
                        j = j0 + jj
                        tg, bg = slots.load(wg[j], 1024)
                        tu, bu = slots.load(wu[j], 1024)
                        for g in groups:
                            c0, n = CG[g]
                            r = rot % 2
                            rot += 1
                            Gp, Up = PS[r], PS[2 + r]
                            for kc in range(DC):
                                mm(Gp[:, 0:n], tg[:, kc * 128:(kc + 1) * 128], uT[:, kc, c0:c0 + n], kc == 0, kc == DC - 1, [bg, ub[g]], [pb[r]])
                            for kc in range(DC):
                                mm(Up[:, 0:n], tu[:, kc * 128:(kc + 1) * 128], uT[:, kc, c0:c0 + n], kc == 0, kc == DC - 1, [bu, ub[g]], [pb[2 + r]])
                            act(lambda h, r=r, Gp=Gp: h.activation(out=sg[r][:, 0:n], in_=Gp[:, 0:n], func=AF.Silu), r=[pb[r]], w=[sgb[r]])
                            dve(lambda h, r=r, Up=Up, jj=jj: h.tensor_tensor(out=aT[:, jj, c0:c0 + n], in0=sg[r][:, 0:n], in1=Up[:, 0:n], op=ALU.mult),
                                r=[sgb[r], pb[2 + r]], w=[ab[jj][g]])
                    for o in range(DC):
                        td, bd = slots.load(wd[o][:, j0 * 128:(j0 + nj) * 128], nj * 128)
                        for g in groups:
                            c0, n = CG[g]
                            r = rot % 2
                            rot += 1
                            Yp = PS[4 + r]
                            for jj in range(nj):
                                mm(Yp[:, 0:n], td[:, jj * 128:(jj + 1) * 128], aT[:, jj, c0:c0 + n], jj == 0, jj == nj - 1, [bd, ab[jj][g]], [pb[4 + r]])
                            dve(lambda h, Yp=Yp, o=o: h.scalar_tensor_tensor(out=hT[:, o, c0:c0 + n], in0=Yp[:, 0:n], scalar=0.5,
                                                                             in1=hT[:, o, c0:c0 + n], op0=ALU.mult, op1=ALU.add),
                                r=[pb[4 + r], hb[o][g]], w=[hb[o][g]])
                S.barrier()

        def rope_tables(lc_t, tdiv, tmod, tdb, g, want):
            (Ct, Cb, St, Sb, tmp, tmpb, tmi, tmib, rowoff, rowoffb) = lc_t
            c0, n = CG[g]
            dve(lambda h: h.tensor_scalar(out=rowoff[:, 0:1], in0=corec[:, 0:1], scalar1=float(c0 // 64), scalar2=vecs[:, 49:50],
                                          op0=ALU.add, op1=ALU.mult), r=[coreb, vecb], w=[rowoffb])
            dve(lambda h: h.tensor_scalar(out=tmp[0][:, :], in0=tdiv[:, :], scalar1=vecs[:, 49:50], scalar2=rowoff[:, 0:1],
                                          op0=ALU.mult, op1=ALU.add), r=[tdb, vecb, rowoffb], w=[tmpb[0]])
            dve(lambda h: h.scalar_tensor_tensor(out=tmp[1][:, :], in0=tmod[:, :], scalar=vecs[:, 50:51], in1=tmp[0][:, :],
                                                 op0=ALU.mult, op1=ALU.add), r=[tdb, vecb, tmpb[0]], w=[tmpb[1]])
            for (shift, T, Tb, sc) in ((0.0, St, Sb, der[:, D_SINSC:D_SINSC + 1]), (0.25, Ct, Cb, -TWO_PI)):
                dve(lambda h, shift=shift: h.tensor_scalar(out=tmp[0][:, :], in0=tmp[1][:, :], scalar1=der[:, D_INVR:D_INVR + 1], scalar2=shift,
                                                           op0=ALU.mult, op1=ALU.add), r=[tmpb[1], derb], w=[tmpb[0]])
                dve(lambda h: h.tensor_copy(out=tmi[:, :], in_=tmp[0][:, :]), r=[tmpb[0]], w=[tmib])
                dve(lambda h: h.tensor_copy(out=tmp[2][:, :], in_=tmi[:, :]), r=[tmib], w=[tmpb[2]])
                dve(lambda h: h.tensor_tensor(out=tmp[0][:, :], in0=tmp[0][:, :], in1=tmp[2][:, :], op=ALU.subtract), r=[tmpb[0], tmpb[2]], w=[tmpb[0]])
                dve(lambda h: h.scalar_tensor_tensor(out=tmp[2][:, :], in0=tmp[0][:, :], scalar=0.5, in1=tmp[0][:, :],
                                                     op0=ALU.is_gt, op1=ALU.subtract), r=[tmpb[0]], w=[tmpb[2]])
                act(lambda h, T=T, sc=sc: h.activation(out=T[:, :], in_=tmp[2][:, :], func=AF.Sin, scale=sc), r=[tmpb[2], derb], w=[Tb])

        def make_rope_ctx(lc):
            Ct = sbt(lc, "ropeC", [128, 512]); St = sbt(lc, "ropeS", [128, 512])
            tmp = [sbt(lc, "ropet%d" % i, [128, 512]) for i in range(3)]
            tmi = sbt(lc, "ropei", [128, 512], I32)
            rowoff = sbt(lc, "rowoff", [128, 1])
            tdiv = sbt(lc, "tdiv", [128, 512]); tmod = sbt(lc, "tmod", [128, 512]); tdb = Buf("td")
            pool(lambda h: h.iota(tdiv[:, :], [[1, 8], [0, 64]], base=0, channel_multiplier=0, allow_small_or_imprecise_dtypes=True), w=[tdb])
            pool(lambda h: h.iota(tmod[:, :], [[0, 8], [1, 64]], base=0, channel_multiplier=0, allow_small_or_imprecise_dtypes=True), w=[tdb])
            return (Ct, Buf("C"), St, Buf("S"), tmp, [Buf("t") for _ in range(3)], tmi, Buf("ti"), rowoff, Buf("ro")), tdiv, tmod, tdb

        def qk_post(lcw, Zp, zpb, Zsp, zspb, n, gcol, gscol, rope, out_ap, outb):
            (sqf, sqfb, t1, t1b, rs, rsb, qn, qnb, qs, qsb) = lcw
            act(lambda h: h.activation(out=sqf[:, 0:n], in_=Zp[:, 0:n], func=AF.Square), r=[zpb], w=[sqfb])
            mm(PS[6][:, 0:n], blk_f[:, :], sqf[:, 0:n], True, True, [blkb, sqfb], [pb[6]])
            act(lambda h: h.activation(out=t1[:, 0:n], in_=PS[6][:, 0:n], func=AF.Ln, scale=1.0 / 64, bias=EPS), r=[pb[6]], w=[t1b])
            act(lambda h: h.activation(out=rs[:, 0:n], in_=t1[:, 0:n], func=AF.Exp, scale=-0.5), r=[t1b], w=[rsb])
            if rope is None:
                dve(lambda h: h.scalar_tensor_tensor(out=out_ap, in0=Zp[:, 0:n], scalar=vecs[:, gcol:gcol + 1], in1=rs[:, 0:n],
                                                     op0=ALU.mult, op1=ALU.mult), r=[zpb, rsb, vecb], w=outb)
                return
            Ct, Cb, St, Sb = rope
            dve(lambda h: h.scalar_tensor_tensor(out=qn[:, 0:n], in0=Zp[:, 0:n], scalar=vecs[:, gcol:gcol + 1], in1=rs[:, 0:n],
                                                 op0=ALU.mult, op1=ALU.mult), r=[zpb, rsb, vecb], w=[qnb])
            dve(lambda h: h.scalar_tensor_tensor(out=qs[:, 0:n], in0=Zsp[:, 0:n], scalar=vecs[:, gscol:gscol + 1], in1=rs[:, 0:n],
                                                 op0=ALU.mult, op1=ALU.mult), r=[zspb, rsb, vecb], w=[qsb])
            dve(lambda h: h.tensor_tensor(out=qn[:, 0:n], in0=qn[:, 0:n], in1=Ct[:, 0:n], op=ALU.mult), r=[qnb, Cb], w=[qnb])
            dve(lambda h: h.tensor_tensor(out=qs[:, 0:n], in0=qs[:, 0:n], in1=St[:, 0:n], op=ALU.mult), r=[qsb, Sb], w=[qsb])
            dve(lambda h: h.tensor_tensor(out=out_ap, in0=qn[:, 0:n], in1=qs[:, 0:n], op=ALU.add), r=[qnb, qsb], w=outb)

        def make_qk_ctx(lc):
            names = ["sqf", "t1", "rs", "qn", "qs"]
            out = []
            for nm in names:
                out.append(sbt(lc, "qk_" + nm, [128, 512]))
                out.append(Buf(nm))
            return tuple(out)

        def hg_gates(lcw, Zf, zfb, n, L, d, h, full, qsil, qsilb, khT_ap, khb, dcol_ap, dcolb, blc_ap,
                     qt_ap=None, kt_ap=None, qkb=None, ecol_ap=None, need_khat=True):
            (sgt, sgb_, lf, lfb, kk, kkb, cum, cumb, bb, bbb, X, Xb, E, Eb) = lcw
            nch = n // L
            v3 = lambda t: t[:, 0:n].rearrange("p (c l) -> p c l", l=L)
            col = lambda t, off: t[:, off:n:L]
            bc = lambda t, off: col(t, off).unsqueeze(2).to_broadcast([128, nch, L])
            lb_c, oml_c, noml_c = (D_LBF, D_OMLF, D_NOMLF) if d == 0 else (D_LBB, D_OMLB, D_NOMLB)
            act(lambda hh: hh.activation(out=sgt[:, 0:n], in_=Zf[:, 0:n], func=AF.Sigmoid), r=[zfb], w=[sgb_])
            yield
            act(lambda hh: hh.activation(out=lf[:, 0:n], in_=sgt[:, 0:n], func=AF.Ln, scale=der[:, oml_c + h:oml_c + h + 1],
                                         bias=der[:, lb_c + h:lb_c + h + 1]), r=[sgb_, derb], w=[lfb])
            yield
            dve(lambda hh: hh.tensor_scalar(out=kk[:, 0:n], in0=sgt[:, 0:n], scalar1=der[:, noml_c + h:noml_c + h + 1],
                                            scalar2=der[:, oml_c + h:oml_c + h + 1], op0=ALU.mult, op1=ALU.add), r=[sgb_, derb], w=[kkb])
            yield
            dve(lambda hh: hh.tensor_tensor_scan(out=cum[:, 0:n], data0=rmask[:, 0:n], data1=lf[:, 0:n], initial=0.0, op0=ALU.mult, op1=ALU.add),
                r=[rmaskb, lfb], w=[cumb])
            yield
            if d == 0:
                b, bbuf = cum, cumb
                last_off, mid_off = L - 1, L // 2 - 1
            else:
                dve(lambda hh: hh.tensor_tensor(out=bb[:, 0:n], in0=lf[:, 0:n], in1=cum[:, 0:n], op=ALU.subtract), r=[lfb, cumb], w=[bbb])
                yield
                dve(lambda hh: hh.tensor_tensor(out=v3(bb), in0=v3(bb), in1=bc(cum, L - 1), op=ALU.add), r=[bbb, cumb], w=[bbb])
                yield
                b, bbuf = bb, bbb
                last_off, mid_off = 0, L // 2
            if need_khat:
                dve(lambda hh: hh.tensor_tensor(out=v3(X), in0=v3(b), in1=bc(b, last_off), op=ALU.subtract), r=[bbuf], w=[Xb])
                yield
                act(lambda hh: hh.activation(out=E[:, 0:n], in_=X[:, 0:n], func=AF.Exp, scale=-1.0), r=[Xb], w=[Eb])
                yield
                dve(lambda hh: hh.tensor_tensor(out=khT_ap, in0=kk[:, 0:n], in1=E[:, 0:n], op=ALU.mult), r=[kkb, Eb], w=[khb])
                yield
                act(lambda hh: hh.activation(out=dcol_ap[:, 0:nch], in_=col(b, last_off), func=AF.Exp), r=[bbuf], w=[dcolb])
                yield
                dve(lambda hh: hh.tensor_copy(out=blc_ap[:, 0:nch], in_=col(b, last_off)), r=[bbuf], w=[dcolb])
                yield
            if not full:
                return
            dve(lambda hh: hh.tensor_tensor(out=v3(X), in0=v3(b), in1=bc(b, mid_off), op=ALU.subtract), r=[bbuf, Xb], w=[Xb])
            yield
            act(lambda hh: hh.activation(out=E[:, 0:n], in_=X[:, 0:n], func=AF.Exp, scale=1.0), r=[Xb] + ([khb] if need_khat else []), w=[Eb])
            yield
            dve(lambda hh: hh.tensor_tensor(out=qt_ap, in0=qsil[:, 0:n], in1=E[:, 0:n], op=ALU.mult), r=[qsilb, Eb], w=[qkb])
            yield
            act(lambda hh: hh.activation(out=E[:, 0:n], in_=X[:, 0:n], func=AF.Exp, scale=-1.0), r=[Xb, qkb], w=[Eb])
            yield
            dve(lambda hh: hh.tensor_tensor(out=kt_ap, in0=kk[:, 0:n], in1=E[:, 0:n], op=ALU.mult), r=[kkb, Eb], w=[qkb])
            yield
            act(lambda hh: hh.activation(out=ecol_ap[:, 0:nch], in_=col(b, mid_off), func=AF.Exp), r=[bbuf], w=[dcolb])
            yield

        def run_interleaved(gens):
            gens = list(gens)
            while gens:
                nxt = []
                for g_ in gens:
                    try:
                        next(g_)
                        nxt.append(g_)
                    except StopIteration:
                        pass
                gens = nxt

        def make_gate_ctx(lc):
            out = []
            for nm in ["sg", "lf", "kk", "cum", "bb", "X", "E"]:
                out.append(sbt(lc, "hg_" + nm, [128, 512]))
                out.append(Buf(nm))
            return tuple(out)

        cY = ExitStack()
        with ExitStack() as cA:
            slots = Slots(cA, 8, 1024)
            uT = sbt(cA, "uT", [128, DC, NA], BF16)
            ub = [Buf("u") for _ in CG]
            sbin0 = sbt(cA, "sbin0", [128, 4, 16, 128], BF16); sbin0b = [Buf("sbin0") for _ in range(4)]
            pbcol = sbt(cA, "pbcol", [128, 4, 16]); pbcolb = [Buf("pbcol") for _ in range(4)]
            s0f = sbt(cA, "s0f", [128, 4, 128]); s0fb = Buf("s0f")
            s0b = sbt(cA, "s0b", [128, 4, 128]); s0bb = Buf("s0b")
            cX = ExitStack()
            smeta = sbt(cX, "smeta", [128, 4, 128]); smetab = Buf("smeta")
            hgpay = sbt(cX, "hgpay", [128, HGW]); hgpayb = Buf("hgpay")
            ALLG = [0, 1, 2, 3, 4]
            cN = ExitStack()
            nctxA = make_norm_ctx(cN)
            rmsnorm_all(nctxA, 0, uT, ub, ALLG)
            ffn(cA, slots, wg_d[0], wu_d[0], wd_d[0], uT, ub, ALLG)
            rmsnorm_all(nctxA, 8, uT, ub, ALLG)
            S.barrier()
            cN.close()

            with ExitStack() as c4:
                KT_loc = sbt(c4, "KT_loc", [128, NA], BF16); ktb = Buf("ktloc")
                V_loc = sbt(c4, "V_loc", [128, 17, 130], BF16); vlb = Buf("vloc")
                ropec, tdiv, tmod, tdb = make_rope_ctx(c4)
                qkc = make_qk_ctx(c4)
                dve(lambda h: h.memset(V_loc[:, :, 64:65], 1.0), w=[vlb])
                dve(lambda h: h.memset(V_loc[:, :, 129:130], 1.0), w=[vlb])
                tk, bk = slots.load(wfm_d[24], 1024)
                tks, bks = slots.load(wfm_d[25], 1024)
                for g in ALLG:
                    c0, n = CG[g]
                    r = g % 2
                    for kc in range(DC):
                        mm(PS[r][:, 0:n], tk[:, kc * 128:(kc + 1) * 128], uT[:, kc, c0:c0 + n], kc == 0, kc == DC - 1, [bk, ub[g]], [pb[r]])
                    for kc in range(DC):
                        mm(PS[2 + r][:, 0:n], tks[:, kc * 128:(kc + 1) * 128], uT[:, kc, c0:c0 + n], kc == 0, kc == DC - 1, [bks, ub[g]], [pb[2 + r]])
                    if g < 4:
                        rope_tables(ropec, tdiv, tmod, tdb, g, None)
                        rp = (ropec[0], ropec[1], ropec[2], ropec[3])
                    else:
                        rp = None
                    qk_post(qkc, PS[r], pb[r], PS[2 + r], pb[2 + r], n, 45, 47, rp, KT_loc[:, c0:c0 + n], [ktb])
                tv, bv = slots.load(wv_d, 1024)
                for blk in range(17):
                    nb_ = 128 if blk < 16 else NM
                    t0 = blk * 128
                    r = blk % 2
                    for kc in range(DC):
                        mm(PS[4 + r][0:nb_, 0:128], uT[:, kc, t0:t0 + nb_], tv[:, kc * 128:(kc + 1) * 128], kc == 0, kc == DC - 1,
                           [bv, ub[min(blk // 4, 4)]], [pb[4 + r]])
                    act(lambda h, r=r, blk=blk, nb_=nb_: h.copy(out=V_loc[0:nb_, blk, 0:64], in_=PS[4 + r][0:nb_, 0:64]), r=[pb[4 + r]], w=[vlb])
                    dve(lambda h, r=r, blk=blk, nb_=nb_: h.tensor_copy(out=V_loc[0:nb_, blk, 65:129], in_=PS[4 + r][0:nb_, 64:128]), r=[pb[4 + r]], w=[vlb])
                ch_k, ch_v = S.chan(), S.chan()
                ktlb, ktab, vlob, vab = Buf("ktl_d"), Buf("kta_d"), Buf("vl_d"), Buf("va_d")
                S.dma("sp", ch_k, lambda h: h.dma_start(out=kt_loc_d.ap(), in_=KT_loc[:, 0:NT]), reads=[ktb], writes=[ktlb])
                S.dma("sp", ch_v, lambda h: h.dma_start(out=v_loc_d.ap().rearrange("p (b n) -> p b n", n=130), in_=V_loc[:, 0:16, :]),
                      reads=[vlb], writes=[vlob])
                ch_cc = [S.chan() for _ in range(3)]
                S.wait_bufs("pool", slots.b)
                S.dma("pool", ch_cc[0], lambda h: h.collective_compute("AllGather", ALU.bypass, replica_groups=[list(range(NCORE))],
                                                                        ins=[kt_loc_d.ap().opt()], outs=[kt_all_d.ap().opt()]),
                      reads=[ktlb], writes=[ktab], inc=1)
                S.wait_bufs("pool", [ktab])
                S.dma("pool", ch_cc[1], lambda h: h.collective_compute("AllGather", ALU.bypass, replica_groups=[list(range(NCORE))],
                                                                        ins=[v_loc_d.ap().opt()], outs=[v_all_d.ap().opt()]),
                      reads=[vlob], writes=[vab], inc=1)
                S.wait_bufs("pool", [vab])
                dve(lambda h: h.tensor_copy(out=KT_meta[:, :], in_=KT_loc[:, NT:NA]), r=[ktb], w=[ktmb])
                dve(lambda h: h.tensor_copy(out=V_meta[0:NM, :], in_=V_loc[0:NM, 16, :]), r=[vlb], w=[vmb])
                S.barrier()

            with ExitStack() as c5:
                gatecs = [make_gate_ctx(c5), make_gate_ctx(c5)]
                v_h = sbt(c5, "v_h", [128, 17, 128], BF16); vhb = Buf("vh")
                khT = [sbt(c5, "khT%d" % d, [128, NA], BF16) for d in range(2)]; khb = [Buf("khT") for _ in range(2)]
                khat = [sbt(c5, "khat%d" % d, [128, 17, 128], BF16) for d in range(2)]; khatb = [Buf("khat") for _ in range(2)]
                dcol = [sbt(c5, "dcol%d" % d, [128, 17]) for d in range(2)]; dcolb = [Buf("dcol") for _ in range(2)]
                blc = [sbt(c5, "blc%d" % d, [128, 17]) for d in range(2)]
                Sst = [sbt(c5, "Sst%d" % d, [128, 128]) for d in range(2)]; Sstb = [Buf("Sst") for _ in range(2)]
                prun = [sbt(c5, "prun%d" % d, [128, 2]) for d in range(2)]; prunb = [Buf("prun") for _ in range(2)]
                for h in range(4):
                    tzi, bzi = slots.load(wzi_d[h], 1024)
                    tfs = [slots.load(wfm_d[4 + 4 * d + h], 1024) for d in range(2)]
                    for blk in range(17):
                        nb_ = 128 if blk < 16 else NM
                        t0 = blk * 128
                        r = blk % 2
                        for kc in range(DC):
                            mm(PS[2 + r][0:nb_, 0:128], uT[:, kc, t0:t0 + nb_], tzi[:, kc * 128:(kc + 1) * 128], kc == 0, kc == DC - 1,
                               [bzi, ub[min(blk // 4, 4)]], [pb[2 + r]])
                        if blk % 2 == 0:
                            act(lambda hh, r=r, blk=blk, nb_=nb_: hh.copy(out=v_h[0:nb_, blk, :], in_=PS[2 + r][0:nb_, 0:128]), r=[pb[2 + r]], w=[vhb])
                        else:
                            dve(lambda hh, r=r, blk=blk, nb_=nb_: hh.tensor_copy(out=v_h[0:nb_, blk, :], in_=PS[2 + r][0:nb_, 0:128]), r=[pb[2 + r]], w=[vhb])
                    for g in ALLG:
                        c0, n = CG[g]
                        L = 128 if g < 4 else NM
                        gens = []
                        for d in range(2):
                            if d == 1 and g == 4:
                                continue
                            tf, bf_ = tfs[d]
                            for kc in range(DC):
                                mm(PS[d][:, 0:n], tf[:, kc * 128:(kc + 1) * 128], uT[:, kc, c0:c0 + n], kc == 0, kc == DC - 1, [bf_, ub[g]], [pb[d]])
                            gens.append(hg_gates(gatecs[d], PS[d], pb[d], n, L, d, h, False, None, None, khT[d][:, c0:c0 + n], khb[d],
                                                 dcol[d][:, 4 * g:4 * g + n // L], dcolb[d], blc[d][:, 4 * g:4 * g + n // L]))
                        run_interleaved(gens)
                    for d in range(2):
                        nblk = 17 if d == 0 else 16
                        for b0 in range(0, nblk, 4):
                            bl = list(range(b0, min(b0 + 4, nblk)))
                            for i, blk in enumerate(bl):
                                nb_ = 128 if blk < 16 else NM
                                S.op("pe", lambda hh, i=i, blk=blk, nb_=nb_, d=d: hh.transpose(PT[0:nb_, i * 128:(i + 1) * 128], khT[d][:, blk * 128:blk * 128 + nb_], cmat[:, 0:128]),
                                     [khb[d], cmatb], [ptb], self_sync=False)
                            for i, blk in enumerate(bl):
                                nb_ = 128 if blk < 16 else NM
                                dve(lambda hh, i=i, blk=blk, nb_=nb_, d=d: hh.tensor_copy(out=khat[d][0:nb_, blk, :], in_=PT[0:nb_, i * 128:(i + 1) * 128]), r=[ptb], w=[khatb[d]])
                    mm(PS[6][:, 0:128], khat[0][0:NM, 16, :], v_h[0:NM, 16, :], True, True, [khatb[0], vhb], [pb[6]])
                    act(lambda hh, h=h: hh.copy(out=smeta[:, h, :], in_=PS[6][:, 0:128]), r=[pb[6]], w=[smetab])
                    for idx in range(16):
                        for d in range(2):
                            j = idx if d == 0 else 15 - idx
                            mm(PS[4 + d][:, 0:128], khat[d][:, j, :], v_h[:, j, :], True, True, [khatb[d], vhb], [pb[4 + d]])
                            if d == 1:
                                if idx == 0:
                                    dve(lambda hh, h=h, j=j: hh.memset(sbin0[:, h, j, :], 0.0), w=[sbin0b[h]])
                                    dve(lambda hh, h=h, j=j: hh.memset(pbcol[:, h, j:j + 1], 1.0), w=[pbcolb[h]])
                                    dve(lambda hh: hh.memset(prun[1][:, 0:1], 1.0), w=[prunb[1]])
                                else:
                                    dve(lambda hh, h=h, j=j: hh.tensor_copy(out=sbin0[:, h, j, :], in_=Sst[1][:, :]), r=[Sstb[1]], w=[sbin0b[h]])
                                    dve(lambda hh, h=h, j=j: hh.tensor_copy(out=pbcol[:, h, j:j + 1], in_=prun[1][:, 0:1]), r=[prunb[1]], w=[pbcolb[h]])
                                dve(lambda hh, j=j: hh.tensor_tensor(out=prun[1][:, 0:1], in0=prun[1][:, 0:1], in1=dcol[1][:, j:j + 1], op=ALU.mult),
                                    r=[prunb[1], dcolb[1]], w=[prunb[1]])
                            if idx == 0:
                                act(lambda hh, d=d: hh.copy(out=Sst[d][:, :], in_=PS[4 + d][:, 0:128]), r=[pb[4 + d]], w=[Sstb[d]])
                            else:
                                dve(lambda hh, d=d, j=j: hh.scalar_tensor_tensor(out=Sst[d][:, :], in0=Sst[d][:, :], scalar=dcol[d][:, j:j + 1], in1=PS[4 + d][:, 0:128],
                                                                                 op0=ALU.mult, op1=ALU.add), r=[Sstb[d], dcolb[d], pb[4 + d]], w=[Sstb[d]])
                    for d in range(2):
                        off = (0 if d == 0 else 512) + h * 128
                        dve(lambda hh, off=off, d=d: hh.tensor_copy(out=hgpay[:, off:off + 128], in_=Sst[d][:, :]), r=[Sstb[d]], w=[hgpayb])
                        dve(lambda hh, d=d: hh.tensor_reduce(out=prun[d][:, 1:2], in_=blc[d][:, 0:16], op=ALU.add, axis=mybir.AxisListType.X), r=[dcolb[d]], w=[prunb[d]])
                        ac = 1024 + 4 * d + h
                        act(lambda hh, ac=ac, d=d: hh.activation(out=hgpay[:, ac:ac + 1], in_=prun[d][:, 1:2], func=AF.Exp), r=[prunb[d]], w=[hgpayb])
                S.barrier()
            ch_h = S.chan()
            hglb, hgab = Buf("hgl"), Buf("hga")
            S.dma("sp", ch_h, lambda h: h.dma_start(out=hg_loc_d.ap(), in_=hgpay[:, :]), reads=[hgpayb], writes=[hglb])
            S.wait_bufs("pool", slots.b)
            S.dma("pool", ch_cc[2], lambda h: h.collective_compute("AllGather", ALU.bypass, replica_groups=[list(range(NCORE))],
                                                                    ins=[hg_loc_d.ap().opt()], outs=[hg_all_d.ap().opt()]),
                  reads=[hglb], writes=[hgab], inc=1)
            S.wait_bufs("pool", [hgab])
            with ExitStack() as c6:
                stg = sbt(c6, "hgstg", [128, NCORE, HGW]); stgb = Buf("stg")
                stch = S.chan()
                aeff = sbt(c6, "aeff", [128, 8]); aeffb = Buf("aeff")
                tmpB = sbt(c6, "tmpB", [128, 4, 128]); tmpBb = Buf("tmpB")
                dve(lambda hh: hh.tensor_copy(out=s0f[:, :, :], in_=smeta[:, :, :]), r=[smetab], w=[s0fb])
                dve(lambda hh: hh.memset(s0b[:, :, :], 0.0), w=[s0bb])
                S.dma("sp", stch, lambda h: h.dma_start(out=stg[:, :, :], in_=hg_all_d.ap().rearrange("(r p) n -> p r n", p=128)), reads=[hgab], writes=[stgb])
                for (d, order) in ((0, list(range(NCORE))), (1, list(range(NCORE - 1, -1, -1)))):
                    st, stb_ = (s0f, s0fb) if d == 0 else (s0b, s0bb)
                    for j in order:
                        mcol = (1 + j) if d == 0 else (9 + j)
                        dve(lambda hh, j=j, d=d, mcol=mcol: hh.tensor_scalar(out=aeff[:, 0:4], in0=stg[:, j, 1024 + 4 * d:1028 + 4 * d], scalar1=-1.0,
                                                                            scalar2=corec[:, mcol:mcol + 1], op0=ALU.add, op1=ALU.mult),
                            r=[stgb, coreb], w=[aeffb])
                        dve(lambda hh: hh.tensor_scalar(out=aeff[:, 0:4], in0=aeff[:, 0:4], scalar1=1.0, scalar2=None, op0=ALU.add), r=[aeffb], w=[aeffb])
                        dve(lambda hh, j=j, d=d, mcol=mcol: hh.tensor_scalar(out=tmpB[:, :, :], in0=stg[:, j, 512 * d:512 * d + 512].rearrange("p (a b) -> p a b", b=128),
                                                                            scalar1=corec[:, mcol:mcol + 1], scalar2=None, op0=ALU.mult),
                            r=[stgb, coreb], w=[tmpBb])
                        dve(lambda hh, st=st: hh.tensor_tensor(out=st[:, :, :], in0=st[:, :, :], in1=aeff[:, 0:4].unsqueeze(2).to_broadcast([128, 4, 128]), op=ALU.mult),
                            r=[aeffb, stb_], w=[stb_])
                        dve(lambda hh, st=st: hh.tensor_tensor(out=st[:, :, :], in0=st[:, :, :], in1=tmpB[:, :, :], op=ALU.add), r=[tmpBb, stb_], w=[stb_])
                S.barrier()
            cX.close()

            yaT = sbr(cY, "yaT", [128, 4, NT], BF16)
            yab = [[Buf("ya") for _ in range(4)] for _ in range(4)]
            with ExitStack() as c7:
                gsets = [make_gate_ctx(c7), make_gate_ctx(c7)]
                qsil = sbt(c7, "qsil", [128, 512]); qsilb = Buf("qsil")
                zgs = sbt(c7, "zgs", [128, 512]); zgsb = Buf("zgs")
                v_h = sbt(c7, "v_h2", [128, 4, 128], BF16); vhb = Buf("vh")
                khT = sbt(c7, "khT2", [128, 512], BF16); khb = Buf("khT")
                khat = sbt(c7, "khat2", [128, 4, 128], BF16); khatb = Buf("khat")
                qtl = [sbt(c7, "qtl%d" % i, [128, 512], BF16) for i in range(2)]
                ktl = [sbt(c7, "ktl%d" % i, [128, 512], BF16) for i in range(2)]
                qkb = [Buf("qk") for _ in range(2)]
                dcol = sbt(c7, "dcol2", [128, 2, 4]); dcolb = [Buf("dcol") for _ in range(2)]
                ecol = sbt(c7, "ecol2", [128, 2, 4])
                blc = sbt(c7, "blc2", [128, 8])
                AT = [sbt(c7, "AT%d" % i, [128, 512], BF16) for i in range(2)]
                ATb = [Buf("AT") for _ in range(2)]
                Sf = sbt(c7, "Sf", [128, 128]); Sfb = Buf("Sf")
                Sp0 = [sbt(c7, "Sp0_%d" % i, [128, 128], BF16) for i in range(2)]
                Sp0b = [Buf("Sp0") for _ in range(2)]
                stmp4 = sbt(c7, "stmp4", [128, 4, 128]); stmp4b = Buf("stmp4")
                Sp1 = sbt(c7, "Sp1", [128, 4, 128], BF16); Sp1b = Buf("Sp1")
                osq = sbt(c7, "osq", [128, 512], BF16); osqb = Buf("osq")
                for h in range(4):
                    tq, bq = slots.load(wfm_d[0 + h], 1024)
                    tff, bff = slots.load(wfm_d[4 + h], 1024)
                    tfb, bfb = slots.load(wfm_d[8 + h], 1024)
                    tg_, bg_ = slots.load(wfm_d[12 + h], 1024)
                    tzi, bzi = slots.load(wzi_d[h], 1024)
                    dve(lambda hh, h=h: hh.tensor_copy(out=Sf[:, :], in_=s0f[:, h, :]), r=[s0fb], w=[Sfb])
                    for g in range(4):
                        c0, n = CG[g]
                        ot1, ot1b, ors, orsb, oy, oyb = gsets[0][0], gsets[0][1], gsets[0][2], gsets[0][3], gsets[0][4], gsets[0][5]
                        for (pi, tw, bw) in ((0, tq, bq), (1, tff, bff), (2, tfb, bfb), (3, tg_, bg_)):
                            for kc in range(DC):
                                mm(PS[pi][:, 0:n], tw[:, kc * 128:(kc + 1) * 128], uT[:, kc, c0:c0 + n], kc == 0, kc == DC - 1, [bw, ub[g]], [pb[pi]])
                        for bi in range(4):
                            t0 = c0 + bi * 128
                            for kc in range(DC):
                                mm(PS[4][:, bi * 128:(bi + 1) * 128], uT[:, kc, t0:t0 + 128], tzi[:, kc * 128:(kc + 1) * 128], kc == 0, kc == DC - 1,
                                   [bzi, ub[g]], [pb[4]])
                        act(lambda hh: hh.activation(out=qsil[:, :], in_=PS[0][:, :], func=AF.Sigmoid), r=[pb[0]], w=[qsilb])
                        act(lambda hh: hh.activation(out=zgs[:, :], in_=PS[3][:, :], func=AF.Sigmoid), r=[pb[3]], w=[zgsb])
                        dve(lambda hh: hh.tensor_tensor(out=qsil[:, :], in0=PS[0][:, :], in1=qsil[:, :], op=ALU.mult), r=[pb[0], qsilb], w=[qsilb])
                        dve(lambda hh: hh.tensor_tensor(out=zgs[:, :], in0=PS[3][:, :], in1=zgs[:, :], op=ALU.mult), r=[pb[3], zgsb], w=[zgsb])
                        dve(lambda hh: hh.tensor_copy(out=v_h[:, :, :], in_=PS[4][:, :].rearrange("p (b n) -> p b n", n=128)), r=[pb[4]], w=[vhb])
                        run_interleaved([
                            hg_gates(gsets[d], PS[1 + d], pb[1 + d], n, 128, d, h, True, qsil, qsilb, khT[:, :], khb,
                                     dcol[:, d, :], dcolb[d], blc[:, 4 * d:4 * d + 4], qtl[d][:, :], ktl[d][:, :], qkb[d], ecol[:, d, :],
                                     need_khat=(d == 0))
                            for d in range(2)])
                        for i in range(4):
                            S.op("pe", lambda hh, i=i: hh.transpose(PT[:, i * 128:(i + 1) * 128], khT[:, i * 128:(i + 1) * 128], cmat[:, 0:128]),
                                 [khb, cmatb], [ptb], self_sync=False)
                        dve(lambda hh: hh.tensor_copy(out=khat[:, :, :], in_=PT[:, 0:512].rearrange("p (b n) -> p b n", n=128)), r=[ptb], w=[khatb])
                        for d in range(2):
                            for i in range(4):
                                mm(PS[d][:, i * 128:(i + 1) * 128], ktl[d][:, i * 128:(i + 1) * 128], qtl[d][:, i * 128:(i + 1) * 128], True, True,
                                   [qkb[d]], [pb[d]])
                            dve(lambda hh, d=d: hh.tensor_tensor(out=AT[d][:, :], in0=PS[d][:, :], in1=mask4[:, d, :], op=ALU.mult),
                                r=[pb[d], maskb], w=[ATb[d]])
                        j0 = g * 4
                        dve(lambda hh, h=h, j0=j0: hh.tensor_tensor(out=stmp4[:, :, :], in0=s0b[:, h, :].unsqueeze(1).to_broadcast([128, 4, 128]),
                                                                    in1=pbcol[:, h, j0:j0 + 4].unsqueeze(2).to_broadcast([128, 4, 128]), op=ALU.mult),
                            r=[s0bb, pbcolb[h]], w=[stmp4b])
                        dve(lambda hh, h=h, j0=j0: hh.tensor_tensor(out=stmp4[:, :, :], in0=stmp4[:, :, :], in1=sbin0[:, h, j0:j0 + 4, :], op=ALU.add),
                            r=[stmp4b, sbin0b[h]], w=[stmp4b])
                        dve(lambda hh: hh.tensor_tensor(out=Sp1[:, :, :], in0=stmp4[:, :, :], in1=ecol[:, 1, :].unsqueeze(2).to_broadcast([128, 4, 128]), op=ALU.mult),
                            r=[stmp4b, dcolb[1]], w=[Sp1b])
                        for i in range(4):
                            q_ = i % 2
                            dve(lambda hh, i=i, q_=q_: hh.tensor_scalar(out=Sp0[q_][:, :], in0=Sf[:, :], scalar1=ecol[:, 0, i:i + 1], scalar2=None, op0=ALU.mult),
                                r=[Sfb, dcolb[0]], w=[Sp0b[q_]])
                            osl = PS[2][:, i * 128:(i + 1) * 128]
                            mm(osl, v_h[:, i, :], AT[0][:, i * 128:(i + 1) * 128], True, False, [vhb, ATb[0]], [pb[2]])
                            mm(osl, v_h[:, i, :], AT[1][:, i * 128:(i + 1) * 128], False, False, [vhb, ATb[1]], [pb[2]])
                            mm(osl, Sp1[:, i, :], qtl[1][:, i * 128:(i + 1) * 128], False, False, [Sp1b, qkb[1]], [pb[2]])
                            mm(osl, Sp0[q_][:, :], qtl[0][:, i * 128:(i + 1) * 128], False, True, [Sp0b[q_], qkb[0]], [pb[2]])
                            mm(PS[5][:, i * 128:(i + 1) * 128], khat[:, i, :], v_h[:, i, :], True, True, [khatb, vhb], [pb[5]])
                            dve(lambda hh, i=i: hh.scalar_tensor_tensor(out=Sf[:, :], in0=Sf[:, :], scalar=dcol[:, 0, i:i + 1], in1=PS[5][:, i * 128:(i + 1) * 128],
                                                                        op0=ALU.mult, op1=ALU.add), r=[Sfb, dcolb[0], pb[5]], w=[Sfb])
                        act(lambda hh: hh.activation(out=osq[:, :], in_=PS[2][:, :], func=AF.Square), r=[pb[2]], w=[osqb])
                        mm(PS[3][:, :], ones_bf[:, :], osq[:, :], True, True, [onesb, osqb], [pb[3]])
                        act(lambda hh: hh.activation(out=ot1[:, :], in_=PS[3][:, :], func=AF.Ln, scale=1.0 / 128, bias=EPS), r=[pb[3]], w=[ot1b])
                        act(lambda hh: hh.activation(out=ors[:, :], in_=ot1[:, :], func=AF.Exp, scale=-0.5), r=[ot1b], w=[orsb])
                        dve(lambda hh, h=h: hh.scalar_tensor_tensor(out=oy[:, :], in0=PS[2][:, :], scalar=vecs[:, 24 + h:25 + h], in1=ors[:, :],
                                                                    op0=ALU.mult, op1=ALU.mult), r=[pb[2], orsb, vecb], w=[oyb])
                        dve(lambda hh, h=h, c0=c0: hh.tensor_tensor(out=yaT[:, h, c0:c0 + 512], in0=oy[:, :], in1=zgs[:, :], op=ALU.mult),
                            r=[oyb, zgsb], w=[yab[h][g]])
                S.barrier()
            QT = sbr(cY, "QT", [128, 4, NT], BF16)
            qtb = [[[Buf("qt") for _ in range(4)] for _ in range(4)] for _ in range(4)]
            with ExitStack() as c8:
                ropec, tdiv, tmod, tdb = make_rope_ctx(c8)
                qkc = make_qk_ctx(c8)
                wq = [slots.load(wfm_d[16 + c], 1024) for c in range(4)]
                wqs = [slots.load(wfm_d[20 + c], 1024) for c in range(4)]
                for g in range(4):
                    c0, n = CG[g]
                    rope_tables(ropec, tdiv, tmod, tdb, g, None)
                    rp = (ropec[0], ropec[1], ropec[2], ropec[3])
                    for c in range(4):
                        r = c % 2
                        for kc in range(DC):
                            mm(PS[r][:, 0:n], wq[c][0][:, kc * 128:(kc + 1) * 128], uT[:, kc, c0:c0 + n], kc == 0, kc == DC - 1, [wq[c][1], ub[g]], [pb[r]])
                        for kc in range(DC):
                            mm(PS[2 + r][:, 0:n], wqs[c][0][:, kc * 128:(kc + 1) * 128], uT[:, kc, c0:c0 + n], kc == 0, kc == DC - 1, [wqs[c][1], ub[g]], [pb[2 + r]])
                        qk_post(qkc, PS[r], pb[r], PS[2 + r], pb[2 + r], n, 44, 46, rp, QT[:, c, c0:c0 + n], qtb[c][g])
                S.barrier()
        S.barrier()

        with ExitStack() as cB:
            KT_all = sbt(cB, "KT_all", [128, NCORE * NT + NM], BF16); ktallb = Buf("ktall")
            V_all = sbt(cB, "V_all", [128, NCORE * 16 + 1, 194], BF16); vallbs = [Buf("vall") for _ in range(NCORE + 1)]
            PTs = [sbt(cB, "PTs%d" % i, [128, 1024], BF16) for i in range(2)]
            PTb = [[Buf("pts") for _ in range(2)] for _ in range(2)]
            rinv = sbt(cB, "rinv", [128, 512]); rinvb = Buf("rinv")
            rbc = sbt(cB, "rbc", [128, 512]); rbcb = Buf("rbc")
            ch_l = [S.chan() for _ in range(2)]
            S.dma("sp", ch_l[0], lambda h: h.dma_start(out=KT_all[:, 0:NCORE * NT].rearrange("p (r n) -> p r n", n=NT),
                                                        in_=kt_all_d.ap().rearrange("(r p) n -> p r n", p=128)), reads=[ktab], writes=[ktallb])
            pool(lambda h: h.memset(V_all[:, :, 130:194], 0.0), w=vallbs)
            vtmp = [Buf("vtmp") for _ in range(NCORE)]
            for r_ in range(NCORE):
                vtmp[r_].w = vallbs[r_].w
                S.dma("sp", ch_l[1], lambda h, r_=r_: h.dma_start(out=V_all[:, r_ * 16:(r_ + 1) * 16, 0:130],
                                                                   in_=v_all_d.ap()[r_ * 128:(r_ + 1) * 128, :].rearrange("p (b n) -> p b n", n=130)),
                      reads=[vab], writes=[vtmp[r_]])
            for r_ in range(NCORE):
                vallbs[r_].w = (ch_l[1], S.count[ch_l[1]])
                vallbs[r_].r = {}
            dve(lambda h: h.tensor_copy(out=KT_all[:, NCORE * NT:NCORE * NT + NM], in_=KT_meta[:, :]), r=[ktmb], w=[ktallb])
            dve(lambda h: h.tensor_copy(out=V_all[0:NM, NCORE * 16, 0:130], in_=V_meta[0:NM, :]), r=[vmb], w=[vallbs[NCORE]])
            NKT = NCORE * 16 + 1
            NP = (NKT + 1) // 2
            grp = 0
            pending = [None]
            QZ = [[sbt(cB, "QZ%d_%d" % (k, i), [128, 512], BF16) for i in range(2)] for k in range(2)]
            QZb = [[Buf("qz") for _ in range(2)] for _ in range(2)]
            for k in range(2):
                for i in range(2):
                    pool(lambda hh, k=k, i=i: hh.memset(QZ[k][i][:, :], 0.0), w=[QZb[k][i]])
            qzi = [0, 0]
            pglob = 0
            for g in range(4):
                c0 = CG[g][0]
                for kvh in range(2):
                    pbase = kvh * 64
                    for half in range(2):
                        t0 = c0 + half * 256
                        for cp in range(2):
                            ob = 4 + grp % 2
                            grp += 1
                            Op = PS[ob]
                            qsl = QT[pbase:pbase + 64, 2 * cp:2 * cp + 2, t0:t0 + 256]
                            qbufs = [qtb[2 * cp][g][kvh * 2 + half], qtb[2 * cp + 1][g][kvh * 2 + half]]
                            zi = qzi[kvh] % 2
                            qzi[kvh] += 1
                            qz, qzb_ = QZ[kvh][zi], QZb[kvh][zi]
                            pool(lambda hh, qz=qz, qsl=qsl, pbase=pbase: hh.tensor_copy(out=qz[pbase:pbase + 64, :].rearrange("p (c n) -> p c n", n=256), in_=qsl),
                                 r=qbufs, w=[qzb_])
                            pg0 = pglob
                            pglob += NP

                            def qk(pp, qz=qz, qzb_=qzb_, pg0=pg0):
                                sp = (pg0 + pp) % 2
                                for hf in range(2):
                                    kt = 2 * pp + hf
                                    if kt >= NKT:
                                        continue
                                    nk = 128 if kt < NKT - 1 else NM
                                    S.op("pe", lambda hh, kt=kt, nk=nk, hf=hf: hh.matmul(PSbig[0:nk, sp * 1024 + hf * 512:sp * 1024 + hf * 512 + 512],
                                                                                       KT_all[:, kt * 128:kt * 128 + nk], qz[:, :], start=True, stop=True),
                                         [ktallb, qzb_], [pb[2 * sp + hf]], self_sync=False)

                            qk(0)
                            for pp in range(NP):
                                sp = (pg0 + pp) % 2
                                full = (2 * pp + 1 < NKT - 1) and WIDE_EXP
                                if full:
                                    act(lambda hh, sp=sp: hh.activation(out=PTs[sp][:, 0:1024], in_=PSbig[:, sp * 1024:sp * 1024 + 1024], func=AF.Exp, scale=0.125),
                                        r=[pb[2 * sp], pb[2 * sp + 1]], w=PTb[sp])
                                else:
                                    for hf in range(2):
                                        kt = 2 * pp + hf
                                        if kt >= NKT:
                                            continue
                                        nk = 128 if kt < NKT - 1 else NM
                                        act(lambda hh, sp=sp, hf=hf, nk=nk: hh.activation(out=PTs[sp][0:nk, hf * 512:hf * 512 + 512],
                                                                                        in_=PSbig[0:nk, sp * 1024 + hf * 512:sp * 1024 + hf * 512 + 512], func=AF.Exp, scale=0.125),
                                            r=[pb[2 * sp + hf]], w=[PTb[sp][hf]])
                                if pp + 1 < NP:
                                    qk(pp + 1)
                                for hf in range(2):
                                    kt = 2 * pp + hf
                                    if kt >= NKT:
                                        continue
                                    nk = 128 if kt < NKT - 1 else NM
                                    S.op("pe", lambda hh, Op=Op, kt=kt, nk=nk, sp=sp, hf=hf, kvh=kvh: hh.matmul(Op[:, :], V_all[0:nk, kt, 65 * kvh:65 * kvh + 128],
                                                                                                            PTs[sp][0:nk, hf * 512:hf * 512 + 512],
                                                                                                            start=(kt == 0), stop=(kt == NKT - 1)),
                                         [vallbs[kt // 16], PTb[sp][hf]], [pb[ob]], self_sync=False)
                                if pp == 6 and pending[0] is not None:
                                    pending[0]()
                                    pending[0] = None

                            def finalize(Op=Op, ob=ob, qsl=qsl, qbufs=qbufs):
                                dve(lambda hh: hh.reciprocal(out=rinv[64:65, :], in_=Op[64:65, :]), r=[pb[ob]], w=[rinvb])
                                mm(PS[6][0:64, :], ones_f[64:65, 0:64], rinv[64:65, :], True, True, [onesfb, rinvb], [pb[6]])
                                act(lambda hh: hh.copy(out=rbc[0:64, :], in_=PS[6][0:64, :]), r=[pb[6]], w=[rbcb])
                                dve(lambda hh: hh.tensor_tensor(out=qsl, in0=Op[0:64, :].rearrange("p (c n) -> p c n", n=256),
                                                                in1=rbc[0:64, :].rearrange("p (c n) -> p c n", n=256), op=ALU.mult),
                                    r=[pb[ob], rbcb], w=qbufs)
                            pending[0] = finalize
            pending[0]()
            S.barrier()
        S.barrier()

        with ExitStack() as cC:
            slots = Slots(cC, 8, 1024)
            for half in range(2):
                HG = [2 * half, 2 * half + 1]
                hoff = half * 1024
                with ExitStack() as c9:
                    uT = sbt(c9, "uTm", [128, DC, 1024], BF16)
                    ub = [Buf("u") for _ in CG]
                    mixT = sbt(c9, "mixT", [128, DC, 1024], BF16)
                    mixb = [[Buf("mix") for _ in range(4)] for _ in range(DC)]
                    sga = [sbt(c9, "sga%d" % i, [128, 512]) for i in range(2)]; sgab = [Buf("sga") for _ in range(2)]
                    sgb2 = [sbt(c9, "sgb%d" % i, [128, 512]) for i in range(2)]; sgbb = [Buf("sgb") for _ in range(2)]
                    m1 = [sbt(c9, "m1%d" % i, [128, 512]) for i in range(2)]; m1b = [Buf("m1") for _ in range(2)]
                    m2 = [sbt(c9, "m2%d" % i, [128, 512]) for i in range(2)]; m2b = [Buf("m2") for _ in range(2)]
                    nctx9 = make_norm_ctx(c9)
                    rmsnorm_all(nctx9, 8, uT, ub, HG, hoff)
                    rot = 0
                    for s in range(4):
                        tua, bua = slots.load(wua_d[s], 1024)
                        tub, bub = slots.load(wub_d[s], 1024)
                        for oo in range(2):
                            o = 2 * s + oo
                            tga, bga = slots.load(wfm_d[26 + o], 1024)
                            tgb, bgb = slots.load(wfm_d[34 + o], 1024)
                            for g in HG:
                                c0, n = CG[g]
                                r = rot % 2
                                rot += 1
                                za, zb, pa, pbk = 0, 1, 2 + r, 4 + r
                                for kc in range(DC):
                                    mm(PS[za][:, :], tga[:, kc * 128:(kc + 1) * 128], uT[:, kc, c0 - hoff:c0 - hoff + n], kc == 0, kc == DC - 1, [bga, ub[g]], [pb[za]])
                                for kc in range(DC):
                                    mm(PS[zb][:, :], tgb[:, kc * 128:(kc + 1) * 128], uT[:, kc, c0 - hoff:c0 - hoff + n], kc == 0, kc == DC - 1, [bgb, ub[g]], [pb[zb]])
                                for kc in range(4):
                                    mm(PS[pa][:, :], tua[:, (oo * 4 + kc) * 128:(oo * 4 + kc + 1) * 128], yaT[:, kc, c0:c0 + n], kc == 0, kc == 3,
                                       [bua, yab[kc][g]], [pb[pa]])
                                for kc in range(4):
                                    mm(PS[pbk][:, :], tub[:, (oo * 4 + kc) * 128:(oo * 4 + kc + 1) * 128], QT[:, kc, c0:c0 + n], kc == 0, kc == 3,
                                       [bub] + qtb[kc][g], [pb[pbk]])
                                act(lambda hh, r=r, za=za: hh.activation(out=sga[r][:, :], in_=PS[za][:, :], func=AF.Sigmoid), r=[pb[za]], w=[sgab[r]])
                                act(lambda hh, r=r, zb=zb: hh.activation(out=sgb2[r][:, :], in_=PS[zb][:, :], func=AF.Sigmoid), r=[pb[zb]], w=[sgbb[r]])
                                dve(lambda hh, r=r, pa=pa: hh.tensor_tensor(out=m1[r][:, :], in0=sga[r][:, :], in1=PS[pa][:, :], op=ALU.mult),
                                    r=[sgab[r], pb[pa]], w=[m1b[r]])
                                dve(lambda hh, r=r, pbk=pbk: hh.tensor_tensor(out=m2[r][:, :], in0=sgb2[r][:, :], in1=PS[pbk][:, :], op=ALU.mult),
                                    r=[sgbb[r], pb[pbk]], w=[m2b[r]])
                                dve(lambda hh, r=r, o=o, c0=c0: hh.tensor_tensor(out=mixT[:, o, c0 - hoff:c0 - hoff + 512], in0=m1[r][:, :], in1=m2[r][:, :], op=ALU.add),
                                    r=[m1b[r], m2b[r]], w=[mixb[o][g]])
                    rot = 0
                    for o in range(DC):
                        two, bwo = slots.load(wo_d[o], 1024)
                        for g in HG:
                            c0, n = CG[g]
                            r = rot % 2
                            rot += 1
                            for kc in range(DC):
                                mm(PS[4 + r][:, :], two[:, kc * 128:(kc + 1) * 128], mixT[:, kc, c0 - hoff:c0 - hoff + n], kc == 0, kc == DC - 1,
                                   [bwo, mixb[kc][g]], [pb[4 + r]])
                            dve(lambda hh, r=r, o=o, c0=c0: hh.tensor_tensor(out=hT[:, o, c0:c0 + 512], in0=PS[4 + r][:, :], in1=hT[:, o, c0:c0 + 512], op=ALU.add),
                                r=[pb[4 + r], hb[o][g]], w=[hb[o][g]])
                    S.barrier()
            cY.close()
            with ExitStack() as c10:
                uT = sbt(c10, "uT2", [128, DC, NA], BF16)
                ub = [Buf("u") for _ in CG]
                nctx10 = make_norm_ctx(c10)
                rmsnorm_all(nctx10, 16, uT, ub, [0, 1, 2, 3])
                ffn(c10, slots, wg_d[1], wu_d[1], wd_d[1], uT, ub, [0, 1, 2, 3])
            outT_v = outT_d.rearrange("(c p) t -> p c t", p=128)
            obufs = []
            for g in range(4):
                c0, n = CG[g]
                cho = S.chan()
                ob_ = Buf("out")
                S.dma("sp", cho, lambda h, c0=c0, n=n: h.dma_start(out=outT_v[:, :, c0:c0 + n], in_=hT[:, :, c0:c0 + n]),
                      reads=[hb[c][g] for c in range(DC)], writes=[ob_])
                obufs.append(ob_)
            S.wait_bufs("sp", obufs)
            S.barrier()
    return nc


def _fm_layout(w_cols):
    return np.ascontiguousarray(w_cols.reshape(8, 128, 128).transpose(1, 0, 2).reshape(128, 1024))


def _host_inputs(x, meta_tokens, ffn1_norm, ffn1_w_gate, ffn1_w_up, ffn1_w_down, mix_norm, w_in, hg_lb_fwd, hg_lb_bwd,
                 hg_out_norm, q_norm, k_norm, w_up_a, w_up_b, w_out, ffn2_norm, ffn2_w_gate, ffn2_w_up, ffn2_w_down):
    f32 = lambda a: np.asarray(a, dtype=np.float32)
    shared = {}
    p = np.arange(128)

    def ffn_pack(idx, wg, wu, wd):
        wg, wu, wd = f32(wg)[0], f32(wu)[0], f32(wd)[0]
        shared["wg%d" % idx] = np.stack([_fm_layout(wg[:, j * 128:(j + 1) * 128]) for j in range(JC)])
        shared["wu%d" % idx] = np.stack([_fm_layout(wu[:, j * 128:(j + 1) * 128]) for j in range(JC)])
        shared["wd%d" % idx] = np.ascontiguousarray(wd.reshape(JC, 128, DC, 128).transpose(2, 1, 0, 3).reshape(DC, 128, DFF))

    ffn_pack(1, ffn1_w_gate, ffn1_w_up, ffn1_w_down)
    ffn_pack(2, ffn2_w_gate, ffn2_w_up, ffn2_w_down)
    W = f32(w_in)[0]
    cols = []
    for base in (0, 1024, 1536, 2048):
        for h in range(4):
            cols.append(base + h * 128 + p)
    dd = np.arange(64)
    for swap in (0, 1):
        for c in range(4):
            d_ = dd ^ 1 if swap else dd
            cols.append(np.concatenate([2560 + c * 64 + d_, 2560 + (4 + c) * 64 + d_]))
    for swap in (0, 1):
        d_ = dd ^ 1 if swap else dd
        cols.append(np.concatenate([3072 + d_, 3072 + 64 + d_]))
    for base in (3328, 4352):
        for o in range(8):
            cols.append(base + o * 128 + p)
    assert len(cols) == 42
    shared["wfm"] = np.stack([_fm_layout(W[:, c]) for c in cols])
    shared["wzi"] = np.stack([_fm_layout(W[:, 512 + h * 128:512 + (h + 1) * 128]) for h in range(4)])
    shared["wv"] = _fm_layout(W[:, 3200:3328])

    def up_pack(wu_, rowperm):
        wu_ = f32(wu_)[0][rowperm]
        out = np.zeros((4, 128, 1024), np.float32)
        for s in range(4):
            for oo in range(2):
                o = 2 * s + oo
                blk = wu_[:, o * 128:(o + 1) * 128].reshape(4, 128, 128).transpose(1, 0, 2).reshape(128, 512)
                out[s, :, oo * 512:(oo + 1) * 512] = blk
        return out

    shared["wua"] = up_pack(w_up_a, np.arange(512))
    permb = np.concatenate([np.concatenate([c * 64 + dd, (4 + c) * 64 + dd]) for c in range(4)])
    shared["wub"] = up_pack(w_up_b, permb)
    Wo = f32(w_out)[0]
    shared["wo"] = np.stack([_fm_layout(Wo[:, o * 128:(o + 1) * 128]) for o in range(DC)])
    vecs = np.zeros((128, NV), np.float32)
    vecs[:, 0:8] = f32(ffn1_norm)[0].reshape(8, 128).T
    vecs[:, 8:16] = f32(mix_norm)[0].reshape(8, 128).T
    vecs[:, 16:24] = f32(ffn2_norm)[0].reshape(8, 128).T
    vecs[:, 24:28] = f32(hg_out_norm)[0].reshape(4, 128).T
    vecs[:, 28:32] = f32(hg_lb_fwd)[0].reshape(4, 128).T
    vecs[:, 32:36] = f32(hg_lb_fwd)[1].reshape(4, 128).T
    vecs[:, 36:40] = f32(hg_lb_bwd)[0].reshape(4, 128).T
    vecs[:, 40:44] = f32(hg_lb_bwd)[1].reshape(4, 128).T
    qn, kn = f32(q_norm)[0], f32(k_norm)[0]
    vecs[:, 44] = qn[p % 64]
    vecs[:, 45] = kn[p % 64]
    vecs[:, 46] = qn[(p % 64) ^ 1]
    vecs[:, 47] = kn[(p % 64) ^ 1]
    pi = (p % 64) // 2
    vecs[:, 48] = (pi % 16).astype(np.float32)
    vecs[:, 49] = (pi < 16).astype(np.float32)
    vecs[:, 50] = (pi >= 16).astype(np.float32)
    vecs[:, 51] = np.where(p % 2 == 0, -1.0, 1.0)
    shared["vecs"] = vecs
    cm = np.zeros((128, 384), np.float32)
    cm[:, 0:128] = np.eye(128, dtype=np.float32)
    cm[:, 128:256] = np.triu(np.ones((128, 128), np.float32))
    cm[:, 256:384] = np.tril(np.ones((128, 128), np.float32))
    shared["cmat"] = cm
    shared["metaT"] = np.ascontiguousarray(f32(meta_tokens).T)
    xs = f32(x)[0]
    in_maps = []
    for r in range(NCORE):
        m = dict(shared)
        m["xT"] = np.ascontiguousarray(xs[r * NT:(r + 1) * NT].T)
        cc = np.zeros((128, 17), np.float32)
        cc[:, 0] = r * (NT // 64)
        for j in range(NCORE):
            cc[:, 1 + j] = 1.0 if j < r else 0.0
            cc[:, 9 + j] = 1.0 if j > r else 0.0
        m["corec"] = cc
        in_maps.append(m)
    return in_maps


_NC_CACHE = {}


def kernel(**inputs):
    in_maps = _host_inputs(**inputs)
    if "nc" not in _NC_CACHE:
        _NC_CACHE["nc"] = build_program()
    res = run_bass_kernel_spmd(_NC_CACHE["nc"], in_maps, core_ids=list(range(NCORE)))
    outs = [np.asarray(res.results[r]["outT"]).T for r in range(NCORE)]
    return np.ascontiguousarray(np.concatenate(outs, axis=0)[None].astype(np.float32))
```
